# Optimizing a Trainium2 kernel written in Bass

```python
import math
import jax, jax.numpy as jnp
from jax import lax
import numpy as np

D_MODEL = 1024
BATCH = 4
SEQ = 4096
DEPTH = 1

HEAD_DIM = 64
NSA_HEADS = 8
NSA_KV_GROUPS = 2
NSA_REP = NSA_HEADS // NSA_KV_GROUPS
CMP_BLOCK = 32
CMP_STRIDE = 16
CMP_HIDDEN = 128
SLC_BLOCK = 64
SLC_TOP = 16
NSA_WINDOW = 512
NSA_QBLOCK = 64
FORCE_SCORE = 1.0e4
DIL_CONFIGS = ((128, 1), (512, 4), (2048, 16))
DIL_GROUPS = 3
DIL_HEADS_PER_GROUP = 4
DIL_HEADS = DIL_GROUPS * DIL_HEADS_PER_GROUP
DIL_QBLOCK = 128
D_FF = 2816
CONV_WIDTH = 3
RMS_EPS = 1e-6
NEG_INF = -1e30
D_IN = (NSA_HEADS * HEAD_DIM + 6 * NSA_KV_GROUPS * HEAD_DIM + 3 * NSA_HEADS
        + 3 * DIL_HEADS * HEAD_DIM + 2 * D_MODEL)

kernel_name = "hybrid_nsa_dilated_gated_merge"


def rmsnorm(x, g):
    xf = x.astype(jnp.float32)
    y = xf * lax.rsqrt(jnp.mean(xf * xf, axis=-1, keepdims=True) + RMS_EPS)
    return (y * g.astype(jnp.float32)).astype(x.dtype)


def alibi_slopes(n):
    return jnp.asarray(2.0 ** (-8.0 * np.arange(1, n + 1) / n), dtype=jnp.float32)


def masked_softmax(s, mask):
    s = jnp.where(mask, s.astype(jnp.float32), NEG_INF)
    m = jnp.max(s, axis=-1, keepdims=True)
    p = jnp.where(mask, jnp.exp(s - m), 0.0)
    denom = jnp.sum(p, axis=-1, keepdims=True)
    probs = p / jnp.maximum(denom, 1e-30)
    return probs, m + jnp.log(denom)


def in_splits():
    sizes = (NSA_HEADS * HEAD_DIM, 6 * NSA_KV_GROUPS * HEAD_DIM, 3 * NSA_HEADS,
             3 * DIL_HEADS * HEAD_DIM)
    idx, acc = [], 0
    for s in sizes:
        acc += s
        idx.append(acc)
    return idx


def compress(kv, pe, w1, w2):
    B, S, G, dh = kv.shape
    b16 = kv.reshape(B, S // CMP_STRIDE, CMP_STRIDE, G, dh)
    blocks = jnp.concatenate([b16[:, :-1], b16[:, 1:]], axis=2)
    blocks = blocks + pe[None, None, :, None, :]
    nc = blocks.shape[1]
    flat = blocks.transpose(0, 1, 3, 2, 4).reshape(B, nc, G, CMP_BLOCK * dh)
    return jax.nn.gelu(flat @ w1) @ w2


def nsa_attention(q, kc, vc, ks, vs, kw, vw, gates):
    B, S, H, dh = q.shape
    G, R, QB, W, L = NSA_KV_GROUPS, NSA_REP, NSA_QBLOCK, NSA_WINDOW, SLC_BLOCK
    NC = kc.shape[1]
    NS = S // L
    NQ = S // QB
    top = min(SLC_TOP, NS)
    scale = dh ** -0.5
    slopes = alibi_slopes(H).reshape(G, R)
    cmp_start = jnp.arange(NC) * CMP_STRIDE
    cmp_end = cmp_start + CMP_BLOCK - 1
    slc_start = jnp.arange(NS) * L
    overlap = jnp.clip(jnp.minimum(cmp_start[:, None] + CMP_BLOCK, slc_start[None, :] + L)
                       - jnp.maximum(cmp_start[:, None], slc_start[None, :]), 0, None)
    overlap = overlap.astype(jnp.float32) / CMP_BLOCK
    kc_t = kc.transpose(0, 2, 1, 3)
    vc_t = vc.transpose(0, 2, 1, 3)
    ks_blk = ks.reshape(B, NS, L, G, dh).transpose(0, 3, 1, 2, 4)
    vs_blk = vs.reshape(B, NS, L, G, dh).transpose(0, 3, 1, 2, 4)
    kw_pad = jnp.pad(kw, ((0, 0), (W, 0), (0, 0), (0, 0)))
    vw_pad = jnp.pad(vw, ((0, 0), (W, 0), (0, 0), (0, 0)))
    bi = jnp.arange(B)[:, None, None, None]
    gi = jnp.arange(G)[None, :, None, None]
    blk = jnp.arange(NS)

    def block_fn(i):
        q0 = i * QB
        t = q0 + jnp.arange(QB)
        qb = lax.dynamic_slice_in_dim(q, q0, QB, axis=1)
        qb = qb.reshape(B, QB, G, R, dh).transpose(0, 2, 3, 1, 4) * scale
        gb = lax.dynamic_slice_in_dim(gates, q0, QB, axis=1)
        gb = gb.reshape(B, QB, G, R, 3).transpose(0, 2, 3, 1, 4)

        d_cmp = (t[:, None] - cmp_end[None, :]).astype(jnp.float32)
        s = jnp.einsum('bgrqd,bgcd->bgrqc', qb, kc_t).astype(jnp.float32)
        s = s - slopes[None, :, :, None, None] * d_cmp
        p_cmp, _ = masked_softmax(s, d_cmp >= 0)
        o_cmp = jnp.einsum('bgrqc,bgcd->bgrqd', p_cmp.astype(vc.dtype), vc_t)

        imp = jnp.einsum('bgrqc,cs->bgqs', p_cmp, overlap)
        cur = t // L
        imp = jnp.where((blk[None, :] == cur[:, None]) | (blk[None, :] == 0), FORCE_SCORE, imp)
        imp = jnp.where(blk[None, :] <= cur[:, None], imp, -1.0)
        top_val, top_idx = lax.top_k(imp, top)
        k_sel = ks_blk[bi, gi, top_idx]
        v_sel = vs_blk[bi, gi, top_idx]
        pos = top_idx[..., None] * L + jnp.arange(L)
        d_sel = t[None, None, :, None, None] - pos
        m_sel = ((top_val >= 0)[..., None] & (d_sel >= 0))[:, :, None]
        s = jnp.einsum('bgrqd,bgqtld->bgrqtl', qb, k_sel).astype(jnp.float32)
        s = s - slopes[None, :, :, None, None, None] * d_sel[:, :, None].astype(jnp.float32)
        p_sel, _ = masked_softmax(s.reshape(B, G, R, QB, top * L), m_sel.reshape(B, G, 1, QB, top * L))
        o_sel = jnp.einsum('bgrqtl,bgqtld->bgrqd',
                           p_sel.reshape(B, G, R, QB, top, L).astype(vs.dtype), v_sel)

        kwb = lax.dynamic_slice_in_dim(kw_pad, q0, QB + W, axis=1)
        vwb = lax.dynamic_slice_in_dim(vw_pad, q0, QB + W, axis=1)
        key_pos = q0 - W + jnp.arange(QB + W)
        d_win = t[:, None] - key_pos[None, :]
        m_win = (d_win >= 0) & (d_win < W) & (key_pos[None, :] >= 0)
        s = jnp.einsum('bgrqd,bkgd->bgrqk', qb, kwb).astype(jnp.float32)
        s = s - slopes[None, :, :, None, None] * d_win.astype(jnp.float32)
        p_win, _ = masked_softmax(s, m_win)
        o_win = jnp.einsum('bgrqk,bkgd->bgrqd', p_win.astype(vw.dtype), vwb)

        o = gb[..., 0:1] * o_cmp + gb[..., 1:2] * o_sel + gb[..., 2:3] * o_win
        return o.transpose(0, 3, 1, 2, 4).reshape(B, QB, H * dh)

    out = lax.map(block_fn, jnp.arange(NQ))
    return out.transpose(1, 0, 2, 3).reshape(B, S, H * dh)


def dilated_attention(q, k, v):
    B, S, NG, HG, dh = q.shape
    QB = DIL_QBLOCK
    NQ = S // QB
    scale = dh ** -0.5
    slopes = alibi_slopes(DIL_HEADS).reshape(NG, HG)

    def block_fn(i):
        q0 = i * QB
        t = q0 + jnp.arange(QB)
        qb = lax.dynamic_slice_in_dim(q, q0, QB, axis=1) * scale
        outs, lses = [], []
        for g, (w, r) in enumerate(DIL_CONFIGS):
            dist = r * jnp.arange(w // r + 1)
            key_pos = t[:, None] - dist[None, :]
            idx = jnp.maximum(key_pos, 0)
            kg = jnp.take(k[:, :, g], idx, axis=1)
            vg = jnp.take(v[:, :, g], idx, axis=1)
            s = jnp.einsum('bqhd,bqnhd->bhqn', qb[:, :, g], kg).astype(jnp.float32)
            s = s - slopes[g][:, None, None] * dist.astype(jnp.float32)
            p, lse = masked_softmax(s, (key_pos >= 0)[None, None])
            outs.append(jnp.einsum('bhqn,bqnhd->bqhd', p.astype(v.dtype), vg))
            lses.append(lse[..., 0])
        wts = jax.nn.softmax(jnp.stack(lses, axis=0), axis=0)
        o = outs[0] * wts[0].transpose(0, 2, 1)[..., None].astype(outs[0].dtype)
        for g in range(1, NG):
            o = o + outs[g] * wts[g].transpose(0, 2, 1)[..., None].astype(outs[g].dtype)
        return o.reshape(B, QB, HG * dh)

    out = lax.map(block_fn, jnp.arange(NQ))
    return out.transpose(1, 0, 2, 3).reshape(B, S, HG * dh)


def conv_ffn(h, w_up, conv_w, conv_b, w_down):
    S = h.shape[1]
    u, gate = jnp.split(h @ w_up, 2, axis=-1)
    up = jnp.pad(u, ((0, 0), (CONV_WIDTH - 1, 0), (0, 0)))
    uc = conv_b
    for j in range(CONV_WIDTH):
        uc = uc + conv_w[j] * up[:, j:j + S]
    return (jax.nn.gelu(uc) * gate) @ w_down


def setup_inputs(seed: int = 0) -> dict:
    key = jax.random.key(seed)
    ks = jax.random.split(key, 20)
    f = jnp.float32
    dh = HEAD_DIM

    def nrm(k, shape, scale):
        return jax.random.normal(k, shape, f) * scale

    return {
        "x": jax.random.normal(ks[0], (BATCH, SEQ, D_MODEL), f),
        "g_mix": 1.0 + nrm(ks[1], (DEPTH, D_MODEL), 0.01),
        "w_in": nrm(ks[2], (DEPTH, D_MODEL, D_IN), D_MODEL ** -0.5),
        "pe_cmp_k": nrm(ks[3], (DEPTH, CMP_BLOCK, dh), 0.1),
        "w_cmp_k1": nrm(ks[4], (DEPTH, CMP_BLOCK * dh, CMP_HIDDEN), (CMP_BLOCK * dh) ** -0.5),
        "w_cmp_k2": nrm(ks[5], (DEPTH, CMP_HIDDEN, dh), CMP_HIDDEN ** -0.5),
        "pe_cmp_v": nrm(ks[6], (DEPTH, CMP_BLOCK, dh), 0.1),
        "w_cmp_v1": nrm(ks[7], (DEPTH, CMP_BLOCK * dh, CMP_HIDDEN), (CMP_BLOCK * dh) ** -0.5),
        "w_cmp_v2": nrm(ks[8], (DEPTH, CMP_HIDDEN, dh), CMP_HIDDEN ** -0.5),
        "w_proj_nsa": nrm(ks[9], (DEPTH, NSA_HEADS * dh, D_MODEL), (NSA_HEADS * dh) ** -0.5),
        "w_proj_dil": nrm(ks[10], (DEPTH, DIL_HEADS_PER_GROUP * dh, D_MODEL), (DIL_HEADS_PER_GROUP * dh) ** -0.5),
        "w_out": nrm(ks[11], (DEPTH, D_MODEL, D_MODEL), D_MODEL ** -0.5),
        "g_ffn": 1.0 + nrm(ks[12], (DEPTH, D_MODEL), 0.01),
        "w_up": nrm(ks[13], (DEPTH, D_MODEL, 2 * D_FF), D_MODEL ** -0.5),
        "conv_w": nrm(ks[14], (DEPTH, CONV_WIDTH, D_FF), CONV_WIDTH ** -0.5),
        "conv_b": nrm(ks[15], (DEPTH, D_FF), 0.01),
        "w_down": nrm(ks[16], (DEPTH, D_FF, D_MODEL), D_FF ** -0.5),
        "g_final": 1.0 + nrm(ks[17], (D_MODEL,), 0.01),
    }


def reference(x, g_mix, w_in, pe_cmp_k, w_cmp_k1, w_cmp_k2, pe_cmp_v, w_cmp_v1, w_cmp_v2,
              w_proj_nsa, w_proj_dil, w_out, g_ffn, w_up, conv_w, conv_b, w_down, g_final):
    B, S, _ = x.shape
    G, dh = NSA_KV_GROUPS, HEAD_DIM
    for l in range(DEPTH):
        h = rmsnorm(x, g_mix[l])
        proj = h @ w_in[l]
        q_a, kv_a, gate_a, qkv_b, merge_logits = jnp.split(proj, in_splits(), axis=-1)
        q_a = q_a.reshape(B, S, NSA_HEADS, dh)
        kv_a = kv_a.reshape(B, S, 6, G, dh)
        kc = compress(kv_a[:, :, 0], pe_cmp_k[l], w_cmp_k1[l], w_cmp_k2[l])
        vc = compress(kv_a[:, :, 1], pe_cmp_v[l], w_cmp_v1[l], w_cmp_v2[l])
        gate_a = jax.nn.sigmoid(gate_a).reshape(B, S, NSA_HEADS, 3)
        o_a = nsa_attention(q_a, kc, vc, kv_a[:, :, 2], kv_a[:, :, 3], kv_a[:, :, 4], kv_a[:, :, 5], gate_a)
        qkv_b = qkv_b.reshape(B, S, 3, DIL_GROUPS, DIL_HEADS_PER_GROUP, dh)
        o_b = dilated_attention(qkv_b[:, :, 0], qkv_b[:, :, 1], qkv_b[:, :, 2])
        gate_m = jax.nn.sigmoid(merge_logits)
        mixed = gate_m[..., :D_MODEL] * (o_a @ w_proj_nsa[l]) + gate_m[..., D_MODEL:] * (o_b @ w_proj_dil[l])
        x = x + mixed @ w_out[l]
        h = rmsnorm(x, g_ffn[l])
        x = x + conv_ffn(h, w_up[l], conv_w[l], conv_b[l], w_down[l])
    return rmsnorm(x, g_final)
```

```python
import contextlib
import numpy as np
import ml_dtypes
import concourse.bass as bass
import concourse.mybir as mybir
from concourse.bass_utils import run_bass_kernel_spmd

F32 = mybir.dt.float32
BF16 = mybir.dt.bfloat16
AF = mybir.ActivationFunctionType
ALU = mybir.AluOpType
AX = mybir.AxisListType

D = 1024
T = 4096
NCH = 32
QB0 = 15
NQB = 17
NQ = NQB * 128
Q0 = QB0 * 128
DFF = 2816
NFF = 22
DIN = 5656
NEG = -30000.0
DIL = ((128, 1), (512, 4), (2048, 16))
NDS = 48
NDS_HW = 32


class Trk:
    __slots__ = ("w", "r")

    def __init__(self):
        self.w = []
        self.r = []


def trks(n):
    return [Trk() for _ in range(n)]


class Prog:
    def __init__(self):
        self.nc = bass.Bass("TRN2", target_bir_lowering=False)
        nc = self.nc
        self.es = contextlib.ExitStack()
        self.E = {"pe": nc.tensor, "act": nc.scalar, "dve": nc.vector, "pool": nc.gpsimd, "sp": nc.sync}
        self.sem = {k: self.es.enter_context(nc.semaphore("s_" + k)) for k in self.E}
        self.cnt = {k: 0 for k in self.E}
        self.seen = {k: {} for k in self.E}
        self.dsem = [self.es.enter_context(nc.semaphore("d%d" % i)) for i in range(NDS)]
        self.dcnt = [0] * NDS
        self.dnext = 0
        self.dnext_sw = NDS_HW
        self.sw_out = []
        self.ninst = 0

    def _wait(self, eng, dep):
        key, val = dep
        if key == "pe" and eng == "pe":
            return
        if self.seen[eng].get(key, 0) >= val:
            return
        self.seen[eng][key] = val
        sem = self.sem[key] if isinstance(key, str) else self.dsem[key[1]]
        self.E[eng].wait_ge(sem, val)

    def _deps(self, eng, r, w):
        deps = set()
        for t in r:
            deps.update(t.w)
        for t in w:
            deps.update(t.w)
            deps.update(t.r)
        best = {}
        for (k, v) in deps:
            if best.get(k, 0) < v:
                best[k] = v
        for k in sorted(best, key=str):
            self._wait(eng, (k, best[k]))

    def _mark(self, toks, r, w):
        for t in w:
            t.w = list(toks)
            t.r = []
        for t in r:
            if t not in w:
                t.r.extend(toks)

    def op(self, eng, fn, r=(), w=()):
        self._deps(eng, r, w)
        ins = fn(self.E[eng])
        self.cnt[eng] += 1
        ins.then_inc(self.sem[eng], 1)
        self.ninst += 1
        tok = (eng, self.cnt[eng])
        self._mark([tok], r, w)
        return tok

    def dma(self, q, out, in_, r=(), w=()):
        return self.dma_multi(q, [(out, in_)], r=r, w=w)

    def dma_multi(self, q, pieces, r=(), w=()):
        self._deps(q, r, w)
        toks = []
        for (out, in_) in pieces:
            if q == "pool":
                nd = 1
                for d_ in list(out.shape)[:-1]:
                    nd *= int(d_)
                nd = nd // 16 + 2
                while self.sw_out and sum(n_ for (_, n_) in self.sw_out) + nd > 300:
                    tok0, _ = self.sw_out.pop(0)
                    self._wait(q, tok0)
                j = self.dnext_sw
                self.dnext_sw = NDS_HW + (j + 1 - NDS_HW) % (NDS - NDS_HW)
            else:
                j = self.dnext
                self.dnext = (j + 1) % NDS_HW
            if self.dcnt[j] > 0:
                self._wait(q, (("d", j), 16 * self.dcnt[j]))
            self.dcnt[j] += 1
            self.E[q].dma_start(out=out, in_=in_).then_inc(self.dsem[j], 16)
            self.ninst += 1
            toks.append((("d", j), 16 * self.dcnt[j]))
            if q == "pool":
                self.sw_out.append((toks[-1], nd))
        self._mark(toks, r, w)
        return toks

    def barrier(self):
        for e in self.E:
            for e2 in self.E:
                if e2 != e and self.cnt[e2] > 0:
                    self._wait(e, (e2, self.cnt[e2]))
            for j in range(NDS):
                if self.dcnt[j] > 0:
                    self._wait(e, (("d", j), 16 * self.dcnt[j]))

    def mm(self, out, lhsT, rhs, start, stop, r=(), w=(), skip=False):
        if skip:
            return self.op("pe", lambda e: e.matmul(out, lhsT=lhsT, rhs=rhs, start=start, stop=stop,
                                                    skip_group_check=True), r=r, w=w)
        return self.op("pe", lambda e: e.matmul(out, lhsT=lhsT, rhs=rhs, start=start, stop=stop), r=r, w=w)

    def tr(self, out, in_, ident, r=(), w=()):
        return self.op("pe", lambda e: e.transpose(out, in_, ident), r=r, w=w)

    def act(self, out, in_, func, r=(), w=(), **kw):
        return self.op("act", lambda e: e.activation(out=out, in_=in_, func=func, **kw), r=r, w=w)

    def copy(self, eng, out, in_, r=(), w=(), scale=None):
        if eng == "act":
            if scale is None:
                return self.act(out, in_, AF.Copy, r=r, w=w)
            return self.act(out, in_, AF.Copy, r=r, w=w, scale=float(scale))
        if scale is None:
            return self.op(eng, lambda e: e.tensor_copy(out=out, in_=in_), r=r, w=w)
        return self.op(eng, lambda e: e.tensor_scalar(out=out, in0=in_, scalar1=float(scale), scalar2=None,
                                                      op0=ALU.mult), r=r, w=w)

    def tt(self, out, in0, in1, op, r=(), w=(), eng="dve"):
        return self.op(eng, lambda e: e.tensor_tensor(out=out, in0=in0, in1=in1, op=op), r=r, w=w)

    def ts(self, out, in0, s1, op0, s2=None, op1=None, r=(), w=(), eng="dve"):
        if op1 is None:
            return self.op(eng, lambda e: e.tensor_scalar(out=out, in0=in0, scalar1=s1, scalar2=None, op0=op0),
                           r=r, w=w)
        return self.op(eng, lambda e: e.tensor_scalar(out=out, in0=in0, scalar1=s1, scalar2=s2, op0=op0, op1=op1),
                       r=r, w=w)

    def stt(self, out, in0, scalar, in1, op0, op1, r=(), w=()):
        return self.op("dve", lambda e: e.scalar_tensor_tensor(out=out, in0=in0, scalar=scalar, in1=in1,
                                                               op0=op0, op1=op1), r=r, w=w)


def alibi(n):
    return (2.0 ** (-8.0 * np.arange(1, n + 1) / n)).astype(np.float64)


def bf(a):
    return np.asarray(a, dtype=np.float32).astype(ml_dtypes.bfloat16)


def const_tables(hf):
    c = {}
    c["ident"] = np.eye(128, dtype=np.float32)
    c["identb"] = bf(np.eye(128))
    pos = np.arange(T)
    gpos = pos if hf == 1 else pos - 2048
    valid = (gpos >= 0).astype(np.float32)
    c["validtm"] = bf(valid.reshape(NCH, 128).T)
    vd = np.zeros((128, 3, 32), np.float32)
    for g, (w, r) in enumerate(DIL):
        npc = 32 // r
        for rho in range(r):
            for jc in range(npc):
                tok = rho + r * (128 * jc + np.arange(128))
                vd[:, g, rho * npc + jc] = valid[tok]
    c["validd"] = bf(vd)
    c["halo"] = np.full((128, 1), 1.0 if hf == 1 else 0.0, np.float32)
    ea = np.zeros((128, T), np.float32)
    ea[pos // 64, pos] = 1.0
    ea[64] = 128.0 * (pos // 128)
    ea[65] = pos % 128
    ea[66] = 1.0
    ea[67] = 1.0
    c["ea"] = bf(ea)
    sl = alibi(8)
    rbc = np.zeros((4, NQB, 2, 4, 128), np.float32)
    ql = np.arange(128)
    for i in range(NQB):
        qb = QB0 + i
        for g in range(2):
            for r in range(4):
                s = sl[4 * g + r]
                rbc[0, i, g, r, :] = s
                rbc[1, i, g, r, :] = s
                rbc[2, i, g, r, :] = -s * 128.0 * qb
                rbc[3, i, g, r, :] = -s * ql
    c["rbc"] = bf(rbc)
    kk = np.arange(128)[:, None]
    qq = np.arange(128)[None, :]
    caus = np.where(kk <= qq, 0.0, NEG).astype(np.float32)
    acaus = np.where(kk > qq, 0.0, NEG).astype(np.float32)
    c["caus"] = np.ascontiguousarray(np.broadcast_to(caus[:, None, :], (128, 4, 128)))
    c["acaus"] = np.ascontiguousarray(np.broadcast_to(acaus[:, None, :], (128, 4, 128)))
    bc = np.zeros((NQB, 128, 2, 2, 4, 128), np.float32)
    for i in range(NQB):
        t = (QB0 + i) * 128 + np.arange(128)
        for kc in range(2):
            cc = kc * 128 + np.arange(128)
            cend = 16 * cc + 31
            cstart_g = 16 * cc - (0 if hf == 1 else 2048)
            dist = t[None, :] - cend[:, None]
            ok = (dist >= 0) & (cstart_g[:, None] >= 0) & (cc[:, None] < 255)
            for g in range(2):
                for r in range(4):
                    bc[i, :, kc, g, r, :] = np.where(ok, -sl[4 * g + r] * dist, NEG)
    c["bcmp"] = bc
    cc = np.arange(256)
    ss = np.arange(64)
    ov = np.clip(np.minimum(16 * cc[:, None] + 32, 64 * ss[None, :] + 64)
                 - np.maximum(16 * cc[:, None], 64 * ss[None, :]), 0, None) / 32.0
    c["ovl"] = bf(ov.reshape(2, 128, 64).transpose(1, 0, 2))
    ma = np.zeros((128, NQB, 64), np.float32)
    mb = np.zeros((128, NQB, 64), np.float32)
    b0 = 0 if hf == 1 else 32
    for i in range(NQB):
        t = (QB0 + i) * 128 + np.arange(128)
        cur = t // 64
        for s in range(64):
            forced = (s == cur) | (s == b0)
            future = (s > cur) | (s < b0)
            normal = (~forced) & (~future)
            ma[:, i, s] = normal
            mb[:, i, s] = np.where(future, -1.0, np.where(forced, 1.0e4, 0.0))
    c["ma"] = ma
    c["mb"] = mb
    sd = alibi(12).reshape(3, 4)
    bd = np.zeros((128, 3, 2, 4, 128), np.float32)
    for g, (w, r) in enumerate(DIL):
        for dl in range(2):
            dj = dl * 128 + qq - kk
            ok = (dj >= 0) & (dj <= 128)
            for h in range(4):
                bd[:, g, dl, h, :] = np.where(ok, -sd[g, h] * r * dj, NEG)
    c["bdil"] = bd
    s65 = np.zeros((65, 64), np.float32)
    s65[64, :] = 1.0
    c["s65"] = s65
    return c


def build(stage=99):
    P = Prog()
    with P.es:
        return _build(P, stage)


def _build(P, stage):
    nc = P.nc
    es = P.es

    def dram_in(name, shape, dt=F32):
        return nc.dram_tensor(name, list(shape), dt, kind="ExternalInput").ap()

    def sb(stack, name, shape, dt):
        return stack.enter_context(nc.sbuf_tensor(name, list(shape), dt))

    def ps(stack, name, shape, dt):
        return stack.enter_context(nc.psum_tensor(name, list(shape), dt))

    x = dram_in("x", [T, D])
    g_mix = dram_in("g_mix", [1, D])
    w_in = dram_in("w_in", [D, DIN])
    pe_k = dram_in("pe_cmp_k", [32, 64])
    w_k1 = dram_in("w_cmp_k1", [2048, 128])
    w_k2 = dram_in("w_cmp_k2", [128, 64])
    pe_v = dram_in("pe_cmp_v", [32, 64])
    w_v1 = dram_in("w_cmp_v1", [2048, 128])
    w_v2 = dram_in("w_cmp_v2", [128, 64])
    w_pa = dram_in("w_proj_nsa", [512, D])
    w_pb = dram_in("w_proj_dil", [256, D])
    w_out = dram_in("w_out", [D, D])
    g_ffn = dram_in("g_ffn", [1, D])
    w_up = dram_in("w_up", [D, 2 * DFF])
    conv_w = dram_in("conv_w", [3, DFF])
    conv_b = dram_in("conv_b", [1, DFF])
    w_down = dram_in("w_down", [DFF, D])
    g_fin = dram_in("g_final", [1, D])
    c_ident = dram_in("c_ident", [128, 128])
    c_identb = dram_in("c_identb", [128, 128], BF16)
    c_validtm = dram_in("c_validtm", [128, 32], BF16)
    c_validd = dram_in("c_validd", [128, 3, 32], BF16)
    c_halo = dram_in("c_halo", [128, 1])
    c_ea = dram_in("c_ea", [128, T], BF16)
    c_rbc = dram_in("c_rbc", [4, NQB, 2, 4, 128], BF16)
    c_caus = dram_in("c_caus", [128, 4, 128])
    c_acaus = dram_in("c_acaus", [128, 4, 128])
    c_bcmp = dram_in("c_bcmp", [NQB, 128, 2, 2, 4, 128])
    c_ovl = dram_in("c_ovl", [128, 2, 64], BF16)
    c_ma = dram_in("c_ma", [128, NQB, 64])
    c_mb = dram_in("c_mb", [128, NQB, 64])
    c_bdil = dram_in("c_bdil", [128, 3, 2, 4, 128])
    c_s65 = dram_in("c_s65", [65, 64])
    y = nc.dram_tensor("y", [2048, D], F32, kind="ExternalOutput").ap()
    xm_d = nc.dram_tensor("xm_scratch", [NQ, D], F32, kind="Internal").ap()
    t_xmd = trks(NQB)

    def dump(name, ap, trk):
        o = nc.dram_tensor("dbg_" + name, list(ap.shape), ap.dtype, kind="ExternalOutput").ap()
        P.dma("sp", o, ap, r=trk)

    def done():
        for j in range(NDS):
            if P.dcnt[j] > 0:
                P._wait("sp", (("d", j), 16 * P.dcnt[j]))
        return nc

    PS = [ps(es, "ps%d" % i, [128, 512], F32) for i in range(7)]
    PST = ps(es, "pst", [128, 1024], BF16)
    tPS = trks(7)
    tPST = Trk()
    psrr = [0]

    def nextps(lo, hi):
        k = lo + psrr[0] % (hi - lo)
        psrr[0] += 1
        return k

    def v4(ap):
        return ap.rearrange("p (a b) -> p a b", a=4)

    ident = sb(es, "ident", [128, 128], F32)
    identb = sb(es, "identb", [128, 128], BF16)
    halo = sb(es, "halo", [128, 1], F32)
    t_const = Trk()
    P.dma("sp", ident[:], c_ident[:, :], w=[t_const])
    P.dma("sp", identb[:], c_identb[:, :], w=[t_const])
    P.dma("sp", halo[:], c_halo[:, :], w=[t_const])

    evac_rr = [0]

    def evac_eng():
        evac_rr[0] += 1
        return "act" if evac_rr[0] % 2 else "dve"

    def rmsnorm_tile(junk, ss, t_s, xt, t_x, gt, t_g, out_t, t_out):
        P.act(junk[:], xt, AF.Square, r=[t_x], w=[t_s], accum_out=ss[:, 0:1])
        P.ts(ss[:, 1:2], ss[:, 0:1], 1.0 / D, ALU.mult, 1e-6, ALU.add, r=[t_s], w=[t_s])
        P.act(ss[:, 2:3], ss[:, 1:2], AF.Sqrt, r=[t_s], w=[t_s])
        P.op("dve", lambda e: e.reciprocal(out=ss[:, 3:4], in_=ss[:, 2:3]), r=[t_s], w=[t_s])
        P.stt(out_t, xt, ss[:, 3:4], gt, ALU.mult, ALU.mult, r=[t_x, t_g, t_s], w=[t_out])

    def chunks_of(tok0, n):
        return list(range(tok0 // 128, (tok0 + n - 1) // 128 + 1))

    hT_d = nc.dram_tensor("hT_scratch", [128, 8, T], BF16, kind="Internal").ap()
    t_hTd = trks(2)
    with contextlib.ExitStack() as PA:
        OAT = sb(PA, "OAT", [128, 4, NQ], BF16)
        t_OAT = trks(NQB)
        NPT = 4
        PT = [sb(PA, "PT%d" % i, [128, 512], BF16) for i in range(NPT)]
        t_PT = trks(NPT)
        ptrr = [0]
        NTMP = 3
        TMP = [sb(PA, "TMP%d" % i, [128, 512], F32) for i in range(NTMP)]
        t_TMP = trks(NTMP)
        tmprr = [0]

        def next_pt():
            k = ptrr[0] % NPT
            ptrr[0] += 1
            return k

        def next_tmp():
            k = tmprr[0] % NTMP
            tmprr[0] += 1
            return k

        GT = [sb(PA, "GT%d" % i, [128, 512], F32) for i in range(2)]
        t_GT = trks(2)

        def gelu_tanh(out, t_out, xin, t_x, n):
            a, b2 = GT[0][:, 0:n], GT[1][:, 0:n]
            P.tt(a, xin, xin, ALU.mult, r=[t_x], w=[t_GT[0]])
            P.ts(a, a, 0.044715, ALU.mult, 1.0, ALU.add, r=[t_GT[0]], w=[t_GT[0]])
            P.tt(a, a, xin, ALU.mult, r=[t_GT[0], t_x], w=[t_GT[0]])
            P.act(b2, a, AF.Sigmoid, r=[t_GT[0]], w=[t_GT[1]], scale=1.5957691216057308)
            P.tt(out, b2, xin, ALU.mult, r=[t_GT[1], t_x], w=t_out)

        def load_wt(dst, t_dst, src, col_specs):
            o = 0
            pieces = []
            for (c0, n) in col_specs:
                pieces.append((dst[:, :, o:o + n], src[:, c0:c0 + n].rearrange("(c p) n -> p c n", p=128)))
                o += n
            P.dma_multi("pool", pieces, w=[t_dst])

        def phase1(ph, hdst, t_hdst, tiles):
            xt2 = [sb(ph, "xt%d" % i, [128, D], F32) for i in range(2)]
            t_xt = trks(2)
            xn2 = [sb(ph, "xn%d" % i, [128, D], BF16) for i in range(2)]
            t_xn = trks(2)
            gmr = sb(ph, "gmr", [128, D], F32)
            t_gm = Trk()
            junk = sb(ph, "junk", [128, D], F32)
            ss2 = [sb(ph, "ss%d" % i, [128, 4], F32) for i in range(2)]
            t_ss = trks(2)
            P.dma("sp", gmr[:], g_mix[0:1, :].partition_broadcast(128), w=[t_gm])

            def run(tiles_):
                for kk, t in enumerate(tiles_):
                    k = kk % 2
                    P.dma("sp", xt2[k][:], x[t * 128:(t + 1) * 128, :], w=[t_xt[k]])
                    rmsnorm_tile(junk, ss2[k], t_ss[k], xt2[k][:], t_xt[k], gmr[:], t_gm, xn2[k][:], t_xn[k])
                    for c in range(8):
                        P.tr(PST[:, c * 128:(c + 1) * 128], xn2[k][:, c * 128:(c + 1) * 128], identb[:],
                             r=[t_xn[k], t_const], w=[tPST])
                    P.copy(evac_eng(), hdst[:, :, kk * 128:(kk + 1) * 128],
                           PST[:, :].rearrange("p (c n) -> p c n", c=8), r=[tPST], w=[t_hdst[kk]])
            return run

        with contextlib.ExitStack() as nsa:
            QN = [sb(nsa, "QN%d" % g, [128, 4, NQ], BF16) for g in range(2)]
            t_QN = trks(NQB)
            KS = sb(nsa, "KS", [128, T], BF16)
            KW = sb(nsa, "KW", [128, T], BF16)
            t_KS = trks(NCH)
            t_KW = trks(NCH)
            VS = sb(nsa, "VS", [128, NCH, 2, 66], BF16)
            VW = sb(nsa, "VW", [128, NCH, 2, 66], BF16)
            t_VS = trks(NCH)
            t_VW = trks(NCH)
            GA = sb(nsa, "GA", [128, NQB, 24], F32)
            t_GA = trks(NQB)
            KC = sb(nsa, "KC", [128, 256], BF16)
            t_KC = Trk()
            VCX = sb(nsa, "VCX", [128, 2, 2, 130], BF16)
            t_VCX = Trk()
            vtm = sb(nsa, "vtm", [128, 32], BF16)
            t_vtm = Trk()
            P.dma("sp", vtm[:], c_validtm[:, :], w=[t_vtm])
            P.copy("dve", VS[:, :, :, 64], vtm[:].unsqueeze(2).broadcast_to([128, NCH, 2]), r=[t_vtm], w=t_VS)
            P.copy("dve", VW[:, :, :, 64], vtm[:].unsqueeze(2).broadcast_to([128, NCH, 2]), r=[t_vtm], w=t_VW)
            P.op("dve", lambda e: e.memset(QN[0][64:128, :, :], 0.0), w=t_QN)
            P.op("dve", lambda e: e.memset(QN[1][0:64, :, :], 0.0), w=t_QN)

            with contextlib.ExitStack() as ph:
                hTh = sb(ph, "hTh", [128, 8, 2048], BF16)
                t_hTh = trks(16)
                wq = sb(ph, "wq", [128, 8, 512], BF16)
                wk = sb(ph, "wk", [128, 8, 512], BF16)
                wv = sb(ph, "wv", [128, 8, 280], BF16)
                t_wq, t_wk, t_wv = Trk(), Trk(), Trk()
                load_wt(wq, t_wq, w_in, [(0, 64), (256, 64), (64, 64), (320, 64), (128, 64), (384, 64), (192, 64),
                                         (448, 64)])
                load_wt(wk, t_wk, w_in, [(768, 128), (1024, 128), (512, 128), (640, 128)])
                load_wt(wv, t_wv, w_in, [(896, 128), (1152, 128), (1280, 24)])
                SRD = [sb(ph, "SRD%d" % kv, [128, 2, 16, 256], BF16) for kv in range(2)]
                t_SRD = trks(2)
                for kv in range(2):
                    P.op("dve", lambda e, kv=kv: e.memset(SRD[kv][:, 1, :, 255:256], 0.0), w=[t_SRD[kv]])
                with contextlib.ExitStack() as p1s:
                    run_p1 = phase1(p1s, hTh, t_hTh, None)
                    for hh_ in range(2):
                        run_p1(list(range(16 * hh_, 16 * hh_ + 16)))
                        P.dma("sp", hT_d[:, :, hh_ * 2048:(hh_ + 1) * 2048], hTh[:, :, :], r=t_hTh, w=[t_hTd[hh_]])
                        if stage == 1 and hh_ == 0:
                            dump("hTh", hTh[:, :, Q0:2048], t_hTh)
                            return done()

                        def fm(wt, t_w, wc0, lt0, n, dst, t_dst, scale=None, eng=None, dst2=None):
                            b = nextps(0, 4)
                            for dm in range(8):
                                P.mm(PS[b][:, 0:n], wt[:, dm, wc0:wc0 + 128], hTh[:, dm, lt0:lt0 + n],
                                     start=(dm == 0), stop=(dm == 7),
                                     r=[t_w] + [t_hTh[c] for c in chunks_of(lt0, n)], w=[tPS[b]])
                            return b

                        qtiles = [(Q0, 128, 0)] if hh_ == 0 else [(n0, 512, 128 + n0) for n0 in range(0, 2048, 512)]
                        for r_ in range(4):
                            for (lt0, n, qoff) in qtiles:
                                b = fm(wq, t_wq, 128 * r_, lt0, n, None, None)
                                tq = [t_QN[c] for c in chunks_of(qoff, n)]
                                P.copy("act", QN[0][0:64, r_, qoff:qoff + n], PS[b][0:64, 0:n], r=[tPS[b]], w=tq,
                                       scale=0.125)
                                P.copy("dve", QN[1][64:128, r_, qoff:qoff + n], PS[b][64:128, 0:n], r=[tPS[b]], w=tq,
                                       scale=0.125)
                        for n0 in range(0, 2048, 512):
                            g0 = hh_ * 2048 + n0
                            b = fm(wk, t_wk, 0, n0, 512, None, None)
                            P.copy(evac_eng(), KS[:, g0:g0 + 512], PS[b][:, 0:512], r=[tPS[b]],
                                   w=[t_KS[c] for c in chunks_of(g0, 512)])
                            b = fm(wk, t_wk, 128, n0, 512, None, None)
                            P.copy(evac_eng(), KW[:, g0:g0 + 512], PS[b][:, 0:512], r=[tPS[b]],
                                   w=[t_KW[c] for c in chunks_of(g0, 512)])
                            for kv in range(2):
                                b = fm(wk, t_wk, 256 + 128 * kv, n0, 512, None, None)
                                c0 = g0 // 16
                                pv_ = PS[b][:, 0:512].rearrange("d (c p) -> d p c", p=16)
                                P.copy("dve", SRD[kv][:, 0, :, c0:c0 + 32], pv_, r=[tPS[b]], w=[t_SRD[kv]])
                                if g0 == 0:
                                    P.copy("dve", SRD[kv][:, 1, :, 0:31],
                                           PS[b][:, 16:512].rearrange("d (c p) -> d p c", p=16),
                                           r=[tPS[b]], w=[t_SRD[kv]])
                                else:
                                    P.copy("dve", SRD[kv][:, 1, :, c0 - 1:c0 + 31], pv_, r=[tPS[b]], w=[t_SRD[kv]])
                        for tl in range(16):
                            t = hh_ * 16 + tl
                            b = nextps(0, 4)
                            for dm in range(8):
                                P.mm(PS[b][:, 0:280], hTh[:, dm, tl * 128:(tl + 1) * 128], wv[:, dm, 0:280],
                                     start=(dm == 0), stop=(dm == 7), r=[t_wv, t_hTh[tl]], w=[tPS[b]])
                            P.copy("dve", VS[:, t, :, 0:64], PS[b][:, 0:128].rearrange("p (g d) -> p g d", g=2),
                                   r=[tPS[b]], w=[t_VS[t]])
                            P.copy("dve", VW[:, t, :, 0:64], PS[b][:, 128:256].rearrange("p (g d) -> p g d", g=2),
                                   r=[tPS[b]], w=[t_VW[t]])
                            if t >= QB0:
                                P.act(GA[:, t - QB0, :], PS[b][:, 256:280], AF.Sigmoid, r=[tPS[b]],
                                      w=[t_GA[t - QB0]])
                P.barrier()
                if stage == 23:
                    dump("QN0", QN[0][:, :, 0:256], t_QN)
                    dump("QN1", QN[1][:, :, 0:256], t_QN)
                    dump("KS", KS[:, Q0:Q0 + 256], t_KS)
                    dump("KW", KW[:, Q0:Q0 + 256], t_KW)
                    dump("VS", VS[:, 16:18, :, 0:65], t_VS)
                    dump("GA", GA[:, 0:2, :], t_GA)
                    return done()

                W1 = sb(ph, "W1", [128, 32, 128], BF16)
                W2 = sb(ph, "W2", [128, 64], BF16)
                peT = sb(ph, "peT", [128, 64], F32)
                peTb = sb(ph, "peTb", [128, 32, 2], BF16)
                hb = sb(ph, "hb", [128, 1], F32)
                HT = sb(ph, "HT", [128, 2, 256], BF16)
                t_W1 = Trk()
                t_W2 = Trk()
                t_pe = Trk()
                t_hb = Trk()
                t_HT = Trk()
                ovl = sb(ph, "ovl", [128, 2, 64], BF16)
                t_ovl = Trk()
                P.dma("sp", ovl[:], c_ovl[:, :, :], w=[t_ovl])
                for kv in range(2):
                    w1d, w2d, ped = ((w_k1, w_k2, pe_k), (w_v1, w_v2, pe_v))[kv]
                    w1v = w1d.rearrange("(p d) h -> d p h", d=64)
                    P.dma_multi("pool", [(W1[0:64, :, :], w1v), (W1[64:128, :, :], w1v)], w=[t_W1])
                    P.dma("pool", W2[:, :], w2d[:, :], w=[t_W2])
                    P.dma("sp", peT[0:32, 0:64], ped[:, :], w=[t_pe])
                    b = nextps(0, 4)
                    P.tr(PS[b][0:64, 0:32], peT[0:32, 0:64], ident[0:32, 0:32], r=[t_pe, t_const], w=[tPS[b]])
                    P.copy("dve", peTb[0:64, :, :], PS[b][0:64, 0:32].unsqueeze(2).broadcast_to([64, 32, 2]),
                           r=[tPS[b]], w=[t_pe])
                    b = nextps(0, 4)
                    for p_ in range(32):
                        P.mm(PS[b][:, 0:2], W1[0:64, p_, :], peTb[0:64, p_, :], start=(p_ == 0),
                             stop=(p_ == 31), r=[t_W1, t_pe], w=[tPS[b]])
                    P.copy("dve", hb[:, 0:1], PS[b][:, 0:1], r=[tPS[b]], w=[t_hb])
                    m = next_tmp()
                    for g in range(2):
                        b = 2 * g + (kv % 2)
                        for p_ in range(32):
                            P.mm(PS[b][:, 0:256], W1[64 * g:64 * g + 64, p_, :],
                                 SRD[kv][64 * g:64 * g + 64, p_ // 16, p_ % 16, :],
                                 start=(p_ == 0), stop=(p_ == 31), r=[t_W1, t_SRD[kv]], w=[tPS[b]])
                        P.ts(TMP[m][:, g * 256:(g + 1) * 256], PS[b][:, 0:256], hb[:, 0:1], ALU.add,
                             r=[tPS[b], t_hb], w=[t_TMP[m]])
                    gelu_tanh(HT[:, :, :].rearrange("p g c -> p (g c)"), [t_HT], TMP[m][:, :], t_TMP[m], 512)
                    if kv == 0:
                        for g in range(2):
                            b = nextps(4, 6)
                            P.mm(PS[b][0:64, 0:256], W2[:, :], HT[:, g, :], start=True, stop=True,
                                 r=[t_W2, t_HT], w=[tPS[b]])
                            P.copy("dve", KC[64 * g:64 * g + 64, :], PS[b][0:64, 0:256], r=[tPS[b]], w=[t_KC])
                    else:
                        for kc in range(2):
                            for g in range(2):
                                b = nextps(4, 6)
                                P.mm(PS[b][:, 0:64], HT[:, g, kc * 128:(kc + 1) * 128], W2[:, :], start=True,
                                     stop=True, r=[t_W2, t_HT], w=[tPS[b]])
                                P.copy("dve", VCX[:, kc, g, 0:64], PS[b][:, 0:64], r=[tPS[b]], w=[t_VCX])
                                P.copy("dve", VCX[:, kc, g, 66:130], ovl[:, kc, :], r=[t_ovl], w=[t_VCX])
                        P.op("dve", lambda e: e.memset(VCX[:, :, :, 64:65], 1.0), w=[t_VCX])
                        P.op("dve", lambda e: e.memset(VCX[:, :, :, 65:66], 0.0), w=[t_VCX])
                P.barrier()

            if stage == 2:
                dump("KC", KC[:, :], [t_KC])
                dump("VCX", VCX[:, :, :, :], [t_VCX])
                return done()

            ea = sb(nsa, "ea", [128, T], BF16)
            caus = sb(nsa, "caus", [128, 4, 128], F32)
            acaus = sb(nsa, "acaus", [128, 4, 128], F32)
            ma = sb(nsa, "ma", [128, NQB, 64], F32)
            mb = sb(nsa, "mb", [128, NQB, 64], F32)
            t_tab = Trk()
            P.dma("sp", ea[:], c_ea[:, :], w=[t_tab])
            P.dma("sp", caus[:], c_caus[:, :, :], w=[t_tab])
            P.dma("sp", acaus[:], c_acaus[:, :, :], w=[t_tab])
            P.dma("sp", ma[:], c_ma[:, :, :], w=[t_tab])
            P.dma("sp", mb[:], c_mb[:, :, :], w=[t_tab])
            RB = [[sb(nsa, "RB%d%d" % (g, k), [128, 4, 128], BF16) for k in range(2)] for g in range(2)]
            RW = [[sb(nsa, "RW%d%d" % (g, k), [128, 4, 128], BF16) for k in range(2)] for g in range(2)]
            t_RB = [[Trk() for k in range(2)] for g in range(2)]
            t_RW = [[Trk() for k in range(2)] for g in range(2)]
            for g in range(2):
                for k in range(2):
                    P.op("dve", lambda e, g=g, k=k: e.memset(RB[g][k][:, :, :], 0.0), w=[t_RB[g][k]])
                    P.op("dve", lambda e, g=g, k=k: e.memset(RW[g][k][:, :, :], 0.0), w=[t_RW[g][k]])
            Bc = [sb(nsa, "Bc%d" % k, [128, 2, 2, 4, 128], F32) for k in range(2)]
            t_Bc = trks(2)
            ONSA = [sb(nsa, "ONSA%d" % k, [128, 512], F32) for k in range(2)]
            t_ONSA = trks(2)
            sm = [sb(nsa, "sm%d" % k, [128, 32], F32) for k in range(2)]
            t_sm = trks(2)
            impn = [sb(nsa, "impn%d" % k, [128, 64], F32) for k in range(2)]
            impw = [sb(nsa, "impw%d" % k, [128, 64], F32) for k in range(2)]
            t_imp = trks(2)
            srr = [0]

            def s_tile(src_fn, kind, g, i, c):
                qs = slice(i * 128, (i + 1) * 128)
                rb = RB[g][i % 2]
                t_rb = t_RB[g][i % 2]
                rw = RW[g][i % 2]
                t_rw = t_RW[g][i % 2]
                b = nextps(0, 2)
                pv = v4(PS[b][:, 0:512])
                if kind == "cmp":
                    P.mm(pv, KC[:, c * 128:(c + 1) * 128], QN[g][:, :, qs], True, True,
                         r=[t_KC, t_QN[i]], w=[tPS[b]])
                    addt = Bc[i % 2][:, c, g, :, :]
                    t_add = t_Bc[i % 2]
                else:
                    KX, t_KX = (KS, t_KS) if kind == "sel" else (KW, t_KW)
                    P.mm(pv, KX[:, c * 128:(c + 1) * 128], QN[g][:, :, qs], True, False,
                         r=[t_KX[c], t_QN[i]], w=[tPS[b]])
                    if kind == "sel":
                        P.mm(pv, ea[:, c * 128:(c + 1) * 128], rb[:, :, :], False, True,
                             r=[t_tab, t_rb], w=[tPS[b]])
                    else:
                        P.mm(pv, ea[:, c * 128:(c + 1) * 128], rw[:, :, :], False, True,
                             r=[t_tab, t_rw], w=[tPS[b]])
                    qb = QB0 + i
                    addt, t_add = None, t_tab
                    if c == qb:
                        addt = caus[:, :, :]
                    elif kind == "win" and c == qb - 4:
                        addt = acaus[:, :, :]
                k = next_pt()
                if addt is not None:
                    m = next_tmp()
                    P.tt(v4(TMP[m][:, :]), pv, addt, ALU.add, r=[tPS[b], t_add], w=[t_TMP[m]])
                    P.act(PT[k][:, :], TMP[m][:, :], AF.Exp, r=[t_TMP[m]], w=[t_PT[k]])
                else:
                    P.act(PT[k][:, :], PS[b][:, 0:512], AF.Exp, r=[tPS[b]], w=[t_PT[k]])
                return k

            def combine(i, g, bi, o_fn, d_fn, t_acc, first):
                k = srr[0] % 2
                srr[0] += 1
                s_ = sm[k]
                t_s = t_sm[k]
                for r_ in range(4):
                    P.ts(s_[:, r_:r_ + 1], d_fn(r_), 1e-30, ALU.max, r=t_acc, w=[t_s])
                P.op("dve", lambda e: e.reciprocal(out=s_[:, 4:8], in_=s_[:, 0:4]), r=[t_s], w=[t_s])
                P.tt(s_[:, 8:12], s_[:, 4:8], GA[:, i, g * 12 + bi:g * 12 + 12:3], ALU.mult,
                     r=[t_s, t_GA[i]], w=[t_s])
                on = ONSA[i % 2]
                for r_ in range(4):
                    h0 = (4 * g + r_) * 64
                    if first:
                        P.ts(on[:, h0:h0 + 64], o_fn(r_), s_[:, 8 + r_:9 + r_], ALU.mult,
                             r=t_acc + [t_s], w=[t_ONSA[i % 2]])
                    else:
                        P.stt(on[:, h0:h0 + 64], o_fn(r_), s_[:, 8 + r_:9 + r_], on[:, h0:h0 + 64], ALU.mult, ALU.add,
                              r=t_acc + [t_s], w=[t_ONSA[i % 2]])
                return s_, t_s

            for i in range(NQB):
                qb = QB0 + i
                P.dma("sp", Bc[i % 2][:], c_bcmp[i], w=[t_Bc[i % 2]])
                for g in range(2):
                    rb = RB[g][i % 2]
                    t_rb = t_RB[g][i % 2]
                    P.dma("sp", rb[64:68, :, :], c_rbc[:, i, g, :, :], w=[t_rb])
                    P.dma("sp", RW[g][i % 2][64:68, :, :], c_rbc[:, i, g, :, :], w=[t_RW[g][i % 2]])
                    for kc in range(2):
                        k = s_tile(None, "cmp", g, i, kc)
                        for r_ in range(4):
                            bb = 4 + r_ // 2
                            P.mm(PS[bb][:, (r_ % 2) * 130:(r_ % 2) * 130 + 130], PT[k][:, r_ * 128:(r_ + 1) * 128],
                                 VCX[:, kc, g, :], start=(kc == 0 and r_ % 2 == 0), stop=(kc == 1),
                                 r=[t_PT[k], t_VCX], w=[tPS[bb]], skip=True)
                    t_acc = [tPS[4], tPS[5]]
                    s_, t_s = combine(i, g, 0, lambda r_: PS[4 + r_ // 2][:, (r_ % 2) * 130:(r_ % 2) * 130 + 64],
                                      lambda r_: PS[4 + r_ // 2][:, (r_ % 2) * 130 + 64:(r_ % 2) * 130 + 65],
                                      t_acc, True)
                    ik = (2 * i + g) % 2
                    im = impn[ik]
                    iw = impw[ik]
                    t_im = t_imp[ik]
                    for r_ in range(4):
                        src = PS[4 + r_ // 2][:, (r_ % 2) * 130 + 66:(r_ % 2) * 130 + 130]
                        if r_ == 0:
                            P.ts(im[:, :], src, s_[:, 4:5], ALU.mult, r=t_acc + [t_s], w=[t_im])
                        else:
                            P.stt(im[:, :], src, s_[:, 4 + r_:5 + r_], im[:, :], ALU.mult, ALU.add,
                                  r=t_acc + [t_s], w=[t_im])
                    P.tt(im[:, :], im[:, :], ma[:, i, :], ALU.mult, r=[t_im, t_tab], w=[t_im])
                    P.tt(im[:, :], im[:, :], mb[:, i, :], ALU.add, r=[t_im, t_tab], w=[t_im])
                    P.op("dve", lambda e: e.max(out=s_[:, 16:24], in_=im[:, :]), r=[t_im], w=[t_s])
                    P.op("dve", lambda e: e.match_replace(out=iw[:, :], in_to_replace=s_[:, 16:24], in_values=im[:, :],
                                                          imm_value=-1.0e9), r=[t_im, t_s], w=[t_im])
                    P.op("dve", lambda e: e.max(out=s_[:, 24:32], in_=iw[:, :]), r=[t_im], w=[t_s])
                    P.ts(s_[:, 12:13], s_[:, 31:32], 0.0, ALU.max, r=[t_s], w=[t_s])
                    P.ts(iw[:, :], im[:, :], s_[:, 12:13], ALU.is_ge, r=[t_im, t_s], w=[t_im])
                    P.ts(iw[:, :], iw[:, :], -1.0, ALU.add, -NEG, ALU.mult, r=[t_im], w=[t_im])
                    P.tr(PS[6][0:64, 0:128], iw[:, 0:64], ident[:, :], r=[t_im, t_const], w=[tPS[6]])
                    P.copy("act", rb[0:64, :, :], PS[6][0:64, 0:128].unsqueeze(1).broadcast_to([64, 4, 128]),
                           r=[tPS[6]], w=[t_rb])
                    for c in range(qb + 1):
                        k = s_tile(None, "sel", g, i, c)
                        for r_ in range(4):
                            P.mm(PS[2][:, r_ * 65:(r_ + 1) * 65], PT[k][:, r_ * 128:(r_ + 1) * 128], VS[:, c, g, 0:65],
                                 start=(c == 0 and r_ == 0), stop=(c == qb), r=[t_PT[k], t_VS[c]], w=[tPS[2]], skip=True)
                    combine(i, g, 1, lambda r_: PS[2][:, r_ * 65:r_ * 65 + 64],
                            lambda r_: PS[2][:, r_ * 65 + 64:r_ * 65 + 65], [tPS[2]], False)
                    for dl in range(4, -1, -1):
                        c = qb - dl
                        k = s_tile(None, "win", g, i, c)
                        for r_ in range(4):
                            P.mm(PS[3][:, r_ * 65:(r_ + 1) * 65], PT[k][:, r_ * 128:(r_ + 1) * 128], VW[:, c, g, 0:65],
                                 start=(dl == 4 and r_ == 0), stop=(dl == 0), r=[t_PT[k], t_VW[c]], w=[tPS[3]], skip=True)
                    combine(i, g, 2, lambda r_: PS[3][:, r_ * 65:r_ * 65 + 64],
                            lambda r_: PS[3][:, r_ * 65 + 64:r_ * 65 + 65], [tPS[3]], False)
                on = ONSA[i % 2]
                for fc in range(4):
                    P.tr(PS[6][:, fc * 128:(fc + 1) * 128], on[:, fc * 128:(fc + 1) * 128], ident[:, :],
                         r=[t_ONSA[i % 2], t_const], w=[tPS[6]])
                P.copy("act", OAT[:, :, i * 128:(i + 1) * 128], v4(PS[6][:, 0:512]), r=[tPS[6]], w=[t_OAT[i]])
            P.barrier()
        if stage == 3:
            dump("OAT", OAT[:, :, :], t_OAT)
            return done()

        OBT = sb(PA, "OBT", [128, 4, NQ], BF16)
        t_OBT = trks(NQB)
        TQ0 = 1536
        with contextlib.ExitStack() as dl_:
            accD = sb(dl_, "accD", [128, 4, NQ], F32)
            t_acc = trks(NQB)
            hq = [sb(dl_, "hq%d" % k, [128, 8, 1024], BF16) for k in range(2)]
            t_hq = trks(2)
            wd = sb(dl_, "wd", [128, 8, 768], BF16)
            t_wd = Trk()
            QD = sb(dl_, "QD", [128, 2, T - TQ0], BF16)
            KD = sb(dl_, "KD", [128, 2, T], BF16)
            VT = sb(dl_, "VT", [128, 2, T], BF16)
            VD = sb(dl_, "VD", [128, NCH, 4, 66], BF16)
            BD = sb(dl_, "BD", [128, 2, 4, 128], F32)
            vdd = sb(dl_, "vdd", [128, 3, 32], BF16)
            t_QD, t_KD, t_VT, t_VD, t_BD, t_vdd = Trk(), Trk(), Trk(), Trk(), Trk(), Trk()
            P.dma("sp", vdd[:], c_validd[:, :, :], w=[t_vdd])
            hqrr = [0]
            for gd, (w_, r_) in enumerate(DIL):
                J = T // r_
                JQ = (T - TQ0) // r_
                jq0 = TQ0 // r_
                npc = J // 128
                c0w = 1304 + gd * 256
                load_wt(wd, t_wd, w_in, [(c0w, 256), (c0w + 768, 256), (c0w + 1536, 256)])
                P.dma("sp", BD[:], c_bdil[:, gd, :, :, :], w=[t_BD])
                P.copy("dve", VD[:, :, :, 64], vdd[:, gd, :].unsqueeze(2).broadcast_to([128, NCH, 4]),
                       r=[t_vdd], w=[t_VD])
                KDv = KD[:, :, :].rearrange("p a (rho j) -> p a rho j", rho=r_)
                VTv = VT[:, :, :].rearrange("p a (rho j) -> p a rho j", rho=r_)
                QDv = QD[:, :, :].rearrange("p a (rho j) -> p a rho j", rho=r_)
                for qt in range(4):
                    k = hqrr[0] % 2
                    hqrr[0] += 1
                    P.dma("sp", hq[k][:, :, :], hT_d[:, :, qt * 1024:(qt + 1) * 1024], r=t_hTd, w=[t_hq[k]])
                    for n0 in range(0, 1024, 512):
                        g0 = qt * 1024 + n0
                        for which in range(3):
                            if which == 0 and g0 < TQ0:
                                continue
                            for pr in range(2):
                                b = nextps(0, 4)
                                for dm in range(8):
                                    P.mm(PS[b][:, 0:512], wd[:, dm, which * 256 + pr * 128:which * 256 + pr * 128 + 128],
                                         hq[k][:, dm, n0:n0 + 512], start=(dm == 0), stop=(dm == 7),
                                         r=[t_wd, t_hq[k]], w=[tPS[b]])
                                src = PS[b][:, 0:512].rearrange("p (j rho) -> p rho j", rho=r_)
                                nj = 512 // r_
                                if which == 0:
                                    j0 = (g0 - TQ0) // r_
                                    P.copy(evac_eng(), QDv[:, pr, :, j0:j0 + nj], src, r=[tPS[b]], w=[t_QD],
                                           scale=0.125)
                                elif which == 1:
                                    j0 = g0 // r_
                                    P.copy(evac_eng(), KDv[:, pr, :, j0:j0 + nj], src, r=[tPS[b]], w=[t_KD])
                                else:
                                    j0 = g0 // r_
                                    P.copy(evac_eng(), VTv[:, pr, :, j0:j0 + nj], src, r=[tPS[b]], w=[t_VT])
                for ci in range(NCH):
                    for pr in range(2):
                        P.tr(PST[:, pr * 128:(pr + 1) * 128], VT[:, pr, ci * 128:(ci + 1) * 128], identb[:],
                             r=[t_VT, t_const], w=[tPST])
                    P.copy(evac_eng(), VD[:, ci, :, 0:64], PST[:, 0:256].rearrange("p (h d) -> p h d", h=4),
                           r=[tPST], w=[t_VD])
                for rho in range(r_):
                    for jb in range(npc):
                        jmin = -(-(Q0 - rho) // r_)
                        q_lo = max(0, jmin - 128 * jb)
                        if q_lo >= 128:
                            continue
                        nq = 128 - q_lo
                        dls = [dl for dl in (0, 1) if jb - dl >= 0]
                        pts = {}
                        for dl in dls:
                            jc = jb - dl
                            kk = next_pt()
                            pts[dl] = kk
                            for hh in range(4):
                                par, pr = hh % 2, hh // 2
                                base = 64 * par
                                kc0 = rho * J + jc * 128
                                qc0 = rho * JQ + (jb * 128 + q_lo - jq0)
                                P.mm(PS[par][:, pr * 128 + q_lo:pr * 128 + 128],
                                     KD[base:base + 64, pr, kc0:kc0 + 128], QD[base:base + 64, pr, qc0:qc0 + nq],
                                     True, True, r=[t_KD, t_QD], w=[tPS[par]])
                            for par in range(2):
                                m = next_tmp()
                                tv = TMP[m][:, 0:256].rearrange("p (a b) -> p a b", a=2)[:, :, q_lo:128]
                                pv_ = PS[par][:, 0:256].rearrange("p (a b) -> p a b", a=2)[:, :, q_lo:128]
                                P.tt(tv, pv_, BD[:, dl, par::2, q_lo:128], ALU.add, r=[tPS[par], t_BD], w=[t_TMP[m]])
                                P.act(v4(PT[kk][:, :])[:, par::2, q_lo:128], tv, AF.Exp, r=[t_TMP[m]], w=[t_PT[kk]])
                        ab = 2 + ((rho * npc + jb) % 2)
                        for hh in range(4):
                            for n_, dl in enumerate(dls):
                                ci = rho * npc + (jb - dl)
                                P.mm(PS[ab][0:65, hh * 128 + q_lo:hh * 128 + 128], VD[:, ci, hh, 0:65],
                                     PT[pts[dl]][:, hh * 128 + q_lo:hh * 128 + 128],
                                     start=(n_ == 0), stop=(n_ == len(dls) - 1),
                                     r=[t_PT[pts[dl]], t_VD], w=[tPS[ab]])
                        tok0 = rho + r_ * (128 * jb + q_lo) - Q0
                        tok1 = rho + r_ * (128 * jb + 127) - Q0
                        blks = [t_acc[c] for c in range(tok0 // 128, tok1 // 128 + 1)]
                        dst = accD[0:65, :, tok0:tok1 + 1:r_]
                        srcp = v4(PS[ab][0:65, 0:512])[:, :, q_lo:128]
                        if gd == 0:
                            P.copy("dve", dst, srcp, r=[tPS[ab]], w=blks)
                        else:
                            P.tt(dst, srcp, dst, ALU.add, r=[tPS[ab]] + blks, w=blks)
            s65 = sb(dl_, "s65", [128, 64], F32)
            t_s65 = Trk()
            P.dma("sp", s65[0:65, :], c_s65[:, :], w=[t_s65])
            for hh in range(4):
                n0 = 0
                while n0 < NQ:
                    n = min(512, NQ - n0)
                    b = nextps(4, 6)
                    tb = [t_acc[c] for c in chunks_of(n0, n)]
                    P.mm(PS[b][0:64, 0:n], s65[0:65, 0:64], accD[0:65, hh, n0:n0 + n], True, True,
                         r=[t_s65] + tb, w=[tPS[b]])
                    m = next_tmp()
                    P.ts(TMP[m][0:64, 0:n], PS[b][0:64, 0:n], 1e-30, ALU.max, r=[tPS[b]], w=[t_TMP[m]])
                    P.op("dve", lambda e, m=m, n=n: e.reciprocal(out=TMP[m][0:64, 0:n], in_=TMP[m][0:64, 0:n]),
                         r=[t_TMP[m]], w=[t_TMP[m]])
                    P.tt(OBT[0:64, hh, n0:n0 + n], accD[0:64, hh, n0:n0 + n], TMP[m][0:64, 0:n], ALU.mult,
                         r=tb + [t_TMP[m]], w=[t_OBT[c] for c in chunks_of(n0, n)])
                    n0 += n
            P.barrier()
        if stage == 4:
            dump("OBT", OBT[0:64, :, :], t_OBT)
            return done()

        with contextlib.ExitStack() as mg:
            WPA = sb(mg, "WPA", [128, 4, D], BF16)
            WPB = sb(mg, "WPB", [128, 4, D], BF16)
            WO = sb(mg, "WO", [128, 8, D], BF16)
            MX = sb(mg, "MX", [128, 8, NQ], BF16)
            hTq = sb(mg, "hTq", [128, 8, NQ], BF16)
            wm = [sb(mg, "wm%d" % k, [128, 8, 256], BF16) for k in range(2)]
            xr = [sb(mg, "xr%d" % k, [128, D], F32) for k in range(2)]
            t_WPA, t_WPB, t_WO, t_hTq = Trk(), Trk(), Trk(), Trk()
            t_MX = trks(NQB)
            t_wm = trks(2)
            t_xr = trks(2)
            P.dma("pool", WPA[:, :, :], w_pa.rearrange("(c p) n -> p c n", p=128), w=[t_WPA])
            P.dma("pool", WPB[0:64, :, :], w_pb.rearrange("(h d) n -> d h n", d=64), w=[t_WPB])
            P.dma_multi("pool", [(WO[:, 0:4, :], w_out[0:512, :].rearrange("(c p) n -> p c n", p=128)),
                                 (WO[:, 4:8, :], w_out[512:1024, :].rearrange("(c p) n -> p c n", p=128))],
                        w=[t_WO])
            P.dma("sp", hTq[:, :, :], hT_d[:, :, Q0:T], r=t_hTd, w=[t_hTq])
            ntiles = []
            n0 = 0
            while n0 < NQ:
                n = min(512, NQ - n0)
                ntiles.append((n0, n))
                n0 += n
            for mc in range(8):
                k = mc % 2
                load_wt(wm[k], t_wm[k], w_in, [(3608 + mc * 128, 128), (4632 + mc * 128, 128)])
                for (n0, n) in ntiles:
                    tb = chunks_of(n0, n)
                    b1, b2_, b3, b4 = [nextps(0, 7) for _ in range(4)]
                    for fc in range(4):
                        P.mm(PS[b1][:, 0:n], WPA[:, fc, mc * 128:(mc + 1) * 128], OAT[:, fc, n0:n0 + n],
                             fc == 0, fc == 3, r=[t_WPA] + [t_OAT[c] for c in tb], w=[tPS[b1]])
                    for hh in range(4):
                        P.mm(PS[b2_][:, 0:n], WPB[0:64, hh, mc * 128:(mc + 1) * 128], OBT[0:64, hh, n0:n0 + n],
                             hh == 0, hh == 3, r=[t_WPB] + [t_OBT[c] for c in tb], w=[tPS[b2_]])
                    for dm in range(8):
                        P.mm(PS[b3][:, 0:n], wm[k][:, dm, 0:128], hTq[:, dm, n0:n0 + n], dm == 0, dm == 7,
                             r=[t_wm[k], t_hTq], w=[tPS[b3]])
                    for dm in range(8):
                        P.mm(PS[b4][:, 0:n], wm[k][:, dm, 128:256], hTq[:, dm, n0:n0 + n], dm == 0, dm == 7,
                             r=[t_wm[k], t_hTq], w=[tPS[b4]])
                    ma_, mb_ = next_tmp(), next_tmp()
                    P.act(TMP[ma_][:, 0:n], PS[b3][:, 0:n], AF.Sigmoid, r=[tPS[b3]], w=[t_TMP[ma_]])
                    P.act(TMP[mb_][:, 0:n], PS[b4][:, 0:n], AF.Sigmoid, r=[tPS[b4]], w=[t_TMP[mb_]])
                    P.tt(TMP[ma_][:, 0:n], TMP[ma_][:, 0:n], PS[b1][:, 0:n], ALU.mult, r=[t_TMP[ma_], tPS[b1]],
                         w=[t_TMP[ma_]])
                    P.tt(TMP[mb_][:, 0:n], TMP[mb_][:, 0:n], PS[b2_][:, 0:n], ALU.mult, r=[t_TMP[mb_], tPS[b2_]],
                         w=[t_TMP[mb_]])
                    P.tt(MX[:, mc, n0:n0 + n], TMP[ma_][:, 0:n], TMP[mb_][:, 0:n], ALU.add,
                         r=[t_TMP[ma_], t_TMP[mb_]], w=[t_MX[c] for c in tb])
            for i in range(NQB):
                k = i % 2
                P.dma("sp", xr[k][:, :], x[Q0 + i * 128:Q0 + (i + 1) * 128, :], w=[t_xr[k]])
                for half in range(2):
                    b = nextps(0, 7)
                    for mc in range(8):
                        P.mm(PS[b][:, 0:512], MX[:, mc, i * 128:(i + 1) * 128], WO[:, mc, half * 512:(half + 1) * 512],
                             mc == 0, mc == 7, r=[t_MX[i], t_WO], w=[tPS[b]])
                    P.tt(xr[k][:, half * 512:(half + 1) * 512], PS[b][:, 0:512], xr[k][:, half * 512:(half + 1) * 512],
                         ALU.add, r=[tPS[b], t_xr[k]], w=[t_xr[k]])
                P.dma("sp", xm_d[i * 128:(i + 1) * 128, :], xr[k][:, :], r=[t_xr[k]], w=[t_xmd[i]])
            P.barrier()
        if stage == 5:
            return done()
    P.barrier()

    with contextlib.ExitStack() as FF:
        H2T = sb(FF, "H2T", [128, 8, 2050], BF16)
        t_H2T = trks(NQB)
        gfr = sb(FF, "gfr", [128, D], F32)
        gfin = sb(FF, "gfin", [128, D], F32)
        t_g = Trk()
        P.dma("sp", gfr[:], g_ffn[0:1, :].partition_broadcast(128), w=[t_g])
        P.dma("sp", gfin[:], g_fin[0:1, :].partition_broadcast(128), w=[t_g])
        cwr = sb(FF, "cwr", [128, 4, 128], F32)
        cw = sb(FF, "cw", [128, 4, 22], F32)
        t_cw = Trk()
        P.dma("sp", cwr[0:22, 0:3, :], conv_w.rearrange("k (j p) -> j k p", p=128), w=[t_cw])
        P.dma("sp", cwr[0:22, 3, :], conv_b.rearrange("o (j p) -> (o j) p", p=128), w=[t_cw])
        for kk in range(4):
            b = nextps(0, 7)
            P.tr(PS[b][:, 0:22], cwr[0:22, kk, :], ident[0:22, 0:22], r=[t_cw, t_const], w=[tPS[b]])
            P.copy("dve", cw[:, kk, :], PS[b][:, 0:22], r=[tPS[b]], w=[t_cw])
        xt2 = [sb(FF, "fx%d" % i, [128, D], F32) for i in range(2)]
        t_xt = trks(2)
        xn2 = [sb(FF, "fn%d" % i, [128, D], BF16) for i in range(2)]
        t_xn = trks(2)
        yo2 = [sb(FF, "yo%d" % i, [128, D], F32) for i in range(2)]
        t_yo = trks(2)
        junk = sb(FF, "fjunk", [128, D], F32)
        ss2 = [sb(FF, "fss%d" % i, [128, 4], F32) for i in range(2)]
        t_ss = trks(2)
        for i in range(NQB):
            k = i % 2
            P.dma("sp", xt2[k][:], xm_d[i * 128:(i + 1) * 128, :], r=[t_xmd[i]], w=[t_xt[k]])
            rmsnorm_tile(junk, ss2[k], t_ss[k], xt2[k][:], t_xt[k], gfr[:], t_g, xn2[k][:], t_xn[k])
            for c in range(8):
                P.tr(PST[:, c * 128:(c + 1) * 128], xn2[k][:, c * 128:(c + 1) * 128], identb[:],
                     r=[t_xn[k], t_const], w=[tPST])
            pv8 = PST[:, :].rearrange("p (c n) -> p c n", c=8)
            if i == 0:
                P.ts(H2T[:, :, 0:2], pv8[:, :, 126:128], halo[:, 0:1], ALU.mult, r=[tPST, t_const], w=[t_H2T[0]])
            else:
                P.copy(evac_eng(), H2T[:, :, 2 + (i - 1) * 128:2 + i * 128], pv8, r=[tPST], w=[t_H2T[i]])
        if stage == 6:
            dump("H2T", H2T[:, :, 0:258], t_H2T)
            dump("cw", cw[:, :, :], [t_cw])
            return done()
        WD = sb(FF, "WD", [128, NFF, D], BF16)
        t_WD = Trk()
        wdv = w_down.rearrange("(j p) n -> p j n", p=128)
        P.dma_multi("pool", [(WD[:, j0:min(j0 + 6, NFF), :], wdv[:, j0:min(j0 + 6, NFF), :]) for j0 in range(0, NFF, 6)],
                    w=[t_WD])
        AT = sb(FF, "AT", [128, NFF, 1024], BF16)
        t_AT = trks(NFF)
        wu = [sb(FF, "wu%d" % k, [128, 8, 256], BF16) for k in range(2)]
        t_wu = trks(2)
        U = [sb(FF, "U%d" % k, [128, 1026], F32) for k in range(2)]
        t_U = trks(2)
        C1 = sb(FF, "C1", [128, 1024], F32)
        C2 = sb(FF, "C2", [128, 1024], F32)
        t_C1, t_C2 = Trk(), Trk()
        FG = [sb(FF, "FG%d" % k, [128, 512], F32) for k in range(2)]
        t_FG = trks(2)

        def gelu2(out, t_out, xin, t_x, n):
            a, b2 = FG[0][:, 0:n], FG[1][:, 0:n]
            P.tt(a, xin, xin, ALU.mult, r=[t_x], w=[t_FG[0]])
            P.ts(a, a, 0.044715, ALU.mult, 1.0, ALU.add, r=[t_FG[0]], w=[t_FG[0]])
            P.tt(a, a, xin, ALU.mult, r=[t_FG[0], t_x], w=[t_FG[0]])
            P.act(b2, a, AF.Sigmoid, r=[t_FG[0]], w=[t_FG[1]], scale=1.5957691216057308)
            P.tt(out, b2, xin, ALU.mult, r=[t_FG[1], t_x], w=t_out)

        jrr = [0]
        for th in range(2):
            base = 2 + th * 1024
            hts = [t_H2T[c] for c in range(max(0, th * 8), th * 8 + 9)]
            for j in range(NFF):
                k = jrr[0] % 2
                jrr[0] += 1
                P.dma_multi("pool", [(wu[k][:, :, 0:128], w_up[:, j * 128:(j + 1) * 128].rearrange("(c p) n -> p c n", p=128)),
                                     (wu[k][:, :, 128:256],
                                      w_up[:, DFF + j * 128:DFF + (j + 1) * 128].rearrange("(c p) n -> p c n", p=128))],
                            w=[t_wu[k]])
                b = nextps(0, 7)
                for dm in range(8):
                    P.mm(PS[b][:, 0:2], wu[k][:, dm, 0:128], H2T[:, dm, base - 2:base], dm == 0, dm == 7,
                         r=[t_wu[k]] + hts, w=[tPS[b]])
                P.copy("dve", U[k][:, 0:2], PS[b][:, 0:2], r=[tPS[b]], w=[t_U[k]])
                for nt in range(2):
                    b = nextps(0, 7)
                    for dm in range(8):
                        P.mm(PS[b][:, 0:512], wu[k][:, dm, 0:128], H2T[:, dm, base + nt * 512:base + (nt + 1) * 512],
                             dm == 0, dm == 7, r=[t_wu[k]] + hts, w=[tPS[b]])
                    P.copy("act", U[k][:, 2 + nt * 512:2 + (nt + 1) * 512], PS[b][:, 0:512], r=[tPS[b]], w=[t_U[k]])
                bg = []
                for nt in range(2):
                    b = nextps(0, 7)
                    bg.append(b)
                    for dm in range(8):
                        P.mm(PS[b][:, 0:512], wu[k][:, dm, 128:256], H2T[:, dm, base + nt * 512:base + (nt + 1) * 512],
                             dm == 0, dm == 7, r=[t_wu[k]] + hts, w=[tPS[b]])
                P.ts(C1[:, :], U[k][:, 2:1026], cw[:, 2, j:j + 1], ALU.mult, cw[:, 3, j:j + 1], ALU.add,
                     r=[t_U[k], t_cw], w=[t_C1])
                P.stt(C1[:, :], U[k][:, 1:1025], cw[:, 1, j:j + 1], C1[:, :], ALU.mult, ALU.add,
                      r=[t_U[k], t_cw, t_C1], w=[t_C1])
                P.stt(C1[:, :], U[k][:, 0:1024], cw[:, 0, j:j + 1], C1[:, :], ALU.mult, ALU.add,
                      r=[t_U[k], t_cw, t_C1], w=[t_C1])
                for nt in range(2):
                    sl = slice(nt * 512, (nt + 1) * 512)
                    gelu2(C2[:, sl], [t_C2], C1[:, sl], t_C1, 512)
                    P.tt(AT[:, j, sl], C2[:, sl], PS[bg[nt]][:, 0:512], ALU.mult, r=[t_C2, tPS[bg[nt]]], w=[t_AT[j]])
            if stage == 7:
                dump("AT", AT[:, 0:2, :], t_AT)
                return done()
            for kt in range(8):
                i = 1 + th * 8 + kt
                k = kt % 2
                P.dma("sp", xt2[k][:], xm_d[i * 128:(i + 1) * 128, :], r=[t_xmd[i]], w=[t_xt[k]])
                for half in range(2):
                    b = nextps(0, 7)
                    for j in range(NFF):
                        P.mm(PS[b][:, 0:512], AT[:, j, kt * 128:(kt + 1) * 128], WD[:, j, half * 512:(half + 1) * 512],
                             j == 0, j == NFF - 1, r=[t_AT[j], t_WD], w=[tPS[b]])
                    P.tt(xt2[k][:, half * 512:(half + 1) * 512], PS[b][:, 0:512], xt2[k][:, half * 512:(half + 1) * 512],
                         ALU.add, r=[tPS[b], t_xt[k]], w=[t_xt[k]])
                rmsnorm_tile(junk, ss2[k], t_ss[k], xt2[k][:], t_xt[k], gfin[:], t_g, yo2[k][:], t_yo[k])
                P.dma("sp", y[(th * 8 + kt) * 128:(th * 8 + kt + 1) * 128, :], yo2[k][:], r=[t_yo[k]])
    return done()


W_NAMES = ["g_mix", "w_in", "pe_cmp_k", "w_cmp_k1", "w_cmp_k2", "pe_cmp_v", "w_cmp_v1", "w_cmp_v2",
           "w_proj_nsa", "w_proj_dil", "w_out", "g_ffn", "w_up", "conv_w", "conv_b", "w_down", "g_final"]


def make_in_maps(inputs, cores):
    x = np.asarray(inputs["x"], dtype=np.float32)
    shared = {}
    for n in W_NAMES:
        a = np.asarray(inputs[n], dtype=np.float32)
        if n == "g_final":
            a = a.reshape(1, D)
        elif a.shape[0] == 1:
            a = a[0]
        if a.ndim == 1:
            a = a.reshape(1, -1)
        shared[n] = np.ascontiguousarray(a)
    tabs = [const_tables(0), const_tables(1)]
    maps = []
    for (b, hf) in cores:
        m = dict(shared)
        if hf == 1:
            xl = x[b]
        else:
            xl = np.concatenate([np.zeros((2048, D), np.float32), x[b, :2048]], axis=0)
        m["x"] = np.ascontiguousarray(xl)
        for k, v in tabs[hf].items():
            m["c_" + k] = v
        maps.append(m)
    return maps


_NC_CACHE = {}


N_LAUNCH = 4


def kernel(**inputs):
    if "nc" not in _NC_CACHE:
        _NC_CACHE["nc"] = build()
    nc = _NC_CACHE["nc"]
    cores = [(b, hf) for b in range(4) for hf in range(2)]
    out = np.zeros((4, T, D), np.float32)
    per = len(cores) // N_LAUNCH
    for li in range(N_LAUNCH):
        cs = cores[li * per:(li + 1) * per]
        maps = make_in_maps(inputs, cs)
        res = run_bass_kernel_spmd(nc, maps, core_ids=list(range(len(cs))))
        for ci, (b, hf) in enumerate(cs):
            out[b, hf * 2048:(hf + 1) * 2048, :] = res.results[ci]["y"]
    return out
```

```python
import contextlib
import numpy as np
import ml_dtypes
import concourse.bass as bass
import concourse.mybir as mybir
from concourse.bass_utils import run_bass_kernel_spmd

F32 = mybir.dt.float32
BF16 = mybir.dt.bfloat16
AF = mybir.ActivationFunctionType
ALU = mybir.AluOpType
AX = mybir.AxisListType

D = 1024
T = 4096
NCH = 32
QB0 = 15
NQB = 17
NQ = NQB * 128
Q0 = QB0 * 128
DFF = 2816
NFF = 22
DIN = 5656
NEG = -30000.0
DIL = ((128, 1), (512, 4), (2048, 16))
NDS = 48
NDS_HW = 32
SW_LIMIT = 300


class Trk:
    __slots__ = ("w", "r", "excl")

    def __init__(self, excl=False):
        self.w = []
        self.r = []
        self.excl = excl


def trks_ex(n):
    return [Trk(True) for _ in range(n)]


def trks(n):
    return [Trk() for _ in range(n)]


class Prog:
    def __init__(self):
        self.nc = bass.Bass("TRN2", target_bir_lowering=False)
        nc = self.nc
        self.es = contextlib.ExitStack()
        self.E = {"pe": nc.tensor, "act": nc.scalar, "dve": nc.vector, "pool": nc.gpsimd, "sp": nc.sync}
        self.sem = {k: self.es.enter_context(nc.semaphore("s_" + k)) for k in self.E}
        self.cnt = {k: 0 for k in self.E}
        self.seen = {k: {} for k in self.E}
        self.seen_seq = {k: {} for k in self.E}
        self.opseq = {k: 0 for k in self.E}
        self.last_ins = {k: None for k in self.E}
        self.sigmap = {k: [] for k in self.E}
        self.dsem = [self.es.enter_context(nc.semaphore("d%d" % i)) for i in range(NDS)]
        self.dcnt = [0] * NDS
        self.dnext = 0
        self.dnext_sw = NDS_HW
        self.sw_out = []
        self.ninst = 0

    def _resolve(self, key, seq):
        sm = self.sigmap[key]
        lo, hi = 0, len(sm)
        while lo < hi:
            mid = (lo + hi) // 2
            if sm[mid][0] >= seq:
                hi = mid
            else:
                lo = mid + 1
        if lo < len(sm):
            return sm[lo][1]
        self.last_ins[key].then_inc(self.sem[key], 1)
        self.cnt[key] += 1
        sm.append((self.opseq[key], self.cnt[key]))
        return self.cnt[key]

    def _wait(self, eng, dep):
        key, val = dep
        if key == "pe" and eng == "pe":
            return
        if isinstance(key, str):
            if self.seen_seq[eng].get(key, 0) >= val:
                return
            self.seen_seq[eng][key] = val
            val = self._resolve(key, val)
        if self.seen[eng].get(key, 0) >= val:
            return
        self.seen[eng][key] = val
        sem = self.sem[key] if isinstance(key, str) else self.dsem[key[1]]
        self.E[eng].wait_ge(sem, val)

    def _deps(self, eng, r, w):
        deps = set()
        for t in r:
            deps.update(t.w)
            if t.excl:
                deps.update(d for d in t.r if d[0] != eng)
        for t in w:
            deps.update(t.w)
            deps.update(t.r)
        best = {}
        for (k, v) in deps:
            if best.get(k, 0) < v:
                best[k] = v
        for k in sorted(best, key=str):
            self._wait(eng, (k, best[k]))

    def _mark(self, toks, r, w):
        for t in w:
            t.w = list(toks)
            t.r = []
        for t in r:
            if t not in w:
                t.r.extend(toks)

    def op(self, eng, fn, r=(), w=()):
        self._deps(eng, r, w)
        ins = fn(self.E[eng])
        self.opseq[eng] += 1
        self.last_ins[eng] = ins
        self.ninst += 1
        tok = (eng, self.opseq[eng])
        self._mark([tok], r, w)
        return tok

    def dma(self, q, out, in_, r=(), w=()):
        return self.dma_multi(q, [(out, in_)], r=r, w=w)

    def dma_multi(self, q, pieces, r=(), w=()):
        self._deps(q, r, w)
        toks = []
        for (out, in_) in pieces:
            if q == "pool":
                nd = 1
                for d_ in list(out.shape)[:-1]:
                    nd *= int(d_)
                nd = nd // 16 + 2
                while self.sw_out and sum(n_ for (_, n_) in self.sw_out) + nd > SW_LIMIT:
                    tok0, _ = self.sw_out.pop(0)
                    self._wait(q, tok0)
                j = self.dnext_sw
                self.dnext_sw = NDS_HW + (j + 1 - NDS_HW) % (NDS - NDS_HW)
            else:
                j = self.dnext
                self.dnext = (j + 1) % NDS_HW
            if self.dcnt[j] > 0:
                self._wait(q, (("d", j), 16 * self.dcnt[j]))
            self.dcnt[j] += 1
            self.E[q].dma_start(out=out, in_=in_).then_inc(self.dsem[j], 16)
            self.ninst += 1
            toks.append((("d", j), 16 * self.dcnt[j]))
            if q == "pool":
                self.sw_out.append((toks[-1], nd))
        self._mark(toks, r, w)
        return toks

    def barrier(self):
        for e in self.E:
            for e2 in self.E:
                if e2 != e and self.opseq[e2] > 0:
                    self._wait(e, (e2, self.opseq[e2]))
            for j in range(NDS):
                if self.dcnt[j] > 0:
                    self._wait(e, (("d", j), 16 * self.dcnt[j]))

    def mm(self, out, lhsT, rhs, start, stop, r=(), w=(), skip=False):
        if skip:
            return self.op("pe", lambda e: e.matmul(out, lhsT=lhsT, rhs=rhs, start=start, stop=stop,
                                                    skip_group_check=True), r=r, w=w)
        return self.op("pe", lambda e: e.matmul(out, lhsT=lhsT, rhs=rhs, start=start, stop=stop), r=r, w=w)

    def tr(self, out, in_, ident, r=(), w=()):
        return self.op("pe", lambda e: e.transpose(out, in_, ident), r=r, w=w)

    def act(self, out, in_, func, r=(), w=(), **kw):
        return self.op("act", lambda e: e.activation(out=out, in_=in_, func=func, **kw), r=r, w=w)

    def copy(self, eng, out, in_, r=(), w=(), scale=None):
        if eng == "act":
            if scale is None:
                return self.act(out, in_, AF.Copy, r=r, w=w)
            return self.act(out, in_, AF.Copy, r=r, w=w, scale=float(scale))
        if scale is None:
            return self.op(eng, lambda e: e.tensor_copy(out=out, in_=in_), r=r, w=w)
        return self.op(eng, lambda e: e.tensor_scalar(out=out, in0=in_, scalar1=float(scale), scalar2=None,
                                                      op0=ALU.mult), r=r, w=w)

    def tt(self, out, in0, in1, op, r=(), w=(), eng="dve"):
        return self.op(eng, lambda e: e.tensor_tensor(out=out, in0=in0, in1=in1, op=op), r=r, w=w)

    def ts(self, out, in0, s1, op0, s2=None, op1=None, r=(), w=(), eng="dve"):
        if op1 is None:
            return self.op(eng, lambda e: e.tensor_scalar(out=out, in0=in0, scalar1=s1, scalar2=None, op0=op0),
                           r=r, w=w)
        return self.op(eng, lambda e: e.tensor_scalar(out=out, in0=in0, scalar1=s1, scalar2=s2, op0=op0, op1=op1),
                       r=r, w=w)

    def stt(self, out, in0, scalar, in1, op0, op1, r=(), w=()):
        return self.op("dve", lambda e: e.scalar_tensor_tensor(out=out, in0=in0, scalar=scalar, in1=in1,
                                                               op0=op0, op1=op1), r=r, w=w)


def alibi(n):
    return (2.0 ** (-8.0 * np.arange(1, n + 1) / n)).astype(np.float64)


def bf(a):
    return np.asarray(a, dtype=np.float32).astype(ml_dtypes.bfloat16)


def const_tables(hf):
    c = {}
    c["ident"] = np.eye(128, dtype=np.float32)
    c["identb"] = bf(np.eye(128))
    pos = np.arange(T)
    gpos = pos if hf == 1 else pos - 2048
    valid = (gpos >= 0).astype(np.float32)
    c["validtm"] = bf(valid.reshape(NCH, 128).T)
    vd = np.zeros((128, 3, 32), np.float32)
    for g, (w, r) in enumerate(DIL):
        npc = 32 // r
        for rho in range(r):
            for jc in range(npc):
                tok = rho + r * (128 * jc + np.arange(128))
                vd[:, g, rho * npc + jc] = valid[tok]
    c["validd"] = bf(vd)
    c["halo"] = np.full((128, 1), 1.0 if hf == 1 else 0.0, np.float32)
    ea = np.zeros((128, T), np.float32)
    ea[pos // 64, pos] = 1.0
    ea[64] = 128.0 * (pos // 128)
    ea[65] = pos % 128
    ea[66] = 1.0
    ea[67] = 1.0
    c["ea"] = bf(ea)
    sl = alibi(8)
    rbc = np.zeros((4, NQB, 2, 4, 128), np.float32)
    ql = np.arange(128)
    for i in range(NQB):
        qb = QB0 + i
        for g in range(2):
            for r in range(4):
                s = sl[4 * g + r]
                rbc[0, i, g, r, :] = s
                rbc[1, i, g, r, :] = s
                rbc[2, i, g, r, :] = -s * 128.0 * qb
                rbc[3, i, g, r, :] = -s * ql
    c["rbc"] = bf(rbc)
    kk = np.arange(128)[:, None]
    qq = np.arange(128)[None, :]
    caus = np.where(kk <= qq, 0.0, NEG).astype(np.float32)
    acaus = np.where(kk > qq, 0.0, NEG).astype(np.float32)
    c["caus"] = np.ascontiguousarray(np.broadcast_to(caus[:, None, :], (128, 4, 128)))
    c["acaus"] = np.ascontiguousarray(np.broadcast_to(acaus[:, None, :], (128, 4, 128)))
    bc = np.zeros((NQB, 128, 2, 2, 4, 128), np.float32)
    for i in range(NQB):
        t = (QB0 + i) * 128 + np.arange(128)
        for kc in range(2):
            cc = kc * 128 + np.arange(128)
            cend = 16 * cc + 31
            cstart_g = 16 * cc - (0 if hf == 1 else 2048)
            dist = t[None, :] - cend[:, None]
            ok = (dist >= 0) & (cstart_g[:, None] >= 0) & (cc[:, None] < 255)
            for g in range(2):
                for r in range(4):
                    bc[i, :, kc, g, r, :] = np.where(ok, -sl[4 * g + r] * dist, NEG)
    c["bcmp"] = bc
    cc = np.arange(256)
    ss = np.arange(64)
    ov = np.clip(np.minimum(16 * cc[:, None] + 32, 64 * ss[None, :] + 64)
                 - np.maximum(16 * cc[:, None], 64 * ss[None, :]), 0, None) / 32.0
    c["ovl"] = bf(ov.reshape(2, 128, 64).transpose(1, 0, 2))
    ma = np.zeros((128, NQB, 64), np.float32)
    mb = np.zeros((128, NQB, 64), np.float32)
    b0 = 0 if hf == 1 else 32
    for i in range(NQB):
        t = (QB0 + i) * 128 + np.arange(128)
        cur = t // 64
        for s in range(64):
            forced = (s == cur) | (s == b0)
            future = (s > cur) | (s < b0)
            normal = (~forced) & (~future)
            ma[:, i, s] = normal
            mb[:, i, s] = np.where(future, -1.0, np.where(forced, 1.0e4, 0.0))
    c["ma"] = ma
    c["mb"] = mb
    sd = alibi(12).reshape(3, 4)
    bd = np.zeros((128, 3, 2, 4, 128), np.float32)
    for g, (w, r) in enumerate(DIL):
        for dl in range(2):
            dj = dl * 128 + qq - kk
            ok = (dj >= 0) & (dj <= 128)
            for h in range(4):
                bd[:, g, dl, h, :] = np.where(ok, -sd[g, h] * r * dj, NEG)
    c["bdil"] = bd
    s65 = np.zeros((65, 64), np.float32)
    s65[64, :] = 1.0
    c["s65"] = s65
    return c


def build(stage=99):
    P = Prog()
    with P.es:
        return _build(P, stage)


def _build(P, stage):
    nc = P.nc
    es = P.es

    def dram_in(name, shape, dt=F32):
        return nc.dram_tensor(name, list(shape), dt, kind="ExternalInput").ap()

    def sb(stack, name, shape, dt):
        return stack.enter_context(nc.sbuf_tensor(name, list(shape), dt))

    def ps(stack, name, shape, dt):
        return stack.enter_context(nc.psum_tensor(name, list(shape), dt))

    x = dram_in("x", [T, D])
    g_mix = dram_in("g_mix", [1, D])
    w_in = dram_in("w_in", [D, DIN])
    pe_k = dram_in("pe_cmp_k", [32, 64])
    w_k1 = dram_in("w_cmp_k1", [2048, 128])
    w_k2 = dram_in("w_cmp_k2", [128, 64])
    pe_v = dram_in("pe_cmp_v", [32, 64])
    w_v1 = dram_in("w_cmp_v1", [2048, 128])
    w_v2 = dram_in("w_cmp_v2", [128, 64])
    w_pa = dram_in("w_proj_nsa", [512, D])
    w_pb = dram_in("w_proj_dil", [256, D])
    w_out = dram_in("w_out", [D, D])
    g_ffn = dram_in("g_ffn", [1, D])
    w_up = dram_in("w_up", [D, 2 * DFF])
    conv_w = dram_in("conv_w", [3, DFF])
    conv_b = dram_in("conv_b", [1, DFF])
    w_down = dram_in("w_down", [DFF, D])
    g_fin = dram_in("g_final", [1, D])
    c_ident = dram_in("c_ident", [128, 128])
    c_identb = dram_in("c_identb", [128, 128], BF16)
    c_validtm = dram_in("c_validtm", [128, 32], BF16)
    c_validd = dram_in("c_validd", [128, 3, 32], BF16)
    c_halo = dram_in("c_halo", [128, 1])
    c_ea = dram_in("c_ea", [128, T], BF16)
    c_rbc = dram_in("c_rbc", [4, NQB, 2, 4, 128], BF16)
    c_caus = dram_in("c_caus", [128, 4, 128])
    c_acaus = dram_in("c_acaus", [128, 4, 128])
    c_bcmp = dram_in("c_bcmp", [NQB, 128, 2, 2, 4, 128])
    c_ovl = dram_in("c_ovl", [128, 2, 64], BF16)
    c_ma = dram_in("c_ma", [128, NQB, 64])
    c_mb = dram_in("c_mb", [128, NQB, 64])
    c_bdil = dram_in("c_bdil", [128, 3, 2, 4, 128])
    c_s65 = dram_in("c_s65", [65, 64])
    y = nc.dram_tensor("y", [2048, D], F32, kind="ExternalOutput").ap()
    xm_d = nc.dram_tensor("xm_scratch", [NQ, D], F32, kind="Internal").ap()
    t_xmd = trks(NQB)

    def dump(name, ap, trk):
        o = nc.dram_tensor("dbg_" + name, list(ap.shape), ap.dtype, kind="ExternalOutput").ap()
        P.dma("sp", o, ap, r=trk)

    def done():
        for j in range(NDS):
            if P.dcnt[j] > 0:
                P._wait("sp", (("d", j), 16 * P.dcnt[j]))
        return nc

    PS = [ps(es, "ps%d" % i, [128, 512], F32) for i in range(7)]
    PST = ps(es, "pst", [128, 1024], BF16)
    tPS = trks_ex(7)
    tPST = Trk(True)
    psrr = [0]

    def nextps(lo, hi):
        k = lo + psrr[0] % (hi - lo)
        psrr[0] += 1
        return k

    def v4(ap):
        return ap.rearrange("p (a b) -> p a b", a=4)

    ident = sb(es, "ident", [128, 128], F32)
    identb = sb(es, "identb", [128, 128], BF16)
    halo = sb(es, "halo", [128, 1], F32)
    t_const = Trk()
    P.dma("sp", ident[:], c_ident[:, :], w=[t_const])
    P.dma("sp", identb[:], c_identb[:, :], w=[t_const])
    P.dma("sp", halo[:], c_halo[:, :], w=[t_const])

    evac_rr = [0]

    def evac_eng():
        evac_rr[0] += 1
        return "act" if evac_rr[0] % 2 else "dve"

    def rmsnorm_tile(junk, ss, t_s, xt, t_x, gt, t_g, out_t, t_out):
        P.act(junk[:], xt, AF.Square, r=[t_x], w=[t_s], accum_out=ss[:, 0:1])
        P.ts(ss[:, 1:2], ss[:, 0:1], 1.0 / D, ALU.mult, 1e-6, ALU.add, r=[t_s], w=[t_s])
        P.act(ss[:, 2:3], ss[:, 1:2], AF.Sqrt, r=[t_s], w=[t_s])
        P.op("dve", lambda e: e.reciprocal(out=ss[:, 3:4], in_=ss[:, 2:3]), r=[t_s], w=[t_s])
        P.stt(out_t, xt, ss[:, 3:4], gt, ALU.mult, ALU.mult, r=[t_x, t_g, t_s], w=[t_out])

    def chunks_of(tok0, n):
        return list(range(tok0 // 128, (tok0 + n - 1) // 128 + 1))

    hT_d = nc.dram_tensor("hT_scratch", [128, 8, T], BF16, kind="Internal").ap()
    t_hTd = trks(2)
    with contextlib.ExitStack() as PA:
        OAT = sb(PA, "OAT", [128, 4, NQ], BF16)
        t_OAT = trks(NQB)
        NPT = 4
        PT = [sb(PA, "PT%d" % i, [128, 512], BF16) for i in range(NPT)]
        t_PT = trks(NPT)
        ptrr = [0]
        NTMP = 3
        TMP = [sb(PA, "TMP%d" % i, [128, 512], F32) for i in range(NTMP)]
        t_TMP = trks(NTMP)
        tmprr = [0]

        def next_pt():
            k = ptrr[0] % NPT
            ptrr[0] += 1
            return k

        def next_tmp():
            k = tmprr[0] % NTMP
            tmprr[0] += 1
            return k

        GT = [sb(PA, "GT%d" % i, [128, 512], F32) for i in range(2)]
        t_GT = trks(2)

        def gelu_tanh(out, t_out, xin, t_x, n):
            a, b2 = GT[0][:, 0:n], GT[1][:, 0:n]
            P.tt(a, xin, xin, ALU.mult, r=[t_x], w=[t_GT[0]])
            P.ts(a, a, 0.044715, ALU.mult, 1.0, ALU.add, r=[t_GT[0]], w=[t_GT[0]])
            P.tt(a, a, xin, ALU.mult, r=[t_GT[0], t_x], w=[t_GT[0]])
            P.act(b2, a, AF.Sigmoid, r=[t_GT[0]], w=[t_GT[1]], scale=1.5957691216057308)
            P.tt(out, b2, xin, ALU.mult, r=[t_GT[1], t_x], w=t_out)

        def load_wt(dst, t_dst, src, col_specs):
            o = 0
            pieces = []
            for (c0, n) in col_specs:
                pieces.append((dst[:, :, o:o + n], src[:, c0:c0 + n].rearrange("(c p) n -> p c n", p=128)))
                o += n
            P.dma_multi("pool", pieces, w=[t_dst])

        def phase1(ph, hdst, t_hdst, tiles):
            xt2 = [sb(ph, "xt%d" % i, [128, D], F32) for i in range(2)]
            t_xt = trks(2)
            xn2 = [sb(ph, "xn%d" % i, [128, D], BF16) for i in range(2)]
            t_xn = trks(2)
            gmr = sb(ph, "gmr", [128, D], F32)
            t_gm = Trk()
            junk = sb(ph, "junk", [128, D], F32)
            ss2 = [sb(ph, "ss%d" % i, [128, 4], F32) for i in range(2)]
            t_ss = trks(2)
            P.dma("sp", gmr[:], g_mix[0:1, :].partition_broadcast(128), w=[t_gm])

            def run(tiles_):
                for kk, t in enumerate(tiles_):
                    k = kk % 2
                    P.dma("sp", xt2[k][:], x[t * 128:(t + 1) * 128, :], w=[t_xt[k]])
                    rmsnorm_tile(junk, ss2[k], t_ss[k], xt2[k][:], t_xt[k], gmr[:], t_gm, xn2[k][:], t_xn[k])
                    for c in range(8):
                        P.tr(PST[:, c * 128:(c + 1) * 128], xn2[k][:, c * 128:(c + 1) * 128], identb[:],
                             r=[t_xn[k], t_const], w=[tPST])
                    P.copy(evac_eng(), hdst[:, :, kk * 128:(kk + 1) * 128],
                           PST[:, :].rearrange("p (c n) -> p c n", c=8), r=[tPST], w=[t_hdst[kk]])
            return run

        with contextlib.ExitStack() as nsa:
            QN = [sb(nsa, "QN%d" % g, [128, 4, NQ], BF16) for g in range(2)]
            t_QN = trks(NQB)
            KS = sb(nsa, "KS", [128, T], BF16)
            KW = sb(nsa, "KW", [128, T], BF16)
            t_KS = trks(NCH)
            t_KW = trks(NCH)
            VS = sb(nsa, "VS", [128, NCH, 2, 66], BF16)
            VW = sb(nsa, "VW", [128, NCH, 2, 66], BF16)
            t_VS = trks(NCH)
            t_VW = trks(NCH)
            GA = sb(nsa, "GA", [128, NQB, 24], F32)
            t_GA = trks(NQB)
            KC = sb(nsa, "KC", [128, 256], BF16)
            t_KC = Trk()
            VCX = sb(nsa, "VCX", [128, 2, 2, 130], BF16)
            t_VCX = Trk()
            vtm = sb(nsa, "vtm", [128, 32], BF16)
            t_vtm = Trk()
            P.dma("sp", vtm[:], c_validtm[:, :], w=[t_vtm])
            P.copy("dve", VS[:, :, :, 64], vtm[:].unsqueeze(2).broadcast_to([128, NCH, 2]), r=[t_vtm], w=t_VS)
            P.copy("dve", VW[:, :, :, 64], vtm[:].unsqueeze(2).broadcast_to([128, NCH, 2]), r=[t_vtm], w=t_VW)
            P.op("dve", lambda e: e.memset(QN[0][64:128, :, :], 0.0), w=t_QN)
            P.op("dve", lambda e: e.memset(QN[1][0:64, :, :], 0.0), w=t_QN)

            with contextlib.ExitStack() as ph:
                hTh = sb(ph, "hTh", [128, 8, 2048], BF16)
                t_hTh = trks(16)
                wq = sb(ph, "wq", [128, 8, 512], BF16)
                wk = sb(ph, "wk", [128, 8, 512], BF16)
                wv = sb(ph, "wv", [128, 8, 280], BF16)
                t_wq, t_wk, t_wv = Trk(), Trk(), Trk()
                load_wt(wq, t_wq, w_in, [(0, 64), (256, 64), (64, 64), (320, 64), (128, 64), (384, 64), (192, 64),
                                         (448, 64)])
                load_wt(wk, t_wk, w_in, [(768, 128), (1024, 128), (512, 128), (640, 128)])
                load_wt(wv, t_wv, w_in, [(896, 128), (1152, 128), (1280, 24)])
                SRD = [sb(ph, "SRD%d" % kv, [128, 2, 16, 256], BF16) for kv in range(2)]
                t_SRD = trks(2)
                for kv in range(2):
                    P.op("dve", lambda e, kv=kv: e.memset(SRD[kv][:, 1, :, 255:256], 0.0), w=[t_SRD[kv]])
                with contextlib.ExitStack() as p1s:
                    run_p1 = phase1(p1s, hTh, t_hTh, None)
                    for hh_ in range(2):
                        run_p1(list(range(16 * hh_, 16 * hh_ + 16)))
                        if stage != 23.1:
                            P.dma("sp", hT_d[:, :, hh_ * 2048:(hh_ + 1) * 2048], hTh[:, :, :], r=t_hTh,
                                  w=[t_hTd[hh_]])
                        if stage == 1 and hh_ == 0:
                            dump("hTh", hTh[:, :, Q0:2048], t_hTh)
                            return done()

                        def fm(wt, t_w, wc0, lt0, n, dst, t_dst, scale=None, eng=None, dst2=None):
                            b = nextps(0, 4)
                            for dm in range(8):
                                P.mm(PS[b][:, 0:n], wt[:, dm, wc0:wc0 + 128], hTh[:, dm, lt0:lt0 + n],
                                     start=(dm == 0), stop=(dm == 7),
                                     r=[t_w] + [t_hTh[c] for c in chunks_of(lt0, n)], w=[tPS[b]])
                            return b

                        if stage == 23.2:
                            continue
                        qtiles = [(Q0, 128, 0)] if hh_ == 0 else [(n0, 512, 128 + n0) for n0 in range(0, 2048, 512)]
                        for r_ in range(4):
                            for (lt0, n, qoff) in qtiles:
                                b = fm(wq, t_wq, 128 * r_, lt0, n, None, None)
                                tq = [t_QN[c] for c in chunks_of(qoff, n)]
                                P.copy("act", QN[0][0:64, r_, qoff:qoff + n], PS[b][0:64, 0:n], r=[tPS[b]], w=tq,
                                       scale=0.125)
                                P.copy("dve", QN[1][64:128, r_, qoff:qoff + n], PS[b][64:128, 0:n], r=[tPS[b]], w=tq,
                                       scale=0.125)
                        for n0 in range(0, 2048, 512):
                            g0 = hh_ * 2048 + n0
                            b = fm(wk, t_wk, 0, n0, 512, None, None)
                            P.copy(evac_eng(), KS[:, g0:g0 + 512], PS[b][:, 0:512], r=[tPS[b]],
                                   w=[t_KS[c] for c in chunks_of(g0, 512)])
                            b = fm(wk, t_wk, 128, n0, 512, None, None)
                            P.copy(evac_eng(), KW[:, g0:g0 + 512], PS[b][:, 0:512], r=[tPS[b]],
                                   w=[t_KW[c] for c in chunks_of(g0, 512)])
                            for kv in range(2):
                                b = fm(wk, t_wk, 256 + 128 * kv, n0, 512, None, None)
                                c0 = g0 // 16
                                pv_ = PS[b][:, 0:512].rearrange("d (c p) -> d p c", p=16)
                                P.copy("dve", SRD[kv][:, 0, :, c0:c0 + 32], pv_, r=[tPS[b]], w=[t_SRD[kv]])
                                if g0 == 0:
                                    P.copy("dve", SRD[kv][:, 1, :, 0:31],
                                           PS[b][:, 16:512].rearrange("d (c p) -> d p c", p=16),
                                           r=[tPS[b]], w=[t_SRD[kv]])
                                else:
                                    P.copy("dve", SRD[kv][:, 1, :, c0 - 1:c0 + 31], pv_, r=[tPS[b]], w=[t_SRD[kv]])
                        for tl in range(16):
                            t = hh_ * 16 + tl
                            b = nextps(0, 4)
                            for dm in range(8):
                                P.mm(PS[b][:, 0:280], hTh[:, dm, tl * 128:(tl + 1) * 128], wv[:, dm, 0:280],
                                     start=(dm == 0), stop=(dm == 7), r=[t_wv, t_hTh[tl]], w=[tPS[b]])
                            P.copy("dve", VS[:, t, :, 0:64], PS[b][:, 0:128].rearrange("p (g d) -> p g d", g=2),
                                   r=[tPS[b]], w=[t_VS[t]])
                            P.copy("dve", VW[:, t, :, 0:64], PS[b][:, 128:256].rearrange("p (g d) -> p g d", g=2),
                                   r=[tPS[b]], w=[t_VW[t]])
                            if t >= QB0:
                                P.act(GA[:, t - QB0, :], PS[b][:, 256:280], AF.Sigmoid, r=[tPS[b]],
                                      w=[t_GA[t - QB0]])
                P.barrier()
                if stage == 23.2:
                    dump("hTh", hTh[:, :, 0:128], t_hTh)
                    return done()
                if stage in (23, 23.1):
                    dump("QN0", QN[0][:, :, 0:256], t_QN)
                    dump("QN1", QN[1][:, :, 0:256], t_QN)
                    dump("KS", KS[:, Q0:Q0 + 256], t_KS)
                    dump("KW", KW[:, Q0:Q0 + 256], t_KW)
                    dump("VS", VS[:, 16:18, :, 0:65], t_VS)
                    dump("GA", GA[:, 0:2, :], t_GA)
                    return done()

                W1 = sb(ph, "W1", [128, 32, 128], BF16)
                W2 = sb(ph, "W2", [128, 64], BF16)
                peT = sb(ph, "peT", [128, 64], F32)
                peTb = sb(ph, "peTb", [128, 32, 2], BF16)
                hb = sb(ph, "hb", [128, 1], F32)
                HT = sb(ph, "HT", [128, 2, 256], BF16)
                t_W1 = Trk()
                t_W2 = Trk()
                t_pe = Trk()
                t_hb = Trk()
                t_HT = Trk()
                ovl = sb(ph, "ovl", [128, 2, 64], BF16)
                t_ovl = Trk()
                P.dma("sp", ovl[:], c_ovl[:, :, :], w=[t_ovl])
                for kv in range(2):
                    w1d, w2d, ped = ((w_k1, w_k2, pe_k), (w_v1, w_v2, pe_v))[kv]
                    w1v = w1d.rearrange("(p d) h -> d p h", d=64)
                    P.dma_multi("pool", [(W1[0:64, :, :], w1v), (W1[64:128, :, :], w1v)], w=[t_W1])
                    P.dma("pool", W2[:, :], w2d[:, :], w=[t_W2])
                    P.dma("sp", peT[0:32, 0:64], ped[:, :], w=[t_pe])
                    b = nextps(0, 4)
                    P.tr(PS[b][0:64, 0:32], peT[0:32, 0:64], ident[0:32, 0:32], r=[t_pe, t_const], w=[tPS[b]])
                    P.copy("dve", peTb[0:64, :, :], PS[b][0:64, 0:32].unsqueeze(2).broadcast_to([64, 32, 2]),
                           r=[tPS[b]], w=[t_pe])
                    b = nextps(0, 4)
                    for p_ in range(32):
                        P.mm(PS[b][:, 0:2], W1[0:64, p_, :], peTb[0:64, p_, :], start=(p_ == 0),
                             stop=(p_ == 31), r=[t_W1, t_pe], w=[tPS[b]])
                    P.copy("dve", hb[:, 0:1], PS[b][:, 0:1], r=[tPS[b]], w=[t_hb])
                    m = next_tmp()
                    for g in range(2):
                        b = 2 * g + (kv % 2)
                        for p_ in range(32):
                            P.mm(PS[b][:, 0:256], W1[64 * g:64 * g + 64, p_, :],
                                 SRD[kv][64 * g:64 * g + 64, p_ // 16, p_ % 16, :],
                                 start=(p_ == 0), stop=(p_ == 31), r=[t_W1, t_SRD[kv]], w=[tPS[b]])
                        P.ts(TMP[m][:, g * 256:(g + 1) * 256], PS[b][:, 0:256], hb[:, 0:1], ALU.add,
                             r=[tPS[b], t_hb], w=[t_TMP[m]])
                    gelu_tanh(HT[:, :, :].rearrange("p g c -> p (g c)"), [t_HT], TMP[m][:, :], t_TMP[m], 512)
                    if kv == 0:
                        for g in range(2):
                            b = nextps(4, 6)
                            P.mm(PS[b][0:64, 0:256], W2[:, :], HT[:, g, :], start=True, stop=True,
                                 r=[t_W2, t_HT], w=[tPS[b]])
                            P.copy("dve", KC[64 * g:64 * g + 64, :], PS[b][0:64, 0:256], r=[tPS[b]], w=[t_KC])
                    else:
                        for kc in range(2):
                            for g in range(2):
                                b = nextps(4, 6)
                                P.mm(PS[b][:, 0:64], HT[:, g, kc * 128:(kc + 1) * 128], W2[:, :], start=True,
                                     stop=True, r=[t_W2, t_HT], w=[tPS[b]])
                                P.copy("dve", VCX[:, kc, g, 0:64], PS[b][:, 0:64], r=[tPS[b]], w=[t_VCX])
                                P.copy("dve", VCX[:, kc, g, 66:130], ovl[:, kc, :], r=[t_ovl], w=[t_VCX])
                        P.op("dve", lambda e: e.memset(VCX[:, :, :, 64:65], 1.0), w=[t_VCX])
                        P.op("dve", lambda e: e.memset(VCX[:, :, :, 65:66], 0.0), w=[t_VCX])
                P.barrier()

            if stage == 2:
                dump("KC", KC[:, :], [t_KC])
                dump("VCX", VCX[:, :, :, :], [t_VCX])
                return done()

            ea = sb(nsa, "ea", [128, T], BF16)
            caus = sb(nsa, "caus", [128, 4, 128], F32)
            acaus = sb(nsa, "acaus", [128, 4, 128], F32)
            ma = sb(nsa, "ma", [128, NQB, 64], F32)
            mb = sb(nsa, "mb", [128, NQB, 64], F32)
            t_tab = Trk()
            P.dma("sp", ea[:], c_ea[:, :], w=[t_tab])
            P.dma("sp", caus[:], c_caus[:, :, :], w=[t_tab])
            P.dma("sp", acaus[:], c_acaus[:, :, :], w=[t_tab])
            P.dma("sp", ma[:], c_ma[:, :, :], w=[t_tab])
            P.dma("sp", mb[:], c_mb[:, :, :], w=[t_tab])
            RB = [[sb(nsa, "RB%d%d" % (g, k), [128, 4, 128], BF16) for k in range(2)] for g in range(2)]
            RW = [[sb(nsa, "RW%d%d" % (g, k), [128, 4, 128], BF16) for k in range(2)] for g in range(2)]
            t_RB = [[Trk() for k in range(2)] for g in range(2)]
            t_RW = [[Trk() for k in range(2)] for g in range(2)]
            for g in range(2):
                for k in range(2):
                    P.op("dve", lambda e, g=g, k=k: e.memset(RB[g][k][:, :, :], 0.0), w=[t_RB[g][k]])
                    P.op("dve", lambda e, g=g, k=k: e.memset(RW[g][k][:, :, :], 0.0), w=[t_RW[g][k]])
            Bc = [sb(nsa, "Bc%d" % k, [128, 2, 2, 4, 128], F32) for k in range(2)]
            t_Bc = trks(2)
            ONSA = [sb(nsa, "ONSA%d" % k, [128, 512], F32) for k in range(2)]
            t_ONSA = trks(2)
            sm = [sb(nsa, "sm%d" % k, [128, 32], F32) for k in range(2)]
            t_sm = trks(2)
            impn = [sb(nsa, "impn%d" % k, [128, 64], F32) for k in range(2)]
            impw = [sb(nsa, "impw%d" % k, [128, 64], F32) for k in range(2)]
            t_imp = trks(2)
            srr = [0]

            def s_tile(src_fn, kind, g, i, c):
                qs = slice(i * 128, (i + 1) * 128)
                rb = RB[g][i % 2]
                t_rb = t_RB[g][i % 2]
                rw = RW[g][i % 2]
                t_rw = t_RW[g][i % 2]
                b = nextps(0, 2)
                pv = v4(PS[b][:, 0:512])
                if kind == "cmp":
                    P.mm(pv, KC[:, c * 128:(c + 1) * 128], QN[g][:, :, qs], True, True,
                         r=[t_KC, t_QN[i]], w=[tPS[b]])
                    addt = Bc[i % 2][:, c, g, :, :]
                    t_add = t_Bc[i % 2]
                else:
                    KX, t_KX = (KS, t_KS) if kind == "sel" else (KW, t_KW)
                    P.mm(pv, KX[:, c * 128:(c + 1) * 128], QN[g][:, :, qs], True, False,
                         r=[t_KX[c], t_QN[i]], w=[tPS[b]])
                    if kind == "sel":
                        P.mm(pv, ea[:, c * 128:(c + 1) * 128], rb[:, :, :], False, True,
                             r=[t_tab, t_rb], w=[tPS[b]])
                    else:
                        P.mm(pv, ea[:, c * 128:(c + 1) * 128], rw[:, :, :], False, True,
                             r=[t_tab, t_rw], w=[tPS[b]])
                    qb = QB0 + i
                    addt, t_add = None, t_tab
                    if c == qb:
                        addt = caus[:, :, :]
                    elif kind == "win" and c == qb - 4:
                        addt = acaus[:, :, :]
                k = next_pt()
                if addt is not None:
                    m = next_tmp()
                    P.tt(v4(TMP[m][:, :]), pv, addt, ALU.add, r=[tPS[b], t_add], w=[t_TMP[m]])
                    P.act(PT[k][:, :], TMP[m][:, :], AF.Exp, r=[t_TMP[m]], w=[t_PT[k]])
                else:
                    P.act(PT[k][:, :], PS[b][:, 0:512], AF.Exp, r=[tPS[b]], w=[t_PT[k]])
                return k

            def combine(i, g, bi, o_fn, d_fn, t_acc, first):
                k = srr[0] % 2
                srr[0] += 1
                s_ = sm[k]
                t_s = t_sm[k]
                for r_ in range(4):
                    P.ts(s_[:, r_:r_ + 1], d_fn(r_), 1e-30, ALU.max, r=t_acc, w=[t_s])
                P.op("dve", lambda e: e.reciprocal(out=s_[:, 4:8], in_=s_[:, 0:4]), r=[t_s], w=[t_s])
                P.tt(s_[:, 8:12], s_[:, 4:8], GA[:, i, g * 12 + bi:g * 12 + 12:3], ALU.mult,
                     r=[t_s, t_GA[i]], w=[t_s])
                on = ONSA[i % 2]
                for r_ in range(4):
                    h0 = (4 * g + r_) * 64
                    if first:
                        P.ts(on[:, h0:h0 + 64], o_fn(r_), s_[:, 8 + r_:9 + r_], ALU.mult,
                             r=t_acc + [t_s], w=[t_ONSA[i % 2]])
                    else:
                        P.stt(on[:, h0:h0 + 64], o_fn(r_), s_[:, 8 + r_:9 + r_], on[:, h0:h0 + 64], ALU.mult, ALU.add,
                              r=t_acc + [t_s], w=[t_ONSA[i % 2]])
                return s_, t_s

            for i in range(NQB):
                qb = QB0 + i
                P.dma("sp", Bc[i % 2][:], c_bcmp[i], w=[t_Bc[i % 2]])
                for g in range(2):
                    rb = RB[g][i % 2]
                    t_rb = t_RB[g][i % 2]
                    P.dma("sp", rb[64:68, :, :], c_rbc[:, i, g, :, :], w=[t_rb])
                    P.dma("sp", RW[g][i % 2][64:68, :, :], c_rbc[:, i, g, :, :], w=[t_RW[g][i % 2]])
                    for kc in range(2):
                        k = s_tile(None, "cmp", g, i, kc)
                        for r_ in range(4):
                            bb = 4 + r_ // 2
                            P.mm(PS[bb][:, (r_ % 2) * 130:(r_ % 2) * 130 + 130], PT[k][:, r_ * 128:(r_ + 1) * 128],
                                 VCX[:, kc, g, :], start=(kc == 0 and r_ % 2 == 0), stop=(kc == 1),
                                 r=[t_PT[k], t_VCX], w=[tPS[bb]], skip=True)
                    t_acc = [tPS[4], tPS[5]]
                    s_, t_s = combine(i, g, 0, lambda r_: PS[4 + r_ // 2][:, (r_ % 2) * 130:(r_ % 2) * 130 + 64],
                                      lambda r_: PS[4 + r_ // 2][:, (r_ % 2) * 130 + 64:(r_ % 2) * 130 + 65],
                                      t_acc, True)
                    ik = (2 * i + g) % 2
                    im = impn[ik]
                    iw = impw[ik]
                    t_im = t_imp[ik]
                    for r_ in range(4):
                        src = PS[4 + r_ // 2][:, (r_ % 2) * 130 + 66:(r_ % 2) * 130 + 130]
                        if r_ == 0:
                            P.ts(im[:, :], src, s_[:, 4:5], ALU.mult, r=t_acc + [t_s], w=[t_im])
                        else:
                            P.stt(im[:, :], src, s_[:, 4 + r_:5 + r_], im[:, :], ALU.mult, ALU.add,
                                  r=t_acc + [t_s], w=[t_im])
                    P.tt(im[:, :], im[:, :], ma[:, i, :], ALU.mult, r=[t_im, t_tab], w=[t_im])
                    P.tt(im[:, :], im[:, :], mb[:, i, :], ALU.add, r=[t_im, t_tab], w=[t_im])
                    P.op("dve", lambda e: e.max(out=s_[:, 16:24], in_=im[:, :]), r=[t_im], w=[t_s])
                    P.op("dve", lambda e: e.match_replace(out=iw[:, :], in_to_replace=s_[:, 16:24], in_values=im[:, :],
                                                          imm_value=-1.0e9), r=[t_im, t_s], w=[t_im])
                    P.op("dve", lambda e: e.max(out=s_[:, 24:32], in_=iw[:, :]), r=[t_im], w=[t_s])
                    P.ts(s_[:, 12:13], s_[:, 31:32], 0.0, ALU.max, r=[t_s], w=[t_s])
                    P.ts(iw[:, :], im[:, :], s_[:, 12:13], ALU.is_ge, r=[t_im, t_s], w=[t_im])
                    P.ts(iw[:, :], iw[:, :], -1.0, ALU.add, -NEG, ALU.mult, r=[t_im], w=[t_im])
                    P.tr(PS[6][0:64, 0:128], iw[:, 0:64], ident[:, :], r=[t_im, t_const], w=[tPS[6]])
                    P.copy("act", rb[0:64, :, :], PS[6][0:64, 0:128].unsqueeze(1).broadcast_to([64, 4, 128]),
                           r=[tPS[6]], w=[t_rb])
                    for c in range(qb + 1):
                        k = s_tile(None, "sel", g, i, c)
                        for r_ in range(4):
                            P.mm(PS[2][:, r_ * 65:(r_ + 1) * 65], PT[k][:, r_ * 128:(r_ + 1) * 128], VS[:, c, g, 0:65],
                                 start=(c == 0 and r_ == 0), stop=(c == qb), r=[t_PT[k], t_VS[c]], w=[tPS[2]], skip=True)
                    combine(i, g, 1, lambda r_: PS[2][:, r_ * 65:r_ * 65 + 64],
                            lambda r_: PS[2][:, r_ * 65 + 64:r_ * 65 + 65], [tPS[2]], False)
                    for dl in range(4, -1, -1):
                        c = qb - dl
                        k = s_tile(None, "win", g, i, c)
                        for r_ in range(4):
                            P.mm(PS[3][:, r_ * 65:(r_ + 1) * 65], PT[k][:, r_ * 128:(r_ + 1) * 128], VW[:, c, g, 0:65],
                                 start=(dl == 4 and r_ == 0), stop=(dl == 0), r=[t_PT[k], t_VW[c]], w=[tPS[3]], skip=True)
                    combine(i, g, 2, lambda r_: PS[3][:, r_ * 65:r_ * 65 + 64],
                            lambda r_: PS[3][:, r_ * 65 + 64:r_ * 65 + 65], [tPS[3]], False)
                on = ONSA[i % 2]
                for fc in range(4):
                    P.tr(PS[6][:, fc * 128:(fc + 1) * 128], on[:, fc * 128:(fc + 1) * 128], ident[:, :],
                         r=[t_ONSA[i % 2], t_const], w=[tPS[6]])
                P.copy("act", OAT[:, :, i * 128:(i + 1) * 128], v4(PS[6][:, 0:512]), r=[tPS[6]], w=[t_OAT[i]])
            P.barrier()
        if stage == 3:
            dump("OAT", OAT[:, :, :], t_OAT)
            return done()

        OBT = sb(PA, "OBT", [128, 4, NQ], BF16)
        t_OBT = trks(NQB)
        TQ0 = 1536
        with contextlib.ExitStack() as dl_:
            accD = sb(dl_, "accD", [128, 4, NQ], F32)
            t_acc = trks(NQB)
            hq = [sb(dl_, "hq%d" % k, [128, 8, 1024], BF16) for k in range(2)]
            t_hq = trks(2)
            wd = sb(dl_, "wd", [128, 8, 768], BF16)
            t_wd = Trk()
            QD = sb(dl_, "QD", [128, 2, T - TQ0], BF16)
            KD = sb(dl_, "KD", [128, 2, T], BF16)
            VT = sb(dl_, "VT", [128, 2, T], BF16)
            VD = sb(dl_, "VD", [128, NCH, 4, 66], BF16)
            BD = sb(dl_, "BD", [128, 2, 4, 128], F32)
            vdd = sb(dl_, "vdd", [128, 3, 32], BF16)
            t_QD, t_KD, t_VT, t_VD, t_BD, t_vdd = Trk(), Trk(), Trk(), Trk(), Trk(), Trk()
            P.dma("sp", vdd[:], c_validd[:, :, :], w=[t_vdd])
            hqrr = [0]
            for gd, (w_, r_) in enumerate(DIL):
                J = T // r_
                JQ = (T - TQ0) // r_
                jq0 = TQ0 // r_
                npc = J // 128
                c0w = 1304 + gd * 256
                load_wt(wd, t_wd, w_in, [(c0w, 256), (c0w + 768, 256), (c0w + 1536, 256)])
                P.dma("sp", BD[:], c_bdil[:, gd, :, :, :], w=[t_BD])
                P.copy("dve", VD[:, :, :, 64], vdd[:, gd, :].unsqueeze(2).broadcast_to([128, NCH, 4]),
                       r=[t_vdd], w=[t_VD])
                KDv = KD[:, :, :].rearrange("p a (rho j) -> p a rho j", rho=r_)
                VTv = VT[:, :, :].rearrange("p a (rho j) -> p a rho j", rho=r_)
                QDv = QD[:, :, :].rearrange("p a (rho j) -> p a rho j", rho=r_)
                for qt in range(4):
                    k = hqrr[0] % 2
                    hqrr[0] += 1
                    P.dma("sp", hq[k][:, :, :], hT_d[:, :, qt * 1024:(qt + 1) * 1024], r=t_hTd, w=[t_hq[k]])
                    for n0 in range(0, 1024, 512):
                        g0 = qt * 1024 + n0
                        for which in range(3):
                            if which == 0 and g0 < TQ0:
                                continue
                            for pr in range(2):
                                b = nextps(0, 4)
                                for dm in range(8):
                                    P.mm(PS[b][:, 0:512], wd[:, dm, which * 256 + pr * 128:which * 256 + pr * 128 + 128],
                                         hq[k][:, dm, n0:n0 + 512], start=(dm == 0), stop=(dm == 7),
                                         r=[t_wd, t_hq[k]], w=[tPS[b]])
                                src = PS[b][:, 0:512].rearrange("p (j rho) -> p rho j", rho=r_)
                                nj = 512 // r_
                                if which == 0:
                                    j0 = (g0 - TQ0) // r_
                                    P.copy(evac_eng(), QDv[:, pr, :, j0:j0 + nj], src, r=[tPS[b]], w=[t_QD],
                                           scale=0.125)
                                elif which == 1:
                                    j0 = g0 // r_
                                    P.copy(evac_eng(), KDv[:, pr, :, j0:j0 + nj], src, r=[tPS[b]], w=[t_KD])
                                else:
                                    j0 = g0 // r_
                                    P.copy(evac_eng(), VTv[:, pr, :, j0:j0 + nj], src, r=[tPS[b]], w=[t_VT])
                for ci in range(NCH):
                    for pr in range(2):
                        P.tr(PST[:, pr * 128:(pr + 1) * 128], VT[:, pr, ci * 128:(ci + 1) * 128], identb[:],
                             r=[t_VT, t_const], w=[tPST])
                    P.copy(evac_eng(), VD[:, ci, :, 0:64], PST[:, 0:256].rearrange("p (h d) -> p h d", h=4),
                           r=[tPST], w=[t_VD])
                for rho in range(r_):
                    for jb in range(npc):
                        jmin = -(-(Q0 - rho) // r_)
                        q_lo = max(0, jmin - 128 * jb)
                        if q_lo >= 128:
                            continue
                        nq = 128 - q_lo
                        dls = [dl for dl in (0, 1) if jb - dl >= 0]
                        pts = {}
                        for dl in dls:
                            jc = jb - dl
                            kk = next_pt()
                            pts[dl] = kk
                            for hh in range(4):
                                par, pr = hh % 2, hh // 2
                                base = 64 * par
                                kc0 = rho * J + jc * 128
                                qc0 = rho * JQ + (jb * 128 + q_lo - jq0)
                                P.mm(PS[par][:, pr * 128 + q_lo:pr * 128 + 128],
                                     KD[base:base + 64, pr, kc0:kc0 + 128], QD[base:base + 64, pr, qc0:qc0 + nq],
                                     True, True, r=[t_KD, t_QD], w=[tPS[par]])
                            for par in range(2):
                                m = next_tmp()
                                tv = TMP[m][:, 0:256].rearrange("p (a b) -> p a b", a=2)[:, :, q_lo:128]
                                pv_ = PS[par][:, 0:256].rearrange("p (a b) -> p a b", a=2)[:, :, q_lo:128]
                                P.tt(tv, pv_, BD[:, dl, par::2, q_lo:128], ALU.add, r=[tPS[par], t_BD], w=[t_TMP[m]])
                                P.act(v4(PT[kk][:, :])[:, par::2, q_lo:128], tv, AF.Exp, r=[t_TMP[m]], w=[t_PT[kk]])
                        ab = 2 + ((rho * npc + jb) % 2)
                        for hh in range(4):
                            for n_, dl in enumerate(dls):
                                ci = rho * npc + (jb - dl)
                                P.mm(PS[ab][0:65, hh * 128 + q_lo:hh * 128 + 128], VD[:, ci, hh, 0:65],
                                     PT[pts[dl]][:, hh * 128 + q_lo:hh * 128 + 128],
                                     start=(n_ == 0), stop=(n_ == len(dls) - 1),
                                     r=[t_PT[pts[dl]], t_VD], w=[tPS[ab]])
                        tok0 = rho + r_ * (128 * jb + q_lo) - Q0
                        tok1 = rho + r_ * (128 * jb + 127) - Q0
                        blks = [t_acc[c] for c in range(tok0 // 128, tok1 // 128 + 1)]
                        dst = accD[0:65, :, tok0:tok1 + 1:r_]
                        srcp = v4(PS[ab][0:65, 0:512])[:, :, q_lo:128]
                        if gd == 0:
                            P.copy("dve", dst, srcp, r=[tPS[ab]], w=blks)
                        else:
                            P.tt(dst, srcp, dst, ALU.add, r=[tPS[ab]] + blks, w=blks)
            s65 = sb(dl_, "s65", [128, 64], F32)
            t_s65 = Trk()
            P.dma("sp", s65[0:65, :], c_s65[:, :], w=[t_s65])
            for hh in range(4):
                n0 = 0
                while n0 < NQ:
                    n = min(512, NQ - n0)
                    b = nextps(4, 6)
                    tb = [t_acc[c] for c in chunks_of(n0, n)]
                    P.mm(PS[b][0:64, 0:n], s65[0:65, 0:64], accD[0:65, hh, n0:n0 + n], True, True,
                         r=[t_s65] + tb, w=[tPS[b]])
                    m = next_tmp()
                    P.ts(TMP[m][0:64, 0:n], PS[b][0:64, 0:n], 1e-30, ALU.max, r=[tPS[b]], w=[t_TMP[m]])
                    P.op("dve", lambda e, m=m, n=n: e.reciprocal(out=TMP[m][0:64, 0:n], in_=TMP[m][0:64, 0:n]),
                         r=[t_TMP[m]], w=[t_TMP[m]])
                    P.tt(OBT[0:64, hh, n0:n0 + n], accD[0:64, hh, n0:n0 + n], TMP[m][0:64, 0:n], ALU.mult,
                         r=tb + [t_TMP[m]], w=[t_OBT[c] for c in chunks_of(n0, n)])
                    n0 += n
            P.barrier()
        if stage == 4:
            dump("OBT", OBT[0:64, :, :], t_OBT)
            return done()

        with contextlib.ExitStack() as mg:
            WPA = sb(mg, "WPA", [128, 4, D], BF16)
            WPB = sb(mg, "WPB", [128, 4, D], BF16)
            WO = sb(mg, "WO", [128, 8, D], BF16)
            MX = sb(mg, "MX", [128, 8, NQ], BF16)
            hTq = sb(mg, "hTq", [128, 8, NQ], BF16)
            wm = [sb(mg, "wm%d" % k, [128, 8, 256], BF16) for k in range(2)]
            xr = [sb(mg, "xr%d" % k, [128, D], F32) for k in range(2)]
            t_WPA, t_WPB, t_WO, t_hTq = Trk(), Trk(), Trk(), Trk()
            t_MX = trks(NQB)
            t_wm = trks(2)
            t_xr = trks(2)
            P.dma("pool", WPA[:, :, :], w_pa.rearrange("(c p) n -> p c n", p=128), w=[t_WPA])
            P.dma("pool", WPB[0:64, :, :], w_pb.rearrange("(h d) n -> d h n", d=64), w=[t_WPB])
            P.dma_multi("pool", [(WO[:, 0:4, :], w_out[0:512, :].rearrange("(c p) n -> p c n", p=128)),
                                 (WO[:, 4:8, :], w_out[512:1024, :].rearrange("(c p) n -> p c n", p=128))],
                        w=[t_WO])
            P.dma("sp", hTq[:, :, :], hT_d[:, :, Q0:T], r=t_hTd, w=[t_hTq])
            ntiles = []
            n0 = 0
            while n0 < NQ:
                n = min(512, NQ - n0)
                ntiles.append((n0, n))
                n0 += n
            for mc in range(8):
                k = mc % 2
                load_wt(wm[k], t_wm[k], w_in, [(3608 + mc * 128, 128), (4632 + mc * 128, 128)])
                for (n0, n) in ntiles:
                    tb = chunks_of(n0, n)
                    b1, b2_, b3, b4 = [nextps(0, 7) for _ in range(4)]
                    for fc in range(4):
                        P.mm(PS[b1][:, 0:n], WPA[:, fc, mc * 128:(mc + 1) * 128], OAT[:, fc, n0:n0 + n],
                             fc == 0, fc == 3, r=[t_WPA] + [t_OAT[c] for c in tb], w=[tPS[b1]])
                    for hh in range(4):
                        P.mm(PS[b2_][:, 0:n], WPB[0:64, hh, mc * 128:(mc + 1) * 128], OBT[0:64, hh, n0:n0 + n],
                             hh == 0, hh == 3, r=[t_WPB] + [t_OBT[c] for c in tb], w=[tPS[b2_]])
                    for dm in range(8):
                        P.mm(PS[b3][:, 0:n], wm[k][:, dm, 0:128], hTq[:, dm, n0:n0 + n], dm == 0, dm == 7,
                             r=[t_wm[k], t_hTq], w=[tPS[b3]])
                    for dm in range(8):
                        P.mm(PS[b4][:, 0:n], wm[k][:, dm, 128:256], hTq[:, dm, n0:n0 + n], dm == 0, dm == 7,
                             r=[t_wm[k], t_hTq], w=[tPS[b4]])
                    ma_, mb_ = next_tmp(), next_tmp()
                    P.act(TMP[ma_][:, 0:n], PS[b3][:, 0:n], AF.Sigmoid, r=[tPS[b3]], w=[t_TMP[ma_]])
                    P.act(TMP[mb_][:, 0:n], PS[b4][:, 0:n], AF.Sigmoid, r=[tPS[b4]], w=[t_TMP[mb_]])
                    P.tt(TMP[ma_][:, 0:n], TMP[ma_][:, 0:n], PS[b1][:, 0:n], ALU.mult, r=[t_TMP[ma_], tPS[b1]],
                         w=[t_TMP[ma_]])
                    P.tt(TMP[mb_][:, 0:n], TMP[mb_][:, 0:n], PS[b2_][:, 0:n], ALU.mult, r=[t_TMP[mb_], tPS[b2_]],
                         w=[t_TMP[mb_]])
                    P.tt(MX[:, mc, n0:n0 + n], TMP[ma_][:, 0:n], TMP[mb_][:, 0:n], ALU.add,
                         r=[t_TMP[ma_], t_TMP[mb_]], w=[t_MX[c] for c in tb])
            for i in range(NQB):
                k = i % 2
                P.dma("sp", xr[k][:, :], x[Q0 + i * 128:Q0 + (i + 1) * 128, :], w=[t_xr[k]])
                for half in range(2):
                    b = nextps(0, 7)
                    for mc in range(8):
                        P.mm(PS[b][:, 0:512], MX[:, mc, i * 128:(i + 1) * 128], WO[:, mc, half * 512:(half + 1) * 512],
                             mc == 0, mc == 7, r=[t_MX[i], t_WO], w=[tPS[b]])
                    P.tt(xr[k][:, half * 512:(half + 1) * 512], PS[b][:, 0:512], xr[k][:, half * 512:(half + 1) * 512],
                         ALU.add, r=[tPS[b], t_xr[k]], w=[t_xr[k]])
                P.dma("sp", xm_d[i * 128:(i + 1) * 128, :], xr[k][:, :], r=[t_xr[k]], w=[t_xmd[i]])
            P.barrier()
        if stage == 5:
            return done()
    P.barrier()

    with contextlib.ExitStack() as FF:
        H2T = sb(FF, "H2T", [128, 8, 2050], BF16)
        t_H2T = trks(NQB)
        gfr = sb(FF, "gfr", [128, D], F32)
        gfin = sb(FF, "gfin", [128, D], F32)
        t_g = Trk()
        P.dma("sp", gfr[:], g_ffn[0:1, :].partition_broadcast(128), w=[t_g])
        P.dma("sp", gfin[:], g_fin[0:1, :].partition_broadcast(128), w=[t_g])
        cwr = sb(FF, "cwr", [128, 4, 128], F32)
        cw = sb(FF, "cw", [128, 4, 22], F32)
        t_cw = Trk()
        P.dma("sp", cwr[0:22, 0:3, :], conv_w.rearrange("k (j p) -> j k p", p=128), w=[t_cw])
        P.dma("sp", cwr[0:22, 3, :], conv_b.rearrange("o (j p) -> (o j) p", p=128), w=[t_cw])
        for kk in range(4):
            b = nextps(0, 7)
            P.tr(PS[b][:, 0:22], cwr[0:22, kk, :], ident[0:22, 0:22], r=[t_cw, t_const], w=[tPS[b]])
            P.copy("dve", cw[:, kk, :], PS[b][:, 0:22], r=[tPS[b]], w=[t_cw])
        xt2 = [sb(FF, "fx%d" % i, [128, D], F32) for i in range(2)]
        t_xt = trks(2)
        xn2 = [sb(FF, "fn%d" % i, [128, D], BF16) for i in range(2)]
        t_xn = trks(2)
        yo2 = [sb(FF, "yo%d" % i, [128, D], F32) for i in range(2)]
        t_yo = trks(2)
        junk = sb(FF, "fjunk", [128, D], F32)
        ss2 = [sb(FF, "fss%d" % i, [128, 4], F32) for i in range(2)]
        t_ss = trks(2)
        for i in range(NQB):
            k = i % 2
            P.dma("sp", xt2[k][:], xm_d[i * 128:(i + 1) * 128, :], r=[t_xmd[i]], w=[t_xt[k]])
            rmsnorm_tile(junk, ss2[k], t_ss[k], xt2[k][:], t_xt[k], gfr[:], t_g, xn2[k][:], t_xn[k])
            for c in range(8):
                P.tr(PST[:, c * 128:(c + 1) * 128], xn2[k][:, c * 128:(c + 1) * 128], identb[:],
                     r=[t_xn[k], t_const], w=[tPST])
            pv8 = PST[:, :].rearrange("p (c n) -> p c n", c=8)
            if i == 0:
                P.ts(H2T[:, :, 0:2], pv8[:, :, 126:128], halo[:, 0:1], ALU.mult, r=[tPST, t_const], w=[t_H2T[0]])
            else:
                P.copy(evac_eng(), H2T[:, :, 2 + (i - 1) * 128:2 + i * 128], pv8, r=[tPST], w=[t_H2T[i]])
        if stage == 6:
            dump("H2T", H2T[:, :, 0:258], t_H2T)
            dump("cw", cw[:, :, :], [t_cw])
            return done()
        WD = sb(FF, "WD", [128, NFF, D], BF16)
        t_WD = Trk()
        wdv = w_down.rearrange("(j p) n -> p j n", p=128)
        P.dma_multi("pool", [(WD[:, j0:min(j0 + 6, NFF), :], wdv[:, j0:min(j0 + 6, NFF), :]) for j0 in range(0, NFF, 6)],
                    w=[t_WD])
        AT = sb(FF, "AT", [128, NFF, 1024], BF16)
        t_AT = trks(NFF)
        wu = [sb(FF, "wu%d" % k, [128, 8, 256], BF16) for k in range(2)]
        t_wu = trks(2)
        U = [sb(FF, "U%d" % k, [128, 1026], F32) for k in range(2)]
        t_U = trks(2)
        C1 = sb(FF, "C1", [128, 1024], F32)
        C2 = sb(FF, "C2", [128, 1024], F32)
        t_C1, t_C2 = Trk(), Trk()
        FG = [sb(FF, "FG%d" % k, [128, 512], F32) for k in range(2)]
        t_FG = trks(2)

        def gelu2(out, t_out, xin, t_x, n):
            a, b2 = FG[0][:, 0:n], FG[1][:, 0:n]
            P.tt(a, xin, xin, ALU.mult, r=[t_x], w=[t_FG[0]])
            P.ts(a, a, 0.044715, ALU.mult, 1.0, ALU.add, r=[t_FG[0]], w=[t_FG[0]])
            P.tt(a, a, xin, ALU.mult, r=[t_FG[0], t_x], w=[t_FG[0]])
            P.act(b2, a, AF.Sigmoid, r=[t_FG[0]], w=[t_FG[1]], scale=1.5957691216057308)
            P.tt(out, b2, xin, ALU.mult, r=[t_FG[1], t_x], w=t_out)

        jrr = [0]
        for th in range(2):
            base = 2 + th * 1024
            hts = [t_H2T[c] for c in range(max(0, th * 8), th * 8 + 9)]
            for j in range(NFF):
                k = jrr[0] % 2
                jrr[0] += 1
                P.dma_multi("pool", [(wu[k][:, :, 0:128], w_up[:, j * 128:(j + 1) * 128].rearrange("(c p) n -> p c n", p=128)),
                                     (wu[k][:, :, 128:256],
                                      w_up[:, DFF + j * 128:DFF + (j + 1) * 128].rearrange("(c p) n -> p c n", p=128))],
                            w=[t_wu[k]])
                b = nextps(0, 7)
                for dm in range(8):
                    P.mm(PS[b][:, 0:2], wu[k][:, dm, 0:128], H2T[:, dm, base - 2:base], dm == 0, dm == 7,
                         r=[t_wu[k]] + hts, w=[tPS[b]])
                P.copy("dve", U[k][:, 0:2], PS[b][:, 0:2], r=[tPS[b]], w=[t_U[k]])
                for nt in range(2):
                    b = nextps(0, 7)
                    for dm in range(8):
                        P.mm(PS[b][:, 0:512], wu[k][:, dm, 0:128], H2T[:, dm, base + nt * 512:base + (nt + 1) * 512],
                             dm == 0, dm == 7, r=[t_wu[k]] + hts, w=[tPS[b]])
                    P.copy("act", U[k][:, 2 + nt * 512:2 + (nt + 1) * 512], PS[b][:, 0:512], r=[tPS[b]], w=[t_U[k]])
                bg = []
                for nt in range(2):
                    b = nextps(0, 7)
                    bg.append(b)
                    for dm in range(8):
                        P.mm(PS[b][:, 0:512], wu[k][:, dm, 128:256], H2T[:, dm, base + nt * 512:base + (nt + 1) * 512],
                             dm == 0, dm == 7, r=[t_wu[k]] + hts, w=[tPS[b]])
                P.ts(C1[:, :], U[k][:, 2:1026], cw[:, 2, j:j + 1], ALU.mult, cw[:, 3, j:j + 1], ALU.add,
                     r=[t_U[k], t_cw], w=[t_C1])
                P.stt(C1[:, :], U[k][:, 1:1025], cw[:, 1, j:j + 1], C1[:, :], ALU.mult, ALU.add,
                      r=[t_U[k], t_cw, t_C1], w=[t_C1])
                P.stt(C1[:, :], U[k][:, 0:1024], cw[:, 0, j:j + 1], C1[:, :], ALU.mult, ALU.add,
                      r=[t_U[k], t_cw, t_C1], w=[t_C1])
                for nt in range(2):
                    sl = slice(nt * 512, (nt + 1) * 512)
                    gelu2(C2[:, sl], [t_C2], C1[:, sl], t_C1, 512)
                    P.tt(AT[:, j, sl], C2[:, sl], PS[bg[nt]][:, 0:512], ALU.mult, r=[t_C2, tPS[bg[nt]]], w=[t_AT[j]])
            if stage == 7:
                dump("AT", AT[:, 0:2, :], t_AT)
                return done()
            for kt in range(8):
                i = 1 + th * 8 + kt
                k = kt % 2
                P.dma("sp", xt2[k][:], xm_d[i * 128:(i + 1) * 128, :], r=[t_xmd[i]], w=[t_xt[k]])
                for half in range(2):
                    b = nextps(0, 7)
                    for j in range(NFF):
                        P.mm(PS[b][:, 0:512], AT[:, j, kt * 128:(kt + 1) * 128], WD[:, j, half * 512:(half + 1) * 512],
                             j == 0, j == NFF - 1, r=[t_AT[j], t_WD], w=[tPS[b]])
                    P.tt(xt2[k][:, half * 512:(half + 1) * 512], PS[b][:, 0:512], xt2[k][:, half * 512:(half + 1) * 512],
                         ALU.add, r=[tPS[b], t_xt[k]], w=[t_xt[k]])
                rmsnorm_tile(junk, ss2[k], t_ss[k], xt2[k][:], t_xt[k], gfin[:], t_g, yo2[k][:], t_yo[k])
                P.dma("sp", y[(th * 8 + kt) * 128:(th * 8 + kt + 1) * 128, :], yo2[k][:], r=[t_yo[k]])
    return done()


W_NAMES = ["g_mix", "w_in", "pe_cmp_k", "w_cmp_k1", "w_cmp_k2", "pe_cmp_v", "w_cmp_v1", "w_cmp_v2",
           "w_proj_nsa", "w_proj_dil", "w_out", "g_ffn", "w_up", "conv_w", "conv_b", "w_down", "g_final"]


def make_in_maps(inputs, cores):
    x = np.asarray(inputs["x"], dtype=np.float32)
    shared = {}
    for n in W_NAMES:
        a = np.asarray(inputs[n], dtype=np.float32)
        if n == "g_final":
            a = a.reshape(1, D)
        elif a.shape[0] == 1:
            a = a[0]
        if a.ndim == 1:
            a = a.reshape(1, -1)
        shared[n] = np.ascontiguousarray(a)
    tabs = [const_tables(0), const_tables(1)]
    maps = []
    for (b, hf) in cores:
        m = dict(shared)
        if hf == 1:
            xl = x[b]
        else:
            xl = np.concatenate([np.zeros((2048, D), np.float32), x[b, :2048]], axis=0)
        m["x"] = np.ascontiguousarray(xl)
        for k, v in tabs[hf].items():
            m["c_" + k] = v
        maps.append(m)
    return maps


_NC_CACHE = {}


N_LAUNCH = 1


def kernel(**inputs):
    if "nc" not in _NC_CACHE:
        _NC_CACHE["nc"] = build()
    nc = _NC_CACHE["nc"]
    cores = [(b, hf) for b in range(4) for hf in range(2)]
    out = np.zeros((4, T, D), np.float32)
    per = len(cores) // N_LAUNCH
    for li in range(N_LAUNCH):
        cs = cores[li * per:(li + 1) * per]
        maps = make_in_maps(inputs, cs)
        res = run_bass_kernel_spmd(nc, maps, core_ids=list(range(len(cs))))
        for ci, (b, hf) in enumerate(cs):
            out[b, hf * 2048:(hf + 1) * 2048, :] = res.results[ci]["y"]
    return out
```

```python
import contextlib
import numpy as np
import ml_dtypes
import concourse.bass as bass
import concourse.mybir as mybir
from concourse.bass_utils import run_bass_kernel_spmd

F32 = mybir.dt.float32
BF16 = mybir.dt.bfloat16
AF = mybir.ActivationFunctionType
ALU = mybir.AluOpType
AX = mybir.AxisListType

D = 1024
T = 4096
NCH = 32
QB0 = 15
NQB = 17
NQ = NQB * 128
Q0 = QB0 * 128
DFF = 2816
NFF = 22
DIN = 5656
NEG = -30000.0
DIL = ((128, 1), (512, 4), (2048, 16))
NDS = 48
NDS_HW = 32
SW_LIMIT = 300


class Trk:
    __slots__ = ("w", "r", "excl")

    def __init__(self, excl=False):
        self.w = []
        self.r = []
        self.excl = excl


def trks_ex(n):
    return [Trk(True) for _ in range(n)]


def trks(n):
    return [Trk() for _ in range(n)]


class Prog:
    def __init__(self):
        self.nc = bass.Bass("TRN2", target_bir_lowering=False)
        nc = self.nc
        self.es = contextlib.ExitStack()
        self.E = {"pe": nc.tensor, "act": nc.scalar, "dve": nc.vector, "pool": nc.gpsimd, "sp": nc.sync}
        self.sem = {k: self.es.enter_context(nc.semaphore("s_" + k)) for k in self.E}
        self.cnt = {k: 0 for k in self.E}
        self.seen = {k: {} for k in self.E}
        self.seen_seq = {k: {} for k in self.E}
        self.opseq = {k: 0 for k in self.E}
        self.last_ins = {k: None for k in self.E}
        self.sigmap = {k: [] for k in self.E}
        self.dsem = [self.es.enter_context(nc.semaphore("d%d" % i)) for i in range(NDS)]
        self.dcnt = [0] * NDS
        self.dnext = 0
        self.dnext_sw = NDS_HW
        self.sw_out = []
        self.ninst = 0

    def _resolve(self, key, seq):
        sm = self.sigmap[key]
        lo, hi = 0, len(sm)
        while lo < hi:
            mid = (lo + hi) // 2
            if sm[mid][0] >= seq:
                hi = mid
            else:
                lo = mid + 1
        if lo < len(sm):
            return sm[lo][1]
        self.last_ins[key].then_inc(self.sem[key], 1)
        self.cnt[key] += 1
        sm.append((self.opseq[key], self.cnt[key]))
        return self.cnt[key]

    def _wait(self, eng, dep):
        key, val = dep
        if key == "pe" and eng == "pe":
            return
        if isinstance(key, str):
            if self.seen_seq[eng].get(key, 0) >= val:
                return
            self.seen_seq[eng][key] = val
            val = self._resolve(key, val)
        if self.seen[eng].get(key, 0) >= val:
            return
        self.seen[eng][key] = val
        sem = self.sem[key] if isinstance(key, str) else self.dsem[key[1]]
        self.E[eng].wait_ge(sem, val)

    def _deps(self, eng, r, w):
        deps = set()
        for t in r:
            deps.update(t.w)
            if t.excl:
                deps.update(d for d in t.r if d[0] != eng)
        for t in w:
            deps.update(t.w)
            deps.update(t.r)
        best = {}
        for (k, v) in deps:
            if best.get(k, 0) < v:
                best[k] = v
        for k in sorted(best, key=str):
            self._wait(eng, (k, best[k]))

    def _mark(self, toks, r, w):
        for t in w:
            t.w = list(toks)
            t.r = []
        for t in r:
            if t not in w:
                t.r.extend(toks)

    def op(self, eng, fn, r=(), w=()):
        self._deps(eng, r, w)
        ins = fn(self.E[eng])
        self.opseq[eng] += 1
        self.last_ins[eng] = ins
        self.ninst += 1
        tok = (eng, self.opseq[eng])
        self._mark([tok], r, w)
        return tok

    def dma(self, q, out, in_, r=(), w=()):
        return self.dma_multi(q, [(out, in_)], r=r, w=w)

    def dma_multi(self, q, pieces, r=(), w=()):
        self._deps(q, r, w)
        toks = []
        for (out, in_) in pieces:
            if q == "pool":
                nd = 1
                for d_ in list(out.shape)[:-1]:
                    nd *= int(d_)
                nd = nd // 16 + 2
                while self.sw_out and sum(n_ for (_, n_) in self.sw_out) + nd > SW_LIMIT:
                    tok0, _ = self.sw_out.pop(0)
                    self._wait(q, tok0)
                j = self.dnext_sw
                self.dnext_sw = NDS_HW + (j + 1 - NDS_HW) % (NDS - NDS_HW)
            else:
                j = self.dnext
                self.dnext = (j + 1) % NDS_HW
            if self.dcnt[j] > 0:
                self._wait(q, (("d", j), 16 * self.dcnt[j]))
            self.dcnt[j] += 1
            self.E[q].dma_start(out=out, in_=in_).then_inc(self.dsem[j], 16)
            self.ninst += 1
            toks.append((("d", j), 16 * self.dcnt[j]))
            if q == "pool":
                self.sw_out.append((toks[-1], nd))
        self._mark(toks, r, w)
        return toks

    def barrier(self):
        for e in self.E:
            for e2 in self.E:
                if e2 != e and self.opseq[e2] > 0:
                    self._wait(e, (e2, self.opseq[e2]))
            for j in range(NDS):
                if self.dcnt[j] > 0:
                    self._wait(e, (("d", j), 16 * self.dcnt[j]))

    def mm(self, out, lhsT, rhs, start, stop, r=(), w=(), skip=False):
        if skip:
            return self.op("pe", lambda e: e.matmul(out, lhsT=lhsT, rhs=rhs, start=start, stop=stop,
                                                    skip_group_check=True), r=r, w=w)
        return self.op("pe", lambda e: e.matmul(out, lhsT=lhsT, rhs=rhs, start=start, stop=stop), r=r, w=w)

    def tr(self, out, in_, ident, r=(), w=()):
        return self.op("pe", lambda e: e.transpose(out, in_, ident), r=r, w=w)

    def act(self, out, in_, func, r=(), w=(), **kw):
        return self.op("act", lambda e: e.activation(out=out, in_=in_, func=func, **kw), r=r, w=w)

    def copy(self, eng, out, in_, r=(), w=(), scale=None):
        if eng == "act":
            if scale is None:
                return self.act(out, in_, AF.Copy, r=r, w=w)
            return self.act(out, in_, AF.Copy, r=r, w=w, scale=float(scale))
        if scale is None:
            return self.op(eng, lambda e: e.tensor_copy(out=out, in_=in_), r=r, w=w)
        return self.op(eng, lambda e: e.tensor_scalar(out=out, in0=in_, scalar1=float(scale), scalar2=None,
                                                      op0=ALU.mult), r=r, w=w)

    def tt(self, out, in0, in1, op, r=(), w=(), eng="dve"):
        return self.op(eng, lambda e: e.tensor_tensor(out=out, in0=in0, in1=in1, op=op), r=r, w=w)

    def ts(self, out, in0, s1, op0, s2=None, op1=None, r=(), w=(), eng="dve"):
        if op1 is None:
            return self.op(eng, lambda e: e.tensor_scalar(out=out, in0=in0, scalar1=s1, scalar2=None, op0=op0),
                           r=r, w=w)
        return self.op(eng, lambda e: e.tensor_scalar(out=out, in0=in0, scalar1=s1, scalar2=s2, op0=op0, op1=op1),
                       r=r, w=w)

    def stt(self, out, in0, scalar, in1, op0, op1, r=(), w=()):
        return self.op("dve", lambda e: e.scalar_tensor_tensor(out=out, in0=in0, scalar=scalar, in1=in1,
                                                               op0=op0, op1=op1), r=r, w=w)


def alibi(n):
    return (2.0 ** (-8.0 * np.arange(1, n + 1) / n)).astype(np.float64)


def bf(a):
    return np.asarray(a, dtype=np.float32).astype(ml_dtypes.bfloat16)


def const_tables(hf):
    c = {}
    c["ident"] = np.eye(128, dtype=np.float32)
    c["identb"] = bf(np.eye(128))
    pos = np.arange(T)
    gpos = pos if hf == 1 else pos - 2048
    valid = (gpos >= 0).astype(np.float32)
    c["validtm"] = bf(valid.reshape(NCH, 128).T)
    vd = np.zeros((128, 3, 32), np.float32)
    for g, (w, r) in enumerate(DIL):
        npc = 32 // r
        for rho in range(r):
            for jc in range(npc):
                tok = rho + r * (128 * jc + np.arange(128))
                vd[:, g, rho * npc + jc] = valid[tok]
    c["validd"] = bf(vd)
    c["halo"] = np.full((128, 1), 1.0 if hf == 1 else 0.0, np.float32)
    ea = np.zeros((128, T), np.float32)
    ea[pos // 64, pos] = 1.0
    ea[64] = 128.0 * (pos // 128)
    ea[65] = pos % 128
    ea[66] = 1.0
    ea[67] = 1.0
    c["ea"] = bf(ea)
    sl = alibi(8)
    rbc = np.zeros((4, NQB, 2, 4, 128), np.float32)
    ql = np.arange(128)
    for i in range(NQB):
        qb = QB0 + i
        for g in range(2):
            for r in range(4):
                s = sl[4 * g + r]
                rbc[0, i, g, r, :] = s
                rbc[1, i, g, r, :] = s
                rbc[2, i, g, r, :] = -s * 128.0 * qb
                rbc[3, i, g, r, :] = -s * ql
    c["rbc"] = bf(rbc)
    kk = np.arange(128)[:, None]
    qq = np.arange(128)[None, :]
    caus = np.where(kk <= qq, 0.0, NEG).astype(np.float32)
    acaus = np.where(kk > qq, 0.0, NEG).astype(np.float32)
    c["caus"] = np.ascontiguousarray(np.broadcast_to(caus[:, None, :], (128, 4, 128)))
    c["acaus"] = np.ascontiguousarray(np.broadcast_to(acaus[:, None, :], (128, 4, 128)))
    bc = np.zeros((NQB, 128, 2, 2, 4, 128), np.float32)
    for i in range(NQB):
        t = (QB0 + i) * 128 + np.arange(128)
        for kc in range(2):
            cc = kc * 128 + np.arange(128)
            cend = 16 * cc + 31
            cstart_g = 16 * cc - (0 if hf == 1 else 2048)
            dist = t[None, :] - cend[:, None]
            ok = (dist >= 0) & (cstart_g[:, None] >= 0) & (cc[:, None] < 255)
            for g in range(2):
                for r in range(4):
                    bc[i, :, kc, g, r, :] = np.where(ok, -sl[4 * g + r] * dist, NEG)
    c["bcmp"] = bc
    cc = np.arange(256)
    ss = np.arange(64)
    ov = np.clip(np.minimum(16 * cc[:, None] + 32, 64 * ss[None, :] + 64)
                 - np.maximum(16 * cc[:, None], 64 * ss[None, :]), 0, None) / 32.0
    c["ovl"] = bf(ov.reshape(2, 128, 64).transpose(1, 0, 2))
    ma = np.zeros((128, NQB, 64), np.float32)
    mb = np.zeros((128, NQB, 64), np.float32)
    b0 = 0 if hf == 1 else 32
    for i in range(NQB):
        t = (QB0 + i) * 128 + np.arange(128)
        cur = t // 64
        for s in range(64):
            forced = (s == cur) | (s == b0)
            future = (s > cur) | (s < b0)
            normal = (~forced) & (~future)
            ma[:, i, s] = normal
            mb[:, i, s] = np.where(future, -1.0, np.where(forced, 1.0e4, 0.0))
    c["ma"] = ma
    c["mb"] = mb
    sd = alibi(12).reshape(3, 4)
    bd = np.zeros((128, 3, 2, 4, 128), np.float32)
    for g, (w, r) in enumerate(DIL):
        for dl in range(2):
            dj = dl * 128 + qq - kk
            ok = (dj >= 0) & (dj <= 128)
            for h in range(4):
                bd[:, g, dl, h, :] = np.where(ok, -sd[g, h] * r * dj, NEG)
    c["bdil"] = bd
    s65 = np.zeros((65, 64), np.float32)
    s65[64, :] = 1.0
    c["s65"] = s65
    return c


def build(stage=99):
    P = Prog()
    with P.es:
        return _build(P, stage)


def _build(P, stage):
    nc = P.nc
    es = P.es

    def dram_in(name, shape, dt=F32):
        return nc.dram_tensor(name, list(shape), dt, kind="ExternalInput").ap()

    def sb(stack, name, shape, dt):
        return stack.enter_context(nc.sbuf_tensor(name, list(shape), dt))

    def ps(stack, name, shape, dt):
        return stack.enter_context(nc.psum_tensor(name, list(shape), dt))

    x = dram_in("x", [T, D])
    g_mix = dram_in("g_mix", [1, D])
    w_in = dram_in("w_in", [D, DIN])
    pe_k = dram_in("pe_cmp_k", [32, 64])
    w_k1 = dram_in("w_cmp_k1", [2048, 128])
    w_k2 = dram_in("w_cmp_k2", [128, 64])
    pe_v = dram_in("pe_cmp_v", [32, 64])
    w_v1 = dram_in("w_cmp_v1", [2048, 128])
    w_v2 = dram_in("w_cmp_v2", [128, 64])
    w_pa = dram_in("w_proj_nsa", [512, D])
    w_pb = dram_in("w_proj_dil", [256, D])
    w_out = dram_in("w_out", [D, D])
    g_ffn = dram_in("g_ffn", [1, D])
    w_up = dram_in("w_up", [D, 2 * DFF])
    conv_w = dram_in("conv_w", [3, DFF])
    conv_b = dram_in("conv_b", [1, DFF])
    w_down = dram_in("w_down", [DFF, D])
    g_fin = dram_in("g_final", [1, D])
    c_ident = dram_in("c_ident", [128, 128])
    c_identb = dram_in("c_identb", [128, 128], BF16)
    c_validtm = dram_in("c_validtm", [128, 32], BF16)
    c_validd = dram_in("c_validd", [128, 3, 32], BF16)
    c_halo = dram_in("c_halo", [128, 1])
    c_ea = dram_in("c_ea", [128, T], BF16)
    c_rbc = dram_in("c_rbc", [4, NQB, 2, 4, 128], BF16)
    c_caus = dram_in("c_caus", [128, 4, 128])
    c_acaus = dram_in("c_acaus", [128, 4, 128])
    c_bcmp = dram_in("c_bcmp", [NQB, 128, 2, 2, 4, 128])
    c_ovl = dram_in("c_ovl", [128, 2, 64], BF16)
    c_ma = dram_in("c_ma", [128, NQB, 64])
    c_mb = dram_in("c_mb", [128, NQB, 64])
    c_bdil = dram_in("c_bdil", [128, 3, 2, 4, 128])
    c_s65 = dram_in("c_s65", [65, 64])
    y = nc.dram_tensor("y", [2048, D], F32, kind="ExternalOutput").ap()
    xm_d = nc.dram_tensor("xm_scratch", [NQ, D], F32, kind="Internal").ap()
    t_xmd = trks(NQB)

    def dump(name, ap, trk):
        o = nc.dram_tensor("dbg_" + name, list(ap.shape), ap.dtype, kind="ExternalOutput").ap()
        P.dma("sp", o, ap, r=trk)

    def done():
        for j in range(NDS):
            if P.dcnt[j] > 0:
                P._wait("sp", (("d", j), 16 * P.dcnt[j]))
        return nc

    PS = [ps(es, "ps%d" % i, [128, 512], F32) for i in range(7)]
    PST = ps(es, "pst", [128, 1024], BF16)
    tPS = trks_ex(7)
    tPST = Trk(True)
    psrr = [0]

    def nextps(lo, hi):
        k = lo + psrr[0] % (hi - lo)
        psrr[0] += 1
        return k

    def v4(ap):
        return ap.rearrange("p (a b) -> p a b", a=4)

    ident = sb(es, "ident", [128, 128], F32)
    identb = sb(es, "identb", [128, 128], BF16)
    halo = sb(es, "halo", [128, 1], F32)
    t_const = Trk()
    P.dma("sp", ident[:], c_ident[:, :], w=[t_const])
    P.dma("sp", identb[:], c_identb[:, :], w=[t_const])
    P.dma("sp", halo[:], c_halo[:, :], w=[t_const])

    evac_rr = [0]

    def evac_eng():
        evac_rr[0] += 1
        return "act" if evac_rr[0] % 2 else "dve"

    def rmsnorm_tile(junk, ss, t_s, xt, t_x, gt, t_g, out_t, t_out):
        P.act(junk[:], xt, AF.Square, r=[t_x], w=[t_s], accum_out=ss[:, 0:1])
        P.ts(ss[:, 1:2], ss[:, 0:1], 1.0 / D, ALU.mult, 1e-6, ALU.add, r=[t_s], w=[t_s])
        P.act(ss[:, 2:3], ss[:, 1:2], AF.Sqrt, r=[t_s], w=[t_s])
        P.op("dve", lambda e: e.reciprocal(out=ss[:, 3:4], in_=ss[:, 2:3]), r=[t_s], w=[t_s])
        P.stt(out_t, xt, ss[:, 3:4], gt, ALU.mult, ALU.mult, r=[t_x, t_g, t_s], w=[t_out])

    def chunks_of(tok0, n):
        return list(range(tok0 // 128, (tok0 + n - 1) // 128 + 1))

    hT_d = nc.dram_tensor("hT_scratch", [128, 8, T], BF16, kind="Internal").ap()
    t_hTd = trks(2)
    with contextlib.ExitStack() as PA:
        OAT = sb(PA, "OAT", [128, 4, NQ], BF16)
        t_OAT = trks(NQB)
        NPT = 4
        PT = [sb(PA, "PT%d" % i, [128, 512], BF16) for i in range(NPT)]
        t_PT = trks(NPT)
        ptrr = [0]
        NTMP = 3
        TMP = [sb(PA, "TMP%d" % i, [128, 512], F32) for i in range(NTMP)]
        t_TMP = trks(NTMP)
        tmprr = [0]

        def next_pt():
            k = ptrr[0] % NPT
            ptrr[0] += 1
            return k

        def next_tmp():
            k = tmprr[0] % NTMP
            tmprr[0] += 1
            return k

        GT = [sb(PA, "GT%d" % i, [128, 512], F32) for i in range(2)]
        t_GT = trks(2)

        def gelu_tanh(out, t_out, xin, t_x, n):
            a, b2 = GT[0][:, 0:n], GT[1][:, 0:n]
            P.tt(a, xin, xin, ALU.mult, r=[t_x], w=[t_GT[0]])
            P.ts(a, a, 0.044715, ALU.mult, 1.0, ALU.add, r=[t_GT[0]], w=[t_GT[0]])
            P.tt(a, a, xin, ALU.mult, r=[t_GT[0], t_x], w=[t_GT[0]])
            P.act(b2, a, AF.Sigmoid, r=[t_GT[0]], w=[t_GT[1]], scale=1.5957691216057308)
            P.tt(out, b2, xin, ALU.mult, r=[t_GT[1], t_x], w=t_out)

        def load_wt(dst, t_dst, src, col_specs):
            o = 0
            pieces = []
            for (c0, n) in col_specs:
                pieces.append((dst[:, :, o:o + n], src[:, c0:c0 + n].rearrange("(c p) n -> p c n", p=128)))
                o += n
            P.dma_multi("pool", pieces, w=[t_dst])

        def phase1(ph, hdst, t_hdst, tiles):
            xt2 = [sb(ph, "xt%d" % i, [128, D], F32) for i in range(2)]
            t_xt = trks(2)
            xn2 = [sb(ph, "xn%d" % i, [128, D], BF16) for i in range(2)]
            t_xn = trks(2)
            gmr = sb(ph, "gmr", [128, D], F32)
            t_gm = Trk()
            junk = sb(ph, "junk", [128, D], F32)
            ss2 = [sb(ph, "ss%d" % i, [128, 4], F32) for i in range(2)]
            t_ss = trks(2)
            P.dma("sp", gmr[:], g_mix[0:1, :].partition_broadcast(128), w=[t_gm])

            def run(tiles_):
                for kk, t in enumerate(tiles_):
                    k = kk % 2
                    P.dma("sp", xt2[k][:], x[t * 128:(t + 1) * 128, :], w=[t_xt[k]])
                    rmsnorm_tile(junk, ss2[k], t_ss[k], xt2[k][:], t_xt[k], gmr[:], t_gm, xn2[k][:], t_xn[k])
                    for c in range(8):
                        P.tr(PST[:, c * 128:(c + 1) * 128], xn2[k][:, c * 128:(c + 1) * 128], identb[:],
                             r=[t_xn[k], t_const], w=[tPST])
                    P.copy(evac_eng(), hdst[:, :, kk * 128:(kk + 1) * 128],
                           PST[:, :].rearrange("p (c n) -> p c n", c=8), r=[tPST], w=[t_hdst[kk]])
            return run

        with contextlib.ExitStack() as nsa:
            QN = [sb(nsa, "QN%d" % g, [128, 4, NQ], BF16) for g in range(2)]
            t_QN = trks(NQB)
            KS = sb(nsa, "KS", [128, T], BF16)
            KW = sb(nsa, "KW", [128, T], BF16)
            t_KS = trks(NCH)
            t_KW = trks(NCH)
            VS = sb(nsa, "VS", [128, NCH, 2, 66], BF16)
            VW = sb(nsa, "VW", [128, NCH, 2, 66], BF16)
            t_VS = trks(NCH)
            t_VW = trks(NCH)
            GA = sb(nsa, "GA", [128, NQB, 24], F32)
            t_GA = trks(NQB)
            KC = sb(nsa, "KC", [128, 256], BF16)
            t_KC = Trk()
            VCX = sb(nsa, "VCX", [128, 2, 2, 130], BF16)
            t_VCX = Trk()
            vtm = sb(nsa, "vtm", [128, 32], BF16)
            t_vtm = Trk()
            P.dma("sp", vtm[:], c_validtm[:, :], w=[t_vtm])
            P.copy("dve", VS[:, :, :, 64], vtm[:].unsqueeze(2).broadcast_to([128, NCH, 2]), r=[t_vtm], w=t_VS)
            P.copy("dve", VW[:, :, :, 64], vtm[:].unsqueeze(2).broadcast_to([128, NCH, 2]), r=[t_vtm], w=t_VW)
            P.op("dve", lambda e: e.memset(QN[0][64:128, :, :], 0.0), w=t_QN)
            P.op("dve", lambda e: e.memset(QN[1][0:64, :, :], 0.0), w=t_QN)

            with contextlib.ExitStack() as ph:
                hTh = sb(ph, "hTh", [128, 8, 2048], BF16)
                t_hTh = trks(16)
                wq = sb(ph, "wq", [128, 8, 512], BF16)
                wk = sb(ph, "wk", [128, 8, 512], BF16)
                wv = sb(ph, "wv", [128, 8, 280], BF16)
                t_wq, t_wk, t_wv = Trk(), Trk(), Trk()
                load_wt(wq, t_wq, w_in, [(0, 64), (256, 64), (64, 64), (320, 64), (128, 64), (384, 64), (192, 64),
                                         (448, 64)])
                load_wt(wk, t_wk, w_in, [(768, 128), (1024, 128), (512, 128), (640, 128)])
                load_wt(wv, t_wv, w_in, [(896, 128), (1152, 128), (1280, 24)])
                SRD = [sb(ph, "SRD%d" % kv, [128, 2, 16, 256], BF16) for kv in range(2)]
                t_SRD = trks(2)
                for kv in range(2):
                    P.op("dve", lambda e, kv=kv: e.memset(SRD[kv][:, 1, :, 255:256], 0.0), w=[t_SRD[kv]])
                with contextlib.ExitStack() as p1s:
                    run_p1 = phase1(p1s, hTh, t_hTh, None)
                    for hh_ in range(2):
                        run_p1(list(range(16 * hh_, 16 * hh_ + 16)))
                        if stage != 23.1:
                            P.dma("sp", hT_d[:, :, hh_ * 2048:(hh_ + 1) * 2048], hTh[:, :, :], r=t_hTh,
                                  w=[t_hTd[hh_]])
                        if stage == 1 and hh_ == 0:
                            dump("hTh", hTh[:, :, Q0:2048], t_hTh)
                            return done()

                        def fm(wt, t_w, wc0, lt0, n, dst, t_dst, scale=None, eng=None, dst2=None):
                            b = nextps(0, 4)
                            for dm in range(8):
                                P.mm(PS[b][:, 0:n], wt[:, dm, wc0:wc0 + 128], hTh[:, dm, lt0:lt0 + n],
                                     start=(dm == 0), stop=(dm == 7),
                                     r=[t_w] + [t_hTh[c] for c in chunks_of(lt0, n)], w=[tPS[b]])
                            return b

                        if stage == 23.2:
                            continue
                        qtiles = [(Q0, 128, 0)] if hh_ == 0 else [(n0, 512, 128 + n0) for n0 in range(0, 2048, 512)]
                        for r_ in range(4):
                            for (lt0, n, qoff) in qtiles:
                                b = fm(wq, t_wq, 128 * r_, lt0, n, None, None)
                                tq = [t_QN[c] for c in chunks_of(qoff, n)]
                                P.copy("act", QN[0][0:64, r_, qoff:qoff + n], PS[b][0:64, 0:n], r=[tPS[b]], w=tq,
                                       scale=0.125)
                                P.copy("dve", QN[1][64:128, r_, qoff:qoff + n], PS[b][64:128, 0:n], r=[tPS[b]], w=tq,
                                       scale=0.125)
                        for n0 in range(0, 2048, 512):
                            g0 = hh_ * 2048 + n0
                            b = fm(wk, t_wk, 0, n0, 512, None, None)
                            P.copy(evac_eng(), KS[:, g0:g0 + 512], PS[b][:, 0:512], r=[tPS[b]],
                                   w=[t_KS[c] for c in chunks_of(g0, 512)])
                            b = fm(wk, t_wk, 128, n0, 512, None, None)
                            P.copy(evac_eng(), KW[:, g0:g0 + 512], PS[b][:, 0:512], r=[tPS[b]],
                                   w=[t_KW[c] for c in chunks_of(g0, 512)])
                            for kv in range(2):
                                b = fm(wk, t_wk, 256 + 128 * kv, n0, 512, None, None)
                                c0 = g0 // 16
                                pv_ = PS[b][:, 0:512].rearrange("d (c p) -> d p c", p=16)
                                P.copy("dve", SRD[kv][:, 0, :, c0:c0 + 32], pv_, r=[tPS[b]], w=[t_SRD[kv]])
                                if g0 == 0:
                                    P.copy("dve", SRD[kv][:, 1, :, 0:31],
                                           PS[b][:, 16:512].rearrange("d (c p) -> d p c", p=16),
                                           r=[tPS[b]], w=[t_SRD[kv]])
                                else:
                                    P.copy("dve", SRD[kv][:, 1, :, c0 - 1:c0 + 31], pv_, r=[tPS[b]], w=[t_SRD[kv]])
                        for tl in range(16):
                            t = hh_ * 16 + tl
                            b = nextps(0, 4)
                            for dm in range(8):
                                P.mm(PS[b][:, 0:280], hTh[:, dm, tl * 128:(tl + 1) * 128], wv[:, dm, 0:280],
                                     start=(dm == 0), stop=(dm == 7), r=[t_wv, t_hTh[tl]], w=[tPS[b]])
                            P.copy("dve", VS[:, t, :, 0:64], PS[b][:, 0:128].rearrange("p (g d) -> p g d", g=2),
                                   r=[tPS[b]], w=[t_VS[t]])
                            P.copy("dve", VW[:, t, :, 0:64], PS[b][:, 128:256].rearrange("p (g d) -> p g d", g=2),
                                   r=[tPS[b]], w=[t_VW[t]])
                            if t >= QB0:
                                P.act(GA[:, t - QB0, :], PS[b][:, 256:280], AF.Sigmoid, r=[tPS[b]],
                                      w=[t_GA[t - QB0]])
                P.barrier()
                if stage == 23.2:
                    dump("hTh", hTh[:, :, 0:128], t_hTh)
                    return done()
                if stage in (23, 23.1):
                    dump("QN0", QN[0][:, :, 0:256], t_QN)
                    dump("QN1", QN[1][:, :, 0:256], t_QN)
                    dump("KS", KS[:, Q0:Q0 + 256], t_KS)
                    dump("KW", KW[:, Q0:Q0 + 256], t_KW)
                    dump("VS", VS[:, 16:18, :, 0:65], t_VS)
                    dump("GA", GA[:, 0:2, :], t_GA)
                    return done()

                W1 = sb(ph, "W1", [128, 32, 128], BF16)
                W2 = sb(ph, "W2", [128, 64], BF16)
                peT = sb(ph, "peT", [128, 64], F32)
                peTb = sb(ph, "peTb", [128, 32, 2], BF16)
                hb = sb(ph, "hb", [128, 1], F32)
                HT = sb(ph, "HT", [128, 2, 256], BF16)
                t_W1 = Trk()
                t_W2 = Trk()
                t_pe = Trk()
                t_hb = Trk()
                t_HT = Trk()
                ovl = sb(ph, "ovl", [128, 2, 64], BF16)
                t_ovl = Trk()
                P.dma("sp", ovl[:], c_ovl[:, :, :], w=[t_ovl])
                for kv in range(2):
                    w1d, w2d, ped = ((w_k1, w_k2, pe_k), (w_v1, w_v2, pe_v))[kv]
                    w1v = w1d.rearrange("(p d) h -> d p h", d=64)
                    P.dma_multi("pool", [(W1[0:64, :, :], w1v), (W1[64:128, :, :], w1v)], w=[t_W1])
                    P.dma("pool", W2[:, :], w2d[:, :], w=[t_W2])
                    P.dma("sp", peT[0:32, 0:64], ped[:, :], w=[t_pe])
                    b = nextps(0, 4)
                    P.tr(PS[b][0:64, 0:32], peT[0:32, 0:64], ident[0:32, 0:32], r=[t_pe, t_const], w=[tPS[b]])
                    P.copy("dve", peTb[0:64, :, :], PS[b][0:64, 0:32].unsqueeze(2).broadcast_to([64, 32, 2]),
                           r=[tPS[b]], w=[t_pe])
                    b = nextps(0, 4)
                    for p_ in range(32):
                        P.mm(PS[b][:, 0:2], W1[0:64, p_, :], peTb[0:64, p_, :], start=(p_ == 0),
                             stop=(p_ == 31), r=[t_W1, t_pe], w=[tPS[b]])
                    P.copy("dve", hb[:, 0:1], PS[b][:, 0:1], r=[tPS[b]], w=[t_hb])
                    m = next_tmp()
                    for g in range(2):
                        b = 2 * g + (kv % 2)
                        for p_ in range(32):
                            P.mm(PS[b][:, 0:256], W1[64 * g:64 * g + 64, p_, :],
                                 SRD[kv][64 * g:64 * g + 64, p_ // 16, p_ % 16, :],
                                 start=(p_ == 0), stop=(p_ == 31), r=[t_W1, t_SRD[kv]], w=[tPS[b]])
                        P.ts(TMP[m][:, g * 256:(g + 1) * 256], PS[b][:, 0:256], hb[:, 0:1], ALU.add,
                             r=[tPS[b], t_hb], w=[t_TMP[m]])
                    gelu_tanh(HT[:, :, :].rearrange("p g c -> p (g c)"), [t_HT], TMP[m][:, :], t_TMP[m], 512)
                    if kv == 0:
                        for g in range(2):
                            b = nextps(4, 6)
                            P.mm(PS[b][0:64, 0:256], W2[:, :], HT[:, g, :], start=True, stop=True,
                                 r=[t_W2, t_HT], w=[tPS[b]])
                            P.copy("dve", KC[64 * g:64 * g + 64, :], PS[b][0:64, 0:256], r=[tPS[b]], w=[t_KC])
                    else:
                        for kc in range(2):
                            for g in range(2):
                                b = nextps(4, 6)
                                P.mm(PS[b][:, 0:64], HT[:, g, kc * 128:(kc + 1) * 128], W2[:, :], start=True,
                                     stop=True, r=[t_W2, t_HT], w=[tPS[b]])
                                P.copy("dve", VCX[:, kc, g, 0:64], PS[b][:, 0:64], r=[tPS[b]], w=[t_VCX])
                                P.copy("dve", VCX[:, kc, g, 66:130], ovl[:, kc, :], r=[t_ovl], w=[t_VCX])
                        P.op("dve", lambda e: e.memset(VCX[:, :, :, 64:65], 1.0), w=[t_VCX])
                        P.op("dve", lambda e: e.memset(VCX[:, :, :, 65:66], 0.0), w=[t_VCX])
                P.barrier()

            if stage == 2:
                dump("KC", KC[:, :], [t_KC])
                dump("VCX", VCX[:, :, :, :], [t_VCX])
                return done()

            ea = sb(nsa, "ea", [128, T], BF16)
            caus = sb(nsa, "caus", [128, 4, 128], F32)
            acaus = sb(nsa, "acaus", [128, 4, 128], F32)
            ma = sb(nsa, "ma", [128, NQB, 64], F32)
            mb = sb(nsa, "mb", [128, NQB, 64], F32)
            t_tab = Trk()
            P.dma("sp", ea[:], c_ea[:, :], w=[t_tab])
            P.dma("sp", caus[:], c_caus[:, :, :], w=[t_tab])
            P.dma("sp", acaus[:], c_acaus[:, :, :], w=[t_tab])
            P.dma("sp", ma[:], c_ma[:, :, :], w=[t_tab])
            P.dma("sp", mb[:], c_mb[:, :, :], w=[t_tab])
            RB = [[sb(nsa, "RB%d%d" % (g, k), [128, 4, 128], BF16) for k in range(2)] for g in range(2)]
            RW = [[sb(nsa, "RW%d%d" % (g, k), [128, 4, 128], BF16) for k in range(2)] for g in range(2)]
            t_RB = [[Trk() for k in range(2)] for g in range(2)]
            t_RW = [[Trk() for k in range(2)] for g in range(2)]
            for g in range(2):
                for k in range(2):
                    P.op("dve", lambda e, g=g, k=k: e.memset(RB[g][k][:, :, :], 0.0), w=[t_RB[g][k]])
                    P.op("dve", lambda e, g=g, k=k: e.memset(RW[g][k][:, :, :], 0.0), w=[t_RW[g][k]])
            Bc = [sb(nsa, "Bc%d" % k, [128, 2, 2, 4, 128], F32) for k in range(2)]
            t_Bc = trks(2)
            ONSA = [sb(nsa, "ONSA%d" % k, [128, 512], F32) for k in range(2)]
            t_ONSA = trks(2)
            sm = [sb(nsa, "sm%d" % k, [128, 32], F32) for k in range(4)]
            t_sm = trks(4)
            impn = [sb(nsa, "impn%d" % k, [128, 64], F32) for k in range(2)]
            impw = [sb(nsa, "impw%d" % k, [128, 64], F32) for k in range(2)]
            t_imp = trks(2)
            srr = [0]
            srot = [0]

            def s_tile(src_fn, kind, g, i, c):
                qs = slice(i * 128, (i + 1) * 128)
                rb = RB[g][i % 2]
                t_rb = t_RB[g][i % 2]
                rw = RW[g][i % 2]
                t_rw = t_RW[g][i % 2]
                b = (0, 1, 6)[srot[0] % 3]
                srot[0] += 1
                pv = v4(PS[b][:, 0:512])
                if kind == "cmp":
                    P.mm(pv, KC[:, c * 128:(c + 1) * 128], QN[g][:, :, qs], True, True,
                         r=[t_KC, t_QN[i]], w=[tPS[b]])
                    addt = Bc[i % 2][:, c, g, :, :]
                    t_add = t_Bc[i % 2]
                else:
                    KX, t_KX = (KS, t_KS) if kind == "sel" else (KW, t_KW)
                    P.mm(pv, KX[:, c * 128:(c + 1) * 128], QN[g][:, :, qs], True, False,
                         r=[t_KX[c], t_QN[i]], w=[tPS[b]])
                    if kind == "sel":
                        P.mm(pv, ea[:, c * 128:(c + 1) * 128], rb[:, :, :], False, True,
                             r=[t_tab, t_rb], w=[tPS[b]])
                    else:
                        P.mm(pv, ea[:, c * 128:(c + 1) * 128], rw[:, :, :], False, True,
                             r=[t_tab, t_rw], w=[tPS[b]])
                    qb = QB0 + i
                    addt, t_add = None, t_tab
                    if c == qb:
                        addt = caus[:, :, :]
                    elif kind == "win" and c == qb - 4:
                        addt = acaus[:, :, :]
                k = next_pt()
                if addt is not None:
                    m = next_tmp()
                    P.tt(v4(TMP[m][:, :]), pv, addt, ALU.add, r=[tPS[b], t_add], w=[t_TMP[m]])
                    P.act(PT[k][:, :], TMP[m][:, :], AF.Exp, r=[t_TMP[m]], w=[t_PT[k]])
                else:
                    P.act(PT[k][:, :], PS[b][:, 0:512], AF.Exp, r=[tPS[b]], w=[t_PT[k]])
                return k

            def combine(i, g, bi, o_fn, d_fn, t_acc, first):
                k = srr[0] % 4
                srr[0] += 1
                s_ = sm[k]
                t_s = t_sm[k]
                for r_ in range(4):
                    P.ts(s_[:, r_:r_ + 1], d_fn(r_), 1e-30, ALU.max, r=t_acc, w=[t_s])
                P.op("dve", lambda e: e.reciprocal(out=s_[:, 4:8], in_=s_[:, 0:4]), r=[t_s], w=[t_s])
                P.tt(s_[:, 8:12], s_[:, 4:8], GA[:, i, g * 12 + bi:g * 12 + 12:3], ALU.mult,
                     r=[t_s, t_GA[i]], w=[t_s])
                on = ONSA[i % 2]
                for r_ in range(4):
                    h0 = (4 * g + r_) * 64
                    if first:
                        P.ts(on[:, h0:h0 + 64], o_fn(r_), s_[:, 8 + r_:9 + r_], ALU.mult,
                             r=t_acc + [t_s], w=[t_ONSA[i % 2]])
                    else:
                        P.stt(on[:, h0:h0 + 64], o_fn(r_), s_[:, 8 + r_:9 + r_], on[:, h0:h0 + 64], ALU.mult, ALU.add,
                              r=t_acc + [t_s], w=[t_ONSA[i % 2]])
                return s_, t_s

            PIPE = 2
            CMPB = {0: (4, 5), 1: (2, 3)}
            WINB = {0: 4, 1: 5}
            SELB = {0: 2, 1: 3}

            def topk_chain(i, g, s_, t_s):
                ba, bb_ = CMPB[g]
                t_acc = [tPS[ba], tPS[bb_]]
                im, iw, t_im = impn[g], impw[g], t_imp[g]
                for r_ in range(4):
                    src = PS[CMPB[g][r_ // 2]][:, (r_ % 2) * 130 + 66:(r_ % 2) * 130 + 130]
                    if r_ == 0:
                        P.ts(im[:, :], src, s_[:, 4:5], ALU.mult, r=t_acc + [t_s], w=[t_im])
                    else:
                        P.stt(im[:, :], src, s_[:, 4 + r_:5 + r_], im[:, :], ALU.mult, ALU.add,
                              r=t_acc + [t_s], w=[t_im])
                P.tt(im[:, :], im[:, :], ma[:, i, :], ALU.mult, r=[t_im, t_tab], w=[t_im])
                P.tt(im[:, :], im[:, :], mb[:, i, :], ALU.add, r=[t_im, t_tab], w=[t_im])
                P.op("dve", lambda e: e.max(out=s_[:, 16:24], in_=im[:, :]), r=[t_im], w=[t_s])
                P.op("dve", lambda e: e.match_replace(out=iw[:, :], in_to_replace=s_[:, 16:24], in_values=im[:, :],
                                                      imm_value=-1.0e9), r=[t_im, t_s], w=[t_im])
                P.op("dve", lambda e: e.max(out=s_[:, 24:32], in_=iw[:, :]), r=[t_im], w=[t_s])
                P.ts(s_[:, 12:13], s_[:, 31:32], 0.0, ALU.max, r=[t_s], w=[t_s])
                P.ts(iw[:, :], im[:, :], s_[:, 12:13], ALU.is_ge, r=[t_im, t_s], w=[t_im])
                P.ts(iw[:, :], iw[:, :], -1.0, ALU.add, -NEG, ALU.mult, r=[t_im], w=[t_im])

            def sel_rows(i, g):
                rb, t_rb = RB[g][i % 2], t_RB[g][i % 2]
                P.tr(PS[6][0:64, 0:128], impw[g][:, 0:64], ident[:, :], r=[t_imp[g], t_const], w=[tPS[6]])
                P.copy("act", rb[0:64, :, :], PS[6][0:64, 0:128].unsqueeze(1).broadcast_to([64, 4, 128]),
                       r=[tPS[6]], w=[t_rb])

            def part_b(i, job, k):
                kind, g, c = job
                qb = QB0 + i
                if kind == "cmp":
                    for r_ in range(4):
                        bb = CMPB[g][r_ // 2]
                        P.mm(PS[bb][:, (r_ % 2) * 130:(r_ % 2) * 130 + 130], PT[k][:, r_ * 128:(r_ + 1) * 128],
                             VCX[:, c, g, :], start=(c == 0 and r_ % 2 == 0), stop=(c == 1),
                             r=[t_PT[k], t_VCX], w=[tPS[bb]], skip=True)
                    if c == 1:
                        t_acc = [tPS[CMPB[g][0]], tPS[CMPB[g][1]]]
                        s_, t_s = combine(i, g, 0,
                                          lambda r_: PS[CMPB[g][r_ // 2]][:, (r_ % 2) * 130:(r_ % 2) * 130 + 64],
                                          lambda r_: PS[CMPB[g][r_ // 2]][:, (r_ % 2) * 130 + 64:(r_ % 2) * 130 + 65],
                                          t_acc, True)
                        topk_chain(i, g, s_, t_s)
                elif kind == "win":
                    bw = WINB[g]
                    for r_ in range(4):
                        P.mm(PS[bw][:, r_ * 65:(r_ + 1) * 65], PT[k][:, r_ * 128:(r_ + 1) * 128], VW[:, c, g, 0:65],
                             start=(c == qb - 4 and r_ == 0), stop=(c == qb), r=[t_PT[k], t_VW[c]], w=[tPS[bw]],
                             skip=True)
                    if c == qb:
                        combine(i, g, 2, lambda r_: PS[bw][:, r_ * 65:r_ * 65 + 64],
                                lambda r_: PS[bw][:, r_ * 65 + 64:r_ * 65 + 65], [tPS[bw]], False)
                else:
                    bs = SELB[g]
                    for r_ in range(4):
                        P.mm(PS[bs][:, r_ * 65:(r_ + 1) * 65], PT[k][:, r_ * 128:(r_ + 1) * 128], VS[:, c, g, 0:65],
                             start=(c == 0 and r_ == 0), stop=(c == qb), r=[t_PT[k], t_VS[c]], w=[tPS[bs]],
                             skip=True)
                    if c == qb:
                        combine(i, g, 1, lambda r_: PS[bs][:, r_ * 65:r_ * 65 + 64],
                                lambda r_: PS[bs][:, r_ * 65 + 64:r_ * 65 + 65], [tPS[bs]], False)

            for i in range(NQB):
                qb = QB0 + i
                P.dma("sp", Bc[i % 2][:], c_bcmp[i], w=[t_Bc[i % 2]])
                for g in range(2):
                    P.dma("sp", RB[g][i % 2][64:68, :, :], c_rbc[:, i, g, :, :], w=[t_RB[g][i % 2]])
                    P.dma("sp", RW[g][i % 2][64:68, :, :], c_rbc[:, i, g, :, :], w=[t_RW[g][i % 2]])
                jobs = []
                for g in range(2):
                    for kc in range(2):
                        jobs.append(("cmp", g, kc))
                for g in range(2):
                    for dl in range(4, -1, -1):
                        jobs.append(("win", g, qb - dl))
                for g in range(2):
                    for c in range(qb + 1):
                        jobs.append(("sel", g, c))
                pend = []
                for job in jobs + [None, None]:
                    if job is not None:
                        kind, g, c = job
                        if kind == "sel" and c == 0:
                            sel_rows(i, g)
                        pend.append((job, s_tile(None, kind, g, i, c)))
                    if len(pend) > PIPE or (job is None and pend):
                        pj, pk = pend.pop(0)
                        part_b(i, pj, pk)
                on = ONSA[i % 2]
                for fc in range(4):
                    P.tr(PS[6][:, fc * 128:(fc + 1) * 128], on[:, fc * 128:(fc + 1) * 128], ident[:, :],
                         r=[t_ONSA[i % 2], t_const], w=[tPS[6]])
                P.copy("act", OAT[:, :, i * 128:(i + 1) * 128], v4(PS[6][:, 0:512]), r=[tPS[6]], w=[t_OAT[i]])
            P.barrier()
        if stage == 3:
            dump("OAT", OAT[:, :, :], t_OAT)
            return done()

        OBT = sb(PA, "OBT", [128, 4, NQ], BF16)
        t_OBT = trks(NQB)
        TQ0 = 1536
        with contextlib.ExitStack() as dl_:
            accD = sb(dl_, "accD", [128, 4, NQ], F32)
            t_acc = trks(NQB)
            hq = [sb(dl_, "hq%d" % k, [128, 8, 1024], BF16) for k in range(2)]
            t_hq = trks(2)
            wd = sb(dl_, "wd", [128, 8, 768], BF16)
            t_wd = Trk()
            QD = sb(dl_, "QD", [128, 2, T - TQ0], BF16)
            KD = sb(dl_, "KD", [128, 2, T], BF16)
            VT = sb(dl_, "VT", [128, 2, T], BF16)
            VD = sb(dl_, "VD", [128, NCH, 4, 66], BF16)
            BD = sb(dl_, "BD", [128, 2, 4, 128], F32)
            vdd = sb(dl_, "vdd", [128, 3, 32], BF16)
            t_QD, t_KD, t_VT, t_VD, t_BD, t_vdd = Trk(), Trk(), Trk(), Trk(), Trk(), Trk()
            P.dma("sp", vdd[:], c_validd[:, :, :], w=[t_vdd])
            hqrr = [0]
            for gd, (w_, r_) in enumerate(DIL):
                J = T // r_
                JQ = (T - TQ0) // r_
                jq0 = TQ0 // r_
                npc = J // 128
                c0w = 1304 + gd * 256
                load_wt(wd, t_wd, w_in, [(c0w, 256), (c0w + 768, 256), (c0w + 1536, 256)])
                P.dma("sp", BD[:], c_bdil[:, gd, :, :, :], w=[t_BD])
                P.copy("dve", VD[:, :, :, 64], vdd[:, gd, :].unsqueeze(2).broadcast_to([128, NCH, 4]),
                       r=[t_vdd], w=[t_VD])
                KDv = KD[:, :, :].rearrange("p a (rho j) -> p a rho j", rho=r_)
                VTv = VT[:, :, :].rearrange("p a (rho j) -> p a rho j", rho=r_)
                QDv = QD[:, :, :].rearrange("p a (rho j) -> p a rho j", rho=r_)
                for qt in range(4):
                    k = hqrr[0] % 2
                    hqrr[0] += 1
                    P.dma("sp", hq[k][:, :, :], hT_d[:, :, qt * 1024:(qt + 1) * 1024], r=t_hTd, w=[t_hq[k]])
                    for n0 in range(0, 1024, 512):
                        g0 = qt * 1024 + n0
                        for which in range(3):
                            if which == 0 and g0 < TQ0:
                                continue
                            for pr in range(2):
                                b = nextps(0, 4)
                                for dm in range(8):
                                    P.mm(PS[b][:, 0:512], wd[:, dm, which * 256 + pr * 128:which * 256 + pr * 128 + 128],
                                         hq[k][:, dm, n0:n0 + 512], start=(dm == 0), stop=(dm == 7),
                                         r=[t_wd, t_hq[k]], w=[tPS[b]])
                                src = PS[b][:, 0:512].rearrange("p (j rho) -> p rho j", rho=r_)
                                nj = 512 // r_
                                if which == 0:
                                    j0 = (g0 - TQ0) // r_
                                    P.copy(evac_eng(), QDv[:, pr, :, j0:j0 + nj], src, r=[tPS[b]], w=[t_QD],
                                           scale=0.125)
                                elif which == 1:
                                    j0 = g0 // r_
                                    P.copy(evac_eng(), KDv[:, pr, :, j0:j0 + nj], src, r=[tPS[b]], w=[t_KD])
                                else:
                                    j0 = g0 // r_
                                    P.copy(evac_eng(), VTv[:, pr, :, j0:j0 + nj], src, r=[tPS[b]], w=[t_VT])
                for ci in range(NCH):
                    for pr in range(2):
                        P.tr(PST[:, pr * 128:(pr + 1) * 128], VT[:, pr, ci * 128:(ci + 1) * 128], identb[:],
                             r=[t_VT, t_const], w=[tPST])
                    P.copy(evac_eng(), VD[:, ci, :, 0:64], PST[:, 0:256].rearrange("p (h d) -> p h d", h=4),
                           r=[tPST], w=[t_VD])
                for rho in range(r_):
                    for jb in range(npc):
                        jmin = -(-(Q0 - rho) // r_)
                        q_lo = max(0, jmin - 128 * jb)
                        if q_lo >= 128:
                            continue
                        nq = 128 - q_lo
                        dls = [dl for dl in (0, 1) if jb - dl >= 0]
                        pts = {}
                        for dl in dls:
                            jc = jb - dl
                            kk = next_pt()
                            pts[dl] = kk
                            for hh in range(4):
                                par, pr = hh % 2, hh // 2
                                base = 64 * par
                                kc0 = rho * J + jc * 128
                                qc0 = rho * JQ + (jb * 128 + q_lo - jq0)
                                P.mm(PS[par][:, pr * 128 + q_lo:pr * 128 + 128],
                                     KD[base:base + 64, pr, kc0:kc0 + 128], QD[base:base + 64, pr, qc0:qc0 + nq],
                                     True, True, r=[t_KD, t_QD], w=[tPS[par]])
                            for par in range(2):
                                m = next_tmp()
                                tv = TMP[m][:, 0:256].rearrange("p (a b) -> p a b", a=2)[:, :, q_lo:128]
                                pv_ = PS[par][:, 0:256].rearrange("p (a b) -> p a b", a=2)[:, :, q_lo:128]
                                P.tt(tv, pv_, BD[:, dl, par::2, q_lo:128], ALU.add, r=[tPS[par], t_BD], w=[t_TMP[m]])
                                P.act(v4(PT[kk][:, :])[:, par::2, q_lo:128], tv, AF.Exp, r=[t_TMP[m]], w=[t_PT[kk]])
                        ab = 2 + ((rho * npc + jb) % 2)
                        for hh in range(4):
                            for n_, dl in enumerate(dls):
                                ci = rho * npc + (jb - dl)
                                P.mm(PS[ab][0:65, hh * 128 + q_lo:hh * 128 + 128], VD[:, ci, hh, 0:65],
                                     PT[pts[dl]][:, hh * 128 + q_lo:hh * 128 + 128],
                                     start=(n_ == 0), stop=(n_ == len(dls) - 1),
                                     r=[t_PT[pts[dl]], t_VD], w=[tPS[ab]])
                        tok0 = rho + r_ * (128 * jb + q_lo) - Q0
                        tok1 = rho + r_ * (128 * jb + 127) - Q0
                        blks = [t_acc[c] for c in range(tok0 // 128, tok1 // 128 + 1)]
                        dst = accD[0:65, :, tok0:tok1 + 1:r_]
                        srcp = v4(PS[ab][0:65, 0:512])[:, :, q_lo:128]
                        if gd == 0:
                            P.copy("dve", dst, srcp, r=[tPS[ab]], w=blks)
                        else:
                            P.tt(dst, srcp, dst, ALU.add, r=[tPS[ab]] + blks, w=blks)
            s65 = sb(dl_, "s65", [128, 64], F32)
            t_s65 = Trk()
            P.dma("sp", s65[0:65, :], c_s65[:, :], w=[t_s65])
            for hh in range(4):
                n0 = 0
                while n0 < NQ:
                    n = min(512, NQ - n0)
                    b = nextps(4, 6)
                    tb = [t_acc[c] for c in chunks_of(n0, n)]
                    P.mm(PS[b][0:64, 0:n], s65[0:65, 0:64], accD[0:65, hh, n0:n0 + n], True, True,
                         r=[t_s65] + tb, w=[tPS[b]])
                    m = next_tmp()
                    P.ts(TMP[m][0:64, 0:n], PS[b][0:64, 0:n], 1e-30, ALU.max, r=[tPS[b]], w=[t_TMP[m]])
                    P.op("dve", lambda e, m=m, n=n: e.reciprocal(out=TMP[m][0:64, 0:n], in_=TMP[m][0:64, 0:n]),
                         r=[t_TMP[m]], w=[t_TMP[m]])
                    P.tt(OBT[0:64, hh, n0:n0 + n], accD[0:64, hh, n0:n0 + n], TMP[m][0:64, 0:n], ALU.mult,
                         r=tb + [t_TMP[m]], w=[t_OBT[c] for c in chunks_of(n0, n)])
                    n0 += n
            P.barrier()
        if stage == 4:
            dump("OBT", OBT[0:64, :, :], t_OBT)
            return done()

        with contextlib.ExitStack() as mg:
            WPA = sb(mg, "WPA", [128, 4, D], BF16)
            WPB = sb(mg, "WPB", [128, 4, D], BF16)
            WO = sb(mg, "WO", [128, 8, D], BF16)
            MX = sb(mg, "MX", [128, 8, NQ], BF16)
            hTq = sb(mg, "hTq", [128, 8, NQ], BF16)
            wm = [sb(mg, "wm%d" % k, [128, 8, 256], BF16) for k in range(2)]
            xr = [sb(mg, "xr%d" % k, [128, D], F32) for k in range(2)]
            t_WPA, t_WPB, t_WO, t_hTq = Trk(), Trk(), Trk(), Trk()
            t_MX = trks(NQB)
            t_wm = trks(2)
            t_xr = trks(2)
            P.dma("pool", WPA[:, :, :], w_pa.rearrange("(c p) n -> p c n", p=128), w=[t_WPA])
            P.dma("pool", WPB[0:64, :, :], w_pb.rearrange("(h d) n -> d h n", d=64), w=[t_WPB])
            P.dma_multi("pool", [(WO[:, 0:4, :], w_out[0:512, :].rearrange("(c p) n -> p c n", p=128)),
                                 (WO[:, 4:8, :], w_out[512:1024, :].rearrange("(c p) n -> p c n", p=128))],
                        w=[t_WO])
            P.dma("sp", hTq[:, :, :], hT_d[:, :, Q0:T], r=t_hTd, w=[t_hTq])
            ntiles = []
            n0 = 0
            while n0 < NQ:
                n = min(512, NQ - n0)
                ntiles.append((n0, n))
                n0 += n
            for mc in range(8):
                k = mc % 2
                load_wt(wm[k], t_wm[k], w_in, [(3608 + mc * 128, 128), (4632 + mc * 128, 128)])
                for (n0, n) in ntiles:
                    tb = chunks_of(n0, n)
                    b1, b2_, b3, b4 = [nextps(0, 7) for _ in range(4)]
                    for fc in range(4):
                        P.mm(PS[b1][:, 0:n], WPA[:, fc, mc * 128:(mc + 1) * 128], OAT[:, fc, n0:n0 + n],
                             fc == 0, fc == 3, r=[t_WPA] + [t_OAT[c] for c in tb], w=[tPS[b1]])
                    for hh in range(4):
                        P.mm(PS[b2_][:, 0:n], WPB[0:64, hh, mc * 128:(mc + 1) * 128], OBT[0:64, hh, n0:n0 + n],
                             hh == 0, hh == 3, r=[t_WPB] + [t_OBT[c] for c in tb], w=[tPS[b2_]])
                    for dm in range(8):
                        P.mm(PS[b3][:, 0:n], wm[k][:, dm, 0:128], hTq[:, dm, n0:n0 + n], dm == 0, dm == 7,
                             r=[t_wm[k], t_hTq], w=[tPS[b3]])
                    for dm in range(8):
                        P.mm(PS[b4][:, 0:n], wm[k][:, dm, 128:256], hTq[:, dm, n0:n0 + n], dm == 0, dm == 7,
                             r=[t_wm[k], t_hTq], w=[tPS[b4]])
                    ma_, mb_ = next_tmp(), next_tmp()
                    P.act(TMP[ma_][:, 0:n], PS[b3][:, 0:n], AF.Sigmoid, r=[tPS[b3]], w=[t_TMP[ma_]])
                    P.act(TMP[mb_][:, 0:n], PS[b4][:, 0:n], AF.Sigmoid, r=[tPS[b4]], w=[t_TMP[mb_]])
                    P.tt(TMP[ma_][:, 0:n], TMP[ma_][:, 0:n], PS[b1][:, 0:n], ALU.mult, r=[t_TMP[ma_], tPS[b1]],
                         w=[t_TMP[ma_]])
                    P.tt(TMP[mb_][:, 0:n], TMP[mb_][:, 0:n], PS[b2_][:, 0:n], ALU.mult, r=[t_TMP[mb_], tPS[b2_]],
                         w=[t_TMP[mb_]])
                    P.tt(MX[:, mc, n0:n0 + n], TMP[ma_][:, 0:n], TMP[mb_][:, 0:n], ALU.add,
                         r=[t_TMP[ma_], t_TMP[mb_]], w=[t_MX[c] for c in tb])
            for i in range(NQB):
                k = i % 2
                P.dma("sp", xr[k][:, :], x[Q0 + i * 128:Q0 + (i + 1) * 128, :], w=[t_xr[k]])
                for half in range(2):
                    b = nextps(0, 7)
                    for mc in range(8):
                        P.mm(PS[b][:, 0:512], MX[:, mc, i * 128:(i + 1) * 128], WO[:, mc, half * 512:(half + 1) * 512],
                             mc == 0, mc == 7, r=[t_MX[i], t_WO], w=[tPS[b]])
                    P.tt(xr[k][:, half * 512:(half + 1) * 512], PS[b][:, 0:512], xr[k][:, half * 512:(half + 1) * 512],
                         ALU.add, r=[tPS[b], t_xr[k]], w=[t_xr[k]])
                P.dma("sp", xm_d[i * 128:(i + 1) * 128, :], xr[k][:, :], r=[t_xr[k]], w=[t_xmd[i]])
            P.barrier()
        if stage == 5:
            return done()
    P.barrier()

    with contextlib.ExitStack() as FF:
        H2T = sb(FF, "H2T", [128, 8, 2050], BF16)
        t_H2T = trks(NQB)
        gfr = sb(FF, "gfr", [128, D], F32)
        gfin = sb(FF, "gfin", [128, D], F32)
        t_g = Trk()
        P.dma("sp", gfr[:], g_ffn[0:1, :].partition_broadcast(128), w=[t_g])
        P.dma("sp", gfin[:], g_fin[0:1, :].partition_broadcast(128), w=[t_g])
        cwr = sb(FF, "cwr", [128, 4, 128], F32)
        cw = sb(FF, "cw", [128, 4, 22], F32)
        t_cw = Trk()
        P.dma("sp", cwr[0:22, 0:3, :], conv_w.rearrange("k (j p) -> j k p", p=128), w=[t_cw])
        P.dma("sp", cwr[0:22, 3, :], conv_b.rearrange("o (j p) -> (o j) p", p=128), w=[t_cw])
        for kk in range(4):
            b = nextps(0, 7)
            P.tr(PS[b][:, 0:22], cwr[0:22, kk, :], ident[0:22, 0:22], r=[t_cw, t_const], w=[tPS[b]])
            P.copy("dve", cw[:, kk, :], PS[b][:, 0:22], r=[tPS[b]], w=[t_cw])
        xt2 = [sb(FF, "fx%d" % i, [128, D], F32) for i in range(2)]
        t_xt = trks(2)
        xn2 = [sb(FF, "fn%d" % i, [128, D], BF16) for i in range(2)]
        t_xn = trks(2)
        yo2 = [sb(FF, "yo%d" % i, [128, D], F32) for i in range(2)]
        t_yo = trks(2)
        junk = sb(FF, "fjunk", [128, D], F32)
        ss2 = [sb(FF, "fss%d" % i, [128, 4], F32) for i in range(2)]
        t_ss = trks(2)
        for i in range(NQB):
            k = i % 2
            P.dma("sp", xt2[k][:], xm_d[i * 128:(i + 1) * 128, :], r=[t_xmd[i]], w=[t_xt[k]])
            rmsnorm_tile(junk, ss2[k], t_ss[k], xt2[k][:], t_xt[k], gfr[:], t_g, xn2[k][:], t_xn[k])
            for c in range(8):
                P.tr(PST[:, c * 128:(c + 1) * 128], xn2[k][:, c * 128:(c + 1) * 128], identb[:],
                     r=[t_xn[k], t_const], w=[tPST])
            pv8 = PST[:, :].rearrange("p (c n) -> p c n", c=8)
            if i == 0:
                P.ts(H2T[:, :, 0:2], pv8[:, :, 126:128], halo[:, 0:1], ALU.mult, r=[tPST, t_const], w=[t_H2T[0]])
            else:
                P.copy(evac_eng(), H2T[:, :, 2 + (i - 1) * 128:2 + i * 128], pv8, r=[tPST], w=[t_H2T[i]])
        if stage == 6:
            dump("H2T", H2T[:, :, 0:258], t_H2T)
            dump("cw", cw[:, :, :], [t_cw])
            return done()
        WD = sb(FF, "WD", [128, NFF, D], BF16)
        t_WD = Trk()
        wdv = w_down.rearrange("(j p) n -> p j n", p=128)
        P.dma_multi("pool", [(WD[:, j0:min(j0 + 6, NFF), :], wdv[:, j0:min(j0 + 6, NFF), :]) for j0 in range(0, NFF, 6)],
                    w=[t_WD])
        AT = sb(FF, "AT", [128, NFF, 1024], BF16)
        t_AT = trks(NFF)
        wu = [sb(FF, "wu%d" % k, [128, 8, 256], BF16) for k in range(2)]
        t_wu = trks(2)
        U = [sb(FF, "U%d" % k, [128, 1026], F32) for k in range(2)]
        t_U = trks(2)
        C1 = sb(FF, "C1", [128, 1024], F32)
        C2 = sb(FF, "C2", [128, 1024], F32)
        t_C1, t_C2 = Trk(), Trk()
        FG = [sb(FF, "FG%d" % k, [128, 512], F32) for k in range(2)]
        t_FG = trks(2)

        def gelu2(out, t_out, xin, t_x, n):
            a, b2 = FG[0][:, 0:n], FG[1][:, 0:n]
            P.tt(a, xin, xin, ALU.mult, r=[t_x], w=[t_FG[0]])
            P.ts(a, a, 0.044715, ALU.mult, 1.0, ALU.add, r=[t_FG[0]], w=[t_FG[0]])
            P.tt(a, a, xin, ALU.mult, r=[t_FG[0], t_x], w=[t_FG[0]])
            P.act(b2, a, AF.Sigmoid, r=[t_FG[0]], w=[t_FG[1]], scale=1.5957691216057308)
            P.tt(out, b2, xin, ALU.mult, r=[t_FG[1], t_x], w=t_out)

        jrr = [0]
        for th in range(2):
            base = 2 + th * 1024
            hts = [t_H2T[c] for c in range(max(0, th * 8), th * 8 + 9)]
            for j in range(NFF):
                k = jrr[0] % 2
                jrr[0] += 1
                P.dma_multi("pool", [(wu[k][:, :, 0:128], w_up[:, j * 128:(j + 1) * 128].rearrange("(c p) n -> p c n", p=128)),
                                     (wu[k][:, :, 128:256],
                                      w_up[:, DFF + j * 128:DFF + (j + 1) * 128].rearrange("(c p) n -> p c n", p=128))],
                            w=[t_wu[k]])
                b = nextps(0, 7)
                for dm in range(8):
                    P.mm(PS[b][:, 0:2], wu[k][:, dm, 0:128], H2T[:, dm, base - 2:base], dm == 0, dm == 7,
                         r=[t_wu[k]] + hts, w=[tPS[b]])
                P.copy("dve", U[k][:, 0:2], PS[b][:, 0:2], r=[tPS[b]], w=[t_U[k]])
                for nt in range(2):
                    b = nextps(0, 7)
                    for dm in range(8):
                        P.mm(PS[b][:, 0:512], wu[k][:, dm, 0:128], H2T[:, dm, base + nt * 512:base + (nt + 1) * 512],
                             dm == 0, dm == 7, r=[t_wu[k]] + hts, w=[tPS[b]])
                    P.copy("act", U[k][:, 2 + nt * 512:2 + (nt + 1) * 512], PS[b][:, 0:512], r=[tPS[b]], w=[t_U[k]])
                bg = []
                for nt in range(2):
                    b = nextps(0, 7)
                    bg.append(b)
                    for dm in range(8):
                        P.mm(PS[b][:, 0:512], wu[k][:, dm, 128:256], H2T[:, dm, base + nt * 512:base + (nt + 1) * 512],
                             dm == 0, dm == 7, r=[t_wu[k]] + hts, w=[tPS[b]])
                P.ts(C1[:, :], U[k][:, 2:1026], cw[:, 2, j:j + 1], ALU.mult, cw[:, 3, j:j + 1], ALU.add,
                     r=[t_U[k], t_cw], w=[t_C1])
                P.stt(C1[:, :], U[k][:, 1:1025], cw[:, 1, j:j + 1], C1[:, :], ALU.mult, ALU.add,
                      r=[t_U[k], t_cw, t_C1], w=[t_C1])
                P.stt(C1[:, :], U[k][:, 0:1024], cw[:, 0, j:j + 1], C1[:, :], ALU.mult, ALU.add,
                      r=[t_U[k], t_cw, t_C1], w=[t_C1])
                for nt in range(2):
                    sl = slice(nt * 512, (nt + 1) * 512)
                    gelu2(C2[:, sl], [t_C2], C1[:, sl], t_C1, 512)
                    P.tt(AT[:, j, sl], C2[:, sl], PS[bg[nt]][:, 0:512], ALU.mult, r=[t_C2, tPS[bg[nt]]], w=[t_AT[j]])
            if stage == 7:
                dump("AT", AT[:, 0:2, :], t_AT)
                return done()
            for kt in range(8):
                i = 1 + th * 8 + kt
                k = kt % 2
                P.dma("sp", xt2[k][:], xm_d[i * 128:(i + 1) * 128, :], r=[t_xmd[i]], w=[t_xt[k]])
                for half in range(2):
                    b = nextps(0, 7)
                    for j in range(NFF):
                        P.mm(PS[b][:, 0:512], AT[:, j, kt * 128:(kt + 1) * 128], WD[:, j, half * 512:(half + 1) * 512],
                             j == 0, j == NFF - 1, r=[t_AT[j], t_WD], w=[tPS[b]])
                    P.tt(xt2[k][:, half * 512:(half + 1) * 512], PS[b][:, 0:512], xt2[k][:, half * 512:(half + 1) * 512],
                         ALU.add, r=[tPS[b], t_xt[k]], w=[t_xt[k]])
                rmsnorm_tile(junk, ss2[k], t_ss[k], xt2[k][:], t_xt[k], gfin[:], t_g, yo2[k][:], t_yo[k])
                P.dma("sp", y[(th * 8 + kt) * 128:(th * 8 + kt + 1) * 128, :], yo2[k][:], r=[t_yo[k]])
    return done()


W_NAMES = ["g_mix", "w_in", "pe_cmp_k", "w_cmp_k1", "w_cmp_k2", "pe_cmp_v", "w_cmp_v1", "w_cmp_v2",
           "w_proj_nsa", "w_proj_dil", "w_out", "g_ffn", "w_up", "conv_w", "conv_b", "w_down", "g_final"]


def make_in_maps(inputs, cores):
    x = np.asarray(inputs["x"], dtype=np.float32)
    shared = {}
    for n in W_NAMES:
        a = np.asarray(inputs[n], dtype=np.float32)
        if n == "g_final":
            a = a.reshape(1, D)
        elif a.shape[0] == 1:
            a = a[0]
        if a.ndim == 1:
            a = a.reshape(1, -1)
        shared[n] = np.ascontiguousarray(a)
    tabs = [const_tables(0), const_tables(1)]
    maps = []
    for (b, hf) in cores:
        m = dict(shared)
        if hf == 1:
            xl = x[b]
        else:
            xl = np.concatenate([np.zeros((2048, D), np.float32), x[b, :2048]], axis=0)
        m["x"] = np.ascontiguousarray(xl)
        for k, v in tabs[hf].items():
            m["c_" + k] = v
        maps.append(m)
    return maps


_NC_CACHE = {}


N_LAUNCH = 1


def kernel(**inputs):
    if "nc" not in _NC_CACHE:
        _NC_CACHE["nc"] = build()
    nc = _NC_CACHE["nc"]
    cores = [(b, hf) for b in range(4) for hf in range(2)]
    out = np.zeros((4, T, D), np.float32)
    per = len(cores) // N_LAUNCH
    for li in range(N_LAUNCH):
        cs = cores[li * per:(li + 1) * per]
        maps = make_in_maps(inputs, cs)
        res = run_bass_kernel_spmd(nc, maps, core_ids=list(range(len(cs))))
        for ci, (b, hf) in enumerate(cs):
            out[b, hf * 2048:(hf + 1) * 2048, :] = res.results[ci]["y"]
    return out
```

```python
import contextlib
import numpy as np
import ml_dtypes
import concourse.bass as bass
import concourse.mybir as mybir
from concourse.bass_utils import run_bass_kernel_spmd

F32 = mybir.dt.float32
BF16 = mybir.dt.bfloat16
AF = mybir.ActivationFunctionType
ALU = mybir.AluOpType
AX = mybir.AxisListType

D = 1024
T = 4096
NCH = 32
QB0 = 15
NQB = 17
NQ = NQB * 128
Q0 = QB0 * 128
DFF = 2816
NFF = 22
DIN = 5656
NEG = -30000.0
DIL = ((128, 1), (512, 4), (2048, 16))
NDS = 48
NDS_HW = 32
SW_LIMIT = 300


class Trk:
    __slots__ = ("w", "r", "excl")

    def __init__(self, excl=False):
        self.w = []
        self.r = []
        self.excl = excl


def trks_ex(n):
    return [Trk(True) for _ in range(n)]


def trks(n):
    return [Trk() for _ in range(n)]


class Prog:
    def __init__(self):
        self.nc = bass.Bass("TRN2", target_bir_lowering=False)
        nc = self.nc
        self.es = contextlib.ExitStack()
        self.E = {"pe": nc.tensor, "act": nc.scalar, "dve": nc.vector, "pool": nc.gpsimd, "sp": nc.sync}
        self.sem = {k: self.es.enter_context(nc.semaphore("s_" + k)) for k in self.E}
        self.cnt = {k: 0 for k in self.E}
        self.seen = {k: {} for k in self.E}
        self.seen_seq = {k: {} for k in self.E}
        self.opseq = {k: 0 for k in self.E}
        self.last_ins = {k: None for k in self.E}
        self.sigmap = {k: [] for k in self.E}
        self.dsem = [self.es.enter_context(nc.semaphore("d%d" % i)) for i in range(NDS)]
        self.dcnt = [0] * NDS
        self.dnext = 0
        self.dnext_sw = NDS_HW
        self.sw_out = []
        self.ninst = 0

    def _resolve(self, key, seq):
        sm = self.sigmap[key]
        lo, hi = 0, len(sm)
        while lo < hi:
            mid = (lo + hi) // 2
            if sm[mid][0] >= seq:
                hi = mid
            else:
                lo = mid + 1
        if lo < len(sm):
            return sm[lo][1]
        self.last_ins[key].then_inc(self.sem[key], 1)
        self.cnt[key] += 1
        sm.append((self.opseq[key], self.cnt[key]))
        return self.cnt[key]

    def _wait(self, eng, dep):
        key, val = dep
        if key == "pe" and eng == "pe":
            return
        if isinstance(key, str):
            if self.seen_seq[eng].get(key, 0) >= val:
                return
            self.seen_seq[eng][key] = val
            val = self._resolve(key, val)
        if self.seen[eng].get(key, 0) >= val:
            return
        self.seen[eng][key] = val
        sem = self.sem[key] if isinstance(key, str) else self.dsem[key[1]]
        self.E[eng].wait_ge(sem, val)

    def _deps(self, eng, r, w):
        deps = set()
        for t in r:
            deps.update(t.w)
            if t.excl:
                deps.update(d for d in t.r if d[0] != eng)
        for t in w:
            deps.update(t.w)
            deps.update(t.r)
        best = {}
        for (k, v) in deps:
            if best.get(k, 0) < v:
                best[k] = v
        for k in sorted(best, key=str):
            self._wait(eng, (k, best[k]))

    def _mark(self, toks, r, w):
        for t in w:
            t.w = list(toks)
            t.r = []
        for t in r:
            if t not in w:
                t.r.extend(toks)

    def op(self, eng, fn, r=(), w=()):
        self._deps(eng, r, w)
        ins = fn(self.E[eng])
        self.opseq[eng] += 1
        self.last_ins[eng] = ins
        self.ninst += 1
        tok = (eng, self.opseq[eng])
        self._mark([tok], r, w)
        return tok

    def dma(self, q, out, in_, r=(), w=()):
        return self.dma_multi(q, [(out, in_)], r=r, w=w)

    def dma_multi(self, q, pieces, r=(), w=()):
        self._deps(q, r, w)
        toks = []
        for (out, in_) in pieces:
            if q == "pool":
                nd = 1
                for d_ in list(out.shape)[:-1]:
                    nd *= int(d_)
                nd = nd // 16 + 2
                while self.sw_out and sum(n_ for (_, n_) in self.sw_out) + nd > SW_LIMIT:
                    tok0, _ = self.sw_out.pop(0)
                    self._wait(q, tok0)
                j = self.dnext_sw
                self.dnext_sw = NDS_HW + (j + 1 - NDS_HW) % (NDS - NDS_HW)
            else:
                j = self.dnext
                self.dnext = (j + 1) % NDS_HW
            if self.dcnt[j] > 0:
                self._wait(q, (("d", j), 16 * self.dcnt[j]))
            self.dcnt[j] += 1
            self.E[q].dma_start(out=out, in_=in_).then_inc(self.dsem[j], 16)
            self.ninst += 1
            toks.append((("d", j), 16 * self.dcnt[j]))
            if q == "pool":
                self.sw_out.append((toks[-1], nd))
        self._mark(toks, r, w)
        return toks

    def barrier(self):
        for e in self.E:
            for e2 in self.E:
                if e2 != e and self.opseq[e2] > 0:
                    self._wait(e, (e2, self.opseq[e2]))
            for j in range(NDS):
                if self.dcnt[j] > 0:
                    self._wait(e, (("d", j), 16 * self.dcnt[j]))

    def mm(self, out, lhsT, rhs, start, stop, r=(), w=(), skip=False):
        if skip:
            return self.op("pe", lambda e: e.matmul(out, lhsT=lhsT, rhs=rhs, start=start, stop=stop,
                                                    skip_group_check=True), r=r, w=w)
        return self.op("pe", lambda e: e.matmul(out, lhsT=lhsT, rhs=rhs, start=start, stop=stop), r=r, w=w)

    def tr(self, out, in_, ident, r=(), w=()):
        return self.op("pe", lambda e: e.transpose(out, in_, ident), r=r, w=w)

    def act(self, out, in_, func, r=(), w=(), **kw):
        return self.op("act", lambda e: e.activation(out=out, in_=in_, func=func, **kw), r=r, w=w)

    def copy(self, eng, out, in_, r=(), w=(), scale=None):
        if eng == "act":
            if scale is None:
                return self.act(out, in_, AF.Copy, r=r, w=w)
            return self.act(out, in_, AF.Copy, r=r, w=w, scale=float(scale))
        if scale is None:
            return self.op(eng, lambda e: e.tensor_copy(out=out, in_=in_), r=r, w=w)
        return self.op(eng, lambda e: e.tensor_scalar(out=out, in0=in_, scalar1=float(scale), scalar2=None,
                                                      op0=ALU.mult), r=r, w=w)

    def tt(self, out, in0, in1, op, r=(), w=(), eng="dve"):
        return self.op(eng, lambda e: e.tensor_tensor(out=out, in0=in0, in1=in1, op=op), r=r, w=w)

    def ts(self, out, in0, s1, op0, s2=None, op1=None, r=(), w=(), eng="dve"):
        if op1 is None:
            return self.op(eng, lambda e: e.tensor_scalar(out=out, in0=in0, scalar1=s1, scalar2=None, op0=op0),
                           r=r, w=w)
        return self.op(eng, lambda e: e.tensor_scalar(out=out, in0=in0, scalar1=s1, scalar2=s2, op0=op0, op1=op1),
                       r=r, w=w)

    def stt(self, out, in0, scalar, in1, op0, op1, r=(), w=()):
        return self.op("dve", lambda e: e.scalar_tensor_tensor(out=out, in0=in0, scalar=scalar, in1=in1,
                                                               op0=op0, op1=op1), r=r, w=w)


def alibi(n):
    return (2.0 ** (-8.0 * np.arange(1, n + 1) / n)).astype(np.float64)


def bf(a):
    return np.asarray(a, dtype=np.float32).astype(ml_dtypes.bfloat16)


def const_tables(hf):
    c = {}
    c["ident"] = np.eye(128, dtype=np.float32)
    c["identb"] = bf(np.eye(128))
    pos = np.arange(T)
    gpos = pos if hf == 1 else pos - 2048
    valid = (gpos >= 0).astype(np.float32)
    c["validtm"] = bf(valid.reshape(NCH, 128).T)
    vd = np.zeros((128, 3, 32), np.float32)
    for g, (w, r) in enumerate(DIL):
        npc = 32 // r
        for rho in range(r):
            for jc in range(npc):
                tok = rho + r * (128 * jc + np.arange(128))
                vd[:, g, rho * npc + jc] = valid[tok]
    c["validd"] = bf(vd)
    c["halo"] = np.full((128, 1), 1.0 if hf == 1 else 0.0, np.float32)
    ea = np.zeros((128, T), np.float32)
    ea[pos // 64, pos] = 1.0
    ea[64] = 128.0 * (pos // 128)
    ea[65] = pos % 128
    ea[66] = 1.0
    ea[67] = 1.0
    c["ea"] = bf(ea)
    sl = alibi(8)
    rbc = np.zeros((4, NQB, 2, 4, 128), np.float32)
    ql = np.arange(128)
    for i in range(NQB):
        qb = QB0 + i
        for g in range(2):
            for r in range(4):
                s = sl[4 * g + r]
                rbc[0, i, g, r, :] = s
                rbc[1, i, g, r, :] = s
                rbc[2, i, g, r, :] = -s * 128.0 * qb
                rbc[3, i, g, r, :] = -s * ql
    c["rbc"] = bf(rbc)
    kk = np.arange(128)[:, None]
    qq = np.arange(128)[None, :]
    caus = np.where(kk <= qq, 0.0, NEG).astype(np.float32)
    acaus = np.where(kk > qq, 0.0, NEG).astype(np.float32)
    c["caus"] = np.ascontiguousarray(np.broadcast_to(caus[:, None, :], (128, 4, 128)))
    c["acaus"] = np.ascontiguousarray(np.broadcast_to(acaus[:, None, :], (128, 4, 128)))
    bc = np.zeros((NQB, 128, 2, 2, 4, 128), np.float32)
    for i in range(NQB):
        t = (QB0 + i) * 128 + np.arange(128)
        for kc in range(2):
            cc = kc * 128 + np.arange(128)
            cend = 16 * cc + 31
            cstart_g = 16 * cc - (0 if hf == 1 else 2048)
            dist = t[None, :] - cend[:, None]
            ok = (dist >= 0) & (cstart_g[:, None] >= 0) & (cc[:, None] < 255)
            for g in range(2):
                for r in range(4):
                    bc[i, :, kc, g, r, :] = np.where(ok, -sl[4 * g + r] * dist, NEG)
    c["bcmp"] = bc
    cc = np.arange(256)
    ss = np.arange(64)
    ov = np.clip(np.minimum(16 * cc[:, None] + 32, 64 * ss[None, :] + 64)
                 - np.maximum(16 * cc[:, None], 64 * ss[None, :]), 0, None) / 32.0
    c["ovl"] = bf(ov.reshape(2, 128, 64).transpose(1, 0, 2))
    ma = np.zeros((128, NQB, 64), np.float32)
    mb = np.zeros((128, NQB, 64), np.float32)
    b0 = 0 if hf == 1 else 32
    for i in range(NQB):
        t = (QB0 + i) * 128 + np.arange(128)
        cur = t // 64
        for s in range(64):
            forced = (s == cur) | (s == b0)
            future = (s > cur) | (s < b0)
            normal = (~forced) & (~future)
            ma[:, i, s] = normal
            mb[:, i, s] = np.where(future, -1.0, np.where(forced, 1.0e4, 0.0))
    c["ma"] = ma
    c["mb"] = mb
    sd = alibi(12).reshape(3, 4)
    bd = np.zeros((128, 3, 2, 4, 128), np.float32)
    for g, (w, r) in enumerate(DIL):
        for dl in range(2):
            dj = dl * 128 + qq - kk
            ok = (dj >= 0) & (dj <= 128)
            for h in range(4):
                bd[:, g, dl, h, :] = np.where(ok, -sd[g, h] * r * dj, NEG)
    c["bdil"] = bd
    s65 = np.zeros((65, 64), np.float32)
    s65[64, :] = 1.0
    c["s65"] = s65
    return c


def build(stage=99):
    P = Prog()
    with P.es:
        return _build(P, stage)


def _build(P, stage):
    nc = P.nc
    es = P.es

    def dram_in(name, shape, dt=F32):
        return nc.dram_tensor(name, list(shape), dt, kind="ExternalInput").ap()

    def sb(stack, name, shape, dt):
        return stack.enter_context(nc.sbuf_tensor(name, list(shape), dt))

    def ps(stack, name, shape, dt):
        return stack.enter_context(nc.psum_tensor(name, list(shape), dt))

    x = dram_in("x", [T, D])
    g_mix = dram_in("g_mix", [1, D])
    w_in = dram_in("w_in", [D, DIN])
    pe_k = dram_in("pe_cmp_k", [32, 64])
    w_k1 = dram_in("w_cmp_k1", [2048, 128])
    w_k2 = dram_in("w_cmp_k2", [128, 64])
    pe_v = dram_in("pe_cmp_v", [32, 64])
    w_v1 = dram_in("w_cmp_v1", [2048, 128])
    w_v2 = dram_in("w_cmp_v2", [128, 64])
    w_pa = dram_in("w_proj_nsa", [512, D])
    w_pb = dram_in("w_proj_dil", [256, D])
    w_out = dram_in("w_out", [D, D])
    g_ffn = dram_in("g_ffn", [1, D])
    w_up = dram_in("w_up", [D, 2 * DFF])
    conv_w = dram_in("conv_w", [3, DFF])
    conv_b = dram_in("conv_b", [1, DFF])
    w_down = dram_in("w_down", [DFF, D])
    g_fin = dram_in("g_final", [1, D])
    c_ident = dram_in("c_ident", [128, 128])
    c_identb = dram_in("c_identb", [128, 128], BF16)
    c_validtm = dram_in("c_validtm", [128, 32], BF16)
    c_validd = dram_in("c_validd", [128, 3, 32], BF16)
    c_halo = dram_in("c_halo", [128, 1])
    c_ea = dram_in("c_ea", [128, T], BF16)
    c_rbc = dram_in("c_rbc", [4, NQB, 2, 4, 128], BF16)
    c_caus = dram_in("c_caus", [128, 4, 128])
    c_acaus = dram_in("c_acaus", [128, 4, 128])
    c_bcmp = dram_in("c_bcmp", [NQB, 128, 2, 2, 4, 128])
    c_ovl = dram_in("c_ovl", [128, 2, 64], BF16)
    c_ma = dram_in("c_ma", [128, NQB, 64])
    c_mb = dram_in("c_mb", [128, NQB, 64])
    c_bdil = dram_in("c_bdil", [128, 3, 2, 4, 128])
    c_s65 = dram_in("c_s65", [65, 64])
    y = nc.dram_tensor("y", [2048, D], F32, kind="ExternalOutput").ap()
    xm_d = nc.dram_tensor("xm_scratch", [NQ, D], F32, kind="Internal").ap()
    t_xmd = trks(NQB)

    def dump(name, ap, trk):
        o = nc.dram_tensor("dbg_" + name, list(ap.shape), ap.dtype, kind="ExternalOutput").ap()
        P.dma("sp", o, ap, r=trk)

    def done():
        for j in range(NDS):
            if P.dcnt[j] > 0:
                P._wait("sp", (("d", j), 16 * P.dcnt[j]))
        return nc

    PS = [ps(es, "ps%d" % i, [128, 512], F32) for i in range(7)]
    PST = ps(es, "pst", [128, 1024], BF16)
    tPS = trks_ex(7)
    tPST = Trk(True)
    psrr = [0]

    def nextps(lo, hi):
        k = lo + psrr[0] % (hi - lo)
        psrr[0] += 1
        return k

    def v4(ap):
        return ap.rearrange("p (a b) -> p a b", a=4)

    ident = sb(es, "ident", [128, 128], F32)
    identb = sb(es, "identb", [128, 128], BF16)
    halo = sb(es, "halo", [128, 1], F32)
    t_const = Trk()
    P.dma("sp", ident[:], c_ident[:, :], w=[t_const])
    P.dma("sp", identb[:], c_identb[:, :], w=[t_const])
    P.dma("sp", halo[:], c_halo[:, :], w=[t_const])

    evac_rr = [0]

    def evac_eng():
        evac_rr[0] += 1
        return "act" if evac_rr[0] % 2 else "dve"

    def rmsnorm_tile(junk, ss, t_s, xt, t_x, gt, t_g, out_t, t_out):
        P.act(junk[:], xt, AF.Square, r=[t_x], w=[t_s], accum_out=ss[:, 0:1])
        P.ts(ss[:, 1:2], ss[:, 0:1], 1.0 / D, ALU.mult, 1e-6, ALU.add, r=[t_s], w=[t_s])
        P.act(ss[:, 2:3], ss[:, 1:2], AF.Sqrt, r=[t_s], w=[t_s])
        P.op("dve", lambda e: e.reciprocal(out=ss[:, 3:4], in_=ss[:, 2:3]), r=[t_s], w=[t_s])
        P.stt(out_t, xt, ss[:, 3:4], gt, ALU.mult, ALU.mult, r=[t_x, t_g, t_s], w=[t_out])

    def chunks_of(tok0, n):
        return list(range(tok0 // 128, (tok0 + n - 1) // 128 + 1))

    hT_d = nc.dram_tensor("hT_scratch", [128, 8, T], BF16, kind="Internal").ap()
    t_hTd = trks(2)
    with contextlib.ExitStack() as PA:
        OAT = sb(PA, "OAT", [128, 4, NQ], BF16)
        t_OAT = trks(NQB)
        NPT = 4
        PT = [sb(PA, "PT%d" % i, [128, 512], BF16) for i in range(NPT)]
        t_PT = trks(NPT)
        ptrr = [0]
        NTMP = 3
        TMP = [sb(PA, "TMP%d" % i, [128, 512], F32) for i in range(NTMP)]
        t_TMP = trks(NTMP)
        tmprr = [0]

        def next_pt():
            k = ptrr[0] % NPT
            ptrr[0] += 1
            return k

        def next_tmp():
            k = tmprr[0] % NTMP
            tmprr[0] += 1
            return k

        GT = [sb(PA, "GT%d" % i, [128, 512], F32) for i in range(2)]
        t_GT = trks(2)

        def gelu_tanh(out, t_out, xin, t_x, n):
            a, b2 = GT[0][:, 0:n], GT[1][:, 0:n]
            P.tt(a, xin, xin, ALU.mult, r=[t_x], w=[t_GT[0]])
            P.ts(a, a, 0.044715, ALU.mult, 1.0, ALU.add, r=[t_GT[0]], w=[t_GT[0]])
            P.tt(a, a, xin, ALU.mult, r=[t_GT[0], t_x], w=[t_GT[0]])
            P.act(b2, a, AF.Sigmoid, r=[t_GT[0]], w=[t_GT[1]], scale=1.5957691216057308)
            P.tt(out, b2, xin, ALU.mult, r=[t_GT[1], t_x], w=t_out)

        def load_wt(dst, t_dst, src, col_specs):
            o = 0
            pieces = []
            for (c0, n) in col_specs:
                pieces.append((dst[:, :, o:o + n], src[:, c0:c0 + n].rearrange("(c p) n -> p c n", p=128)))
                o += n
            P.dma_multi("pool", pieces, w=[t_dst])

        def phase1(ph, hdst, t_hdst, tiles):
            xt2 = [sb(ph, "xt%d" % i, [128, D], F32) for i in range(2)]
            t_xt = trks(2)
            xn2 = [sb(ph, "xn%d" % i, [128, D], BF16) for i in range(2)]
            t_xn = trks(2)
            gmr = sb(ph, "gmr", [128, D], F32)
            t_gm = Trk()
            junk = sb(ph, "junk", [128, D], F32)
            ss2 = [sb(ph, "ss%d" % i, [128, 4], F32) for i in range(2)]
            t_ss = trks(2)
            P.dma("sp", gmr[:], g_mix[0:1, :].partition_broadcast(128), w=[t_gm])

            def run(tiles_):
                for kk, t in enumerate(tiles_):
                    k = kk % 2
                    P.dma("sp", xt2[k][:], x[t * 128:(t + 1) * 128, :], w=[t_xt[k]])
                    rmsnorm_tile(junk, ss2[k], t_ss[k], xt2[k][:], t_xt[k], gmr[:], t_gm, xn2[k][:], t_xn[k])
                    for c in range(8):
                        P.tr(PST[:, c * 128:(c + 1) * 128], xn2[k][:, c * 128:(c + 1) * 128], identb[:],
                             r=[t_xn[k], t_const], w=[tPST])
                    P.copy(evac_eng(), hdst[:, :, kk * 128:(kk + 1) * 128],
                           PST[:, :].rearrange("p (c n) -> p c n", c=8), r=[tPST], w=[t_hdst[kk]])
            return run

        with contextlib.ExitStack() as nsa:
            QN = [sb(nsa, "QN%d" % g, [128, 4, NQ], BF16) for g in range(2)]
            t_QN = trks(NQB)
            KS = sb(nsa, "KS", [128, T], BF16)
            KW = sb(nsa, "KW", [128, T], BF16)
            t_KS = trks(NCH)
            t_KW = trks(NCH)
            VS = sb(nsa, "VS", [128, NCH, 2, 66], BF16)
            VW = sb(nsa, "VW", [128, NCH, 2, 66], BF16)
            t_VS = trks(NCH)
            t_VW = trks(NCH)
            GA = sb(nsa, "GA", [128, NQB, 24], F32)
            t_GA = trks(NQB)
            KC = sb(nsa, "KC", [128, 256], BF16)
            t_KC = Trk()
            VCX = sb(nsa, "VCX", [128, 2, 2, 130], BF16)
            t_VCX = Trk()
            vtm = sb(nsa, "vtm", [128, 32], BF16)
            t_vtm = Trk()
            P.dma("sp", vtm[:], c_validtm[:, :], w=[t_vtm])
            P.copy("dve", VS[:, :, :, 64], vtm[:].unsqueeze(2).broadcast_to([128, NCH, 2]), r=[t_vtm], w=t_VS)
            P.copy("dve", VW[:, :, :, 64], vtm[:].unsqueeze(2).broadcast_to([128, NCH, 2]), r=[t_vtm], w=t_VW)
            P.op("dve", lambda e: e.memset(QN[0][64:128, :, :], 0.0), w=t_QN)
            P.op("dve", lambda e: e.memset(QN[1][0:64, :, :], 0.0), w=t_QN)

            with contextlib.ExitStack() as ph:
                hTh = sb(ph, "hTh", [128, 8, 2048], BF16)
                t_hTh = trks(16)
                wq = sb(ph, "wq", [128, 8, 512], BF16)
                wk = sb(ph, "wk", [128, 8, 512], BF16)
                wv = sb(ph, "wv", [128, 8, 280], BF16)
                t_wq, t_wk, t_wv = Trk(), Trk(), Trk()
                load_wt(wq, t_wq, w_in, [(0, 64), (256, 64), (64, 64), (320, 64), (128, 64), (384, 64), (192, 64),
                                         (448, 64)])
                load_wt(wk, t_wk, w_in, [(768, 128), (1024, 128), (512, 128), (640, 128)])
                load_wt(wv, t_wv, w_in, [(896, 128), (1152, 128), (1280, 24)])
                SRD = [sb(ph, "SRD%d" % kv, [128, 2, 16, 256], BF16) for kv in range(2)]
                t_SRD = trks(2)
                for kv in range(2):
                    P.op("dve", lambda e, kv=kv: e.memset(SRD[kv][:, 1, :, 255:256], 0.0), w=[t_SRD[kv]])
                with contextlib.ExitStack() as p1s:
                    run_p1 = phase1(p1s, hTh, t_hTh, None)
                    for hh_ in range(2):
                        run_p1(list(range(16 * hh_, 16 * hh_ + 16)))
                        if stage != 23.1:
                            P.dma("sp", hT_d[:, :, hh_ * 2048:(hh_ + 1) * 2048], hTh[:, :, :], r=t_hTh,
                                  w=[t_hTd[hh_]])
                        if stage == 1 and hh_ == 0:
                            dump("hTh", hTh[:, :, Q0:2048], t_hTh)
                            return done()

                        def fm(wt, t_w, wc0, lt0, n, dst, t_dst, scale=None, eng=None, dst2=None):
                            b = nextps(0, 4)
                            for dm in range(8):
                                P.mm(PS[b][:, 0:n], wt[:, dm, wc0:wc0 + 128], hTh[:, dm, lt0:lt0 + n],
                                     start=(dm == 0), stop=(dm == 7),
                                     r=[t_w] + [t_hTh[c] for c in chunks_of(lt0, n)], w=[tPS[b]])
                            return b

                        if stage == 23.2:
                            continue
                        qtiles = [(Q0, 128, 0)] if hh_ == 0 else [(n0, 512, 128 + n0) for n0 in range(0, 2048, 512)]
                        for r_ in range(4):
                            for (lt0, n, qoff) in qtiles:
                                b = fm(wq, t_wq, 128 * r_, lt0, n, None, None)
                                tq = [t_QN[c] for c in chunks_of(qoff, n)]
                                P.copy("act", QN[0][0:64, r_, qoff:qoff + n], PS[b][0:64, 0:n], r=[tPS[b]], w=tq,
                                       scale=0.125)
                                P.copy("dve", QN[1][64:128, r_, qoff:qoff + n], PS[b][64:128, 0:n], r=[tPS[b]], w=tq,
                                       scale=0.125)
                        for n0 in range(0, 2048, 512):
                            g0 = hh_ * 2048 + n0
                            b = fm(wk, t_wk, 0, n0, 512, None, None)
                            P.copy(evac_eng(), KS[:, g0:g0 + 512], PS[b][:, 0:512], r=[tPS[b]],
                                   w=[t_KS[c] for c in chunks_of(g0, 512)])
                            b = fm(wk, t_wk, 128, n0, 512, None, None)
                            P.copy(evac_eng(), KW[:, g0:g0 + 512], PS[b][:, 0:512], r=[tPS[b]],
                                   w=[t_KW[c] for c in chunks_of(g0, 512)])
                            for kv in range(2):
                                b = fm(wk, t_wk, 256 + 128 * kv, n0, 512, None, None)
                                c0 = g0 // 16
                                pv_ = PS[b][:, 0:512].rearrange("d (c p) -> d p c", p=16)
                                P.copy("dve", SRD[kv][:, 0, :, c0:c0 + 32], pv_, r=[tPS[b]], w=[t_SRD[kv]])
                                if g0 == 0:
                                    P.copy("dve", SRD[kv][:, 1, :, 0:31],
                                           PS[b][:, 16:512].rearrange("d (c p) -> d p c", p=16),
                                           r=[tPS[b]], w=[t_SRD[kv]])
                                else:
                                    P.copy("dve", SRD[kv][:, 1, :, c0 - 1:c0 + 31], pv_, r=[tPS[b]], w=[t_SRD[kv]])
                        for tl in range(16):
                            t = hh_ * 16 + tl
                            b = nextps(0, 4)
                            for dm in range(8):
                                P.mm(PS[b][:, 0:280], hTh[:, dm, tl * 128:(tl + 1) * 128], wv[:, dm, 0:280],
                                     start=(dm == 0), stop=(dm == 7), r=[t_wv, t_hTh[tl]], w=[tPS[b]])
                            P.copy("dve", VS[:, t, :, 0:64], PS[b][:, 0:128].rearrange("p (g d) -> p g d", g=2),
                                   r=[tPS[b]], w=[t_VS[t]])
                            P.copy("dve", VW[:, t, :, 0:64], PS[b][:, 128:256].rearrange("p (g d) -> p g d", g=2),
                                   r=[tPS[b]], w=[t_VW[t]])
                            if t >= QB0:
                                P.act(GA[:, t - QB0, :], PS[b][:, 256:280], AF.Sigmoid, r=[tPS[b]],
                                      w=[t_GA[t - QB0]])
                P.barrier()
                if stage == 23.2:
                    dump("hTh", hTh[:, :, 0:128], t_hTh)
                    return done()
                if stage in (23, 23.1):
                    dump("QN0", QN[0][:, :, 0:256], t_QN)
                    dump("QN1", QN[1][:, :, 0:256], t_QN)
                    dump("KS", KS[:, Q0:Q0 + 256], t_KS)
                    dump("KW", KW[:, Q0:Q0 + 256], t_KW)
                    dump("VS", VS[:, 16:18, :, 0:65], t_VS)
                    dump("GA", GA[:, 0:2, :], t_GA)
                    return done()

                W1 = sb(ph, "W1", [128, 32, 128], BF16)
                W2 = sb(ph, "W2", [128, 64], BF16)
                peT = sb(ph, "peT", [128, 64], F32)
                peTb = sb(ph, "peTb", [128, 32, 2], BF16)
                hb = sb(ph, "hb", [128, 1], F32)
                HT = sb(ph, "HT", [128, 2, 256], BF16)
                t_W1 = Trk()
                t_W2 = Trk()
                t_pe = Trk()
                t_hb = Trk()
                t_HT = Trk()
                ovl = sb(ph, "ovl", [128, 2, 64], BF16)
                t_ovl = Trk()
                P.dma("sp", ovl[:], c_ovl[:, :, :], w=[t_ovl])
                for kv in range(2):
                    w1d, w2d, ped = ((w_k1, w_k2, pe_k), (w_v1, w_v2, pe_v))[kv]
                    w1v = w1d.rearrange("(p d) h -> d p h", d=64)
                    P.dma_multi("pool", [(W1[0:64, :, :], w1v), (W1[64:128, :, :], w1v)], w=[t_W1])
                    P.dma("pool", W2[:, :], w2d[:, :], w=[t_W2])
                    P.dma("sp", peT[0:32, 0:64], ped[:, :], w=[t_pe])
                    b = nextps(0, 4)
                    P.tr(PS[b][0:64, 0:32], peT[0:32, 0:64], ident[0:32, 0:32], r=[t_pe, t_const], w=[tPS[b]])
                    P.copy("dve", peTb[0:64, :, :], PS[b][0:64, 0:32].unsqueeze(2).broadcast_to([64, 32, 2]),
                           r=[tPS[b]], w=[t_pe])
                    b = nextps(0, 4)
                    for p_ in range(32):
                        P.mm(PS[b][:, 0:2], W1[0:64, p_, :], peTb[0:64, p_, :], start=(p_ == 0),
                             stop=(p_ == 31), r=[t_W1, t_pe], w=[tPS[b]])
                    P.copy("dve", hb[:, 0:1], PS[b][:, 0:1], r=[tPS[b]], w=[t_hb])
                    m = next_tmp()
                    for g in range(2):
                        b = 2 * g + (kv % 2)
                        for p_ in range(32):
                            P.mm(PS[b][:, 0:256], W1[64 * g:64 * g + 64, p_, :],
                                 SRD[kv][64 * g:64 * g + 64, p_ // 16, p_ % 16, :],
                                 start=(p_ == 0), stop=(p_ == 31), r=[t_W1, t_SRD[kv]], w=[tPS[b]])
                        P.ts(TMP[m][:, g * 256:(g + 1) * 256], PS[b][:, 0:256], hb[:, 0:1], ALU.add,
                             r=[tPS[b], t_hb], w=[t_TMP[m]])
                    gelu_tanh(HT[:, :, :].rearrange("p g c -> p (g c)"), [t_HT], TMP[m][:, :], t_TMP[m], 512)
                    if kv == 0:
                        for g in range(2):
                            b = nextps(4, 6)
                            P.mm(PS[b][0:64, 0:256], W2[:, :], HT[:, g, :], start=True, stop=True,
                                 r=[t_W2, t_HT], w=[tPS[b]])
                            P.copy("dve", KC[64 * g:64 * g + 64, :], PS[b][0:64, 0:256], r=[tPS[b]], w=[t_KC])
                    else:
                        for kc in range(2):
                            for g in range(2):
                                b = nextps(4, 6)
                                P.mm(PS[b][:, 0:64], HT[:, g, kc * 128:(kc + 1) * 128], W2[:, :], start=True,
                                     stop=True, r=[t_W2, t_HT], w=[tPS[b]])
                                P.copy("dve", VCX[:, kc, g, 0:64], PS[b][:, 0:64], r=[tPS[b]], w=[t_VCX])
                                P.copy("dve", VCX[:, kc, g, 66:130], ovl[:, kc, :], r=[t_ovl], w=[t_VCX])
                        P.op("dve", lambda e: e.memset(VCX[:, :, :, 64:65], 1.0), w=[t_VCX])
                        P.op("dve", lambda e: e.memset(VCX[:, :, :, 65:66], 0.0), w=[t_VCX])
                P.barrier()

            if stage == 2:
                dump("KC", KC[:, :], [t_KC])
                dump("VCX", VCX[:, :, :, :], [t_VCX])
                return done()

            ea = sb(nsa, "ea", [128, T], BF16)
            caus = sb(nsa, "caus", [128, 4, 128], F32)
            acaus = sb(nsa, "acaus", [128, 4, 128], F32)
            ma = sb(nsa, "ma", [128, NQB, 64], F32)
            mb = sb(nsa, "mb", [128, NQB, 64], F32)
            t_tab = Trk()
            P.dma("sp", ea[:], c_ea[:, :], w=[t_tab])
            P.dma("sp", caus[:], c_caus[:, :, :], w=[t_tab])
            P.dma("sp", acaus[:], c_acaus[:, :, :], w=[t_tab])
            P.dma("sp", ma[:], c_ma[:, :, :], w=[t_tab])
            P.dma("sp", mb[:], c_mb[:, :, :], w=[t_tab])
            RB = [[sb(nsa, "RB%d%d" % (g, k), [128, 4, 128], BF16) for k in range(2)] for g in range(2)]
            RW = [[sb(nsa, "RW%d%d" % (g, k), [128, 4, 128], BF16) for k in range(2)] for g in range(2)]
            t_RB = [[Trk() for k in range(2)] for g in range(2)]
            t_RW = [[Trk() for k in range(2)] for g in range(2)]
            for g in range(2):
                for k in range(2):
                    P.op("dve", lambda e, g=g, k=k: e.memset(RB[g][k][:, :, :], 0.0), w=[t_RB[g][k]])
                    P.op("dve", lambda e, g=g, k=k: e.memset(RW[g][k][:, :, :], 0.0), w=[t_RW[g][k]])
            Bc = [sb(nsa, "Bc%d" % k, [128, 2, 2, 4, 128], F32) for k in range(2)]
            t_Bc = trks(2)
            ONSA = [sb(nsa, "ONSA%d" % k, [128, 512], F32) for k in range(2)]
            t_ONSA = trks(2)
            sm = [sb(nsa, "sm%d" % k, [128, 32], F32) for k in range(4)]
            t_sm = trks(4)
            impn = [sb(nsa, "impn%d" % k, [128, 64], F32) for k in range(2)]
            impw = [sb(nsa, "impw%d" % k, [128, 64], F32) for k in range(2)]
            t_imp = trks(2)
            srr = [0]
            srot = [0]

            def s_tile(src_fn, kind, g, i, c):
                qs = slice(i * 128, (i + 1) * 128)
                rb = RB[g][i % 2]
                t_rb = t_RB[g][i % 2]
                rw = RW[g][i % 2]
                t_rw = t_RW[g][i % 2]
                b = (0, 1, 6)[srot[0] % 3]
                srot[0] += 1
                pv = v4(PS[b][:, 0:512])
                if kind == "cmp":
                    P.mm(pv, KC[:, c * 128:(c + 1) * 128], QN[g][:, :, qs], True, True,
                         r=[t_KC, t_QN[i]], w=[tPS[b]])
                    addt = Bc[i % 2][:, c, g, :, :]
                    t_add = t_Bc[i % 2]
                else:
                    KX, t_KX = (KS, t_KS) if kind == "sel" else (KW, t_KW)
                    P.mm(pv, KX[:, c * 128:(c + 1) * 128], QN[g][:, :, qs], True, False,
                         r=[t_KX[c], t_QN[i]], w=[tPS[b]])
                    if kind == "sel":
                        P.mm(pv, ea[:, c * 128:(c + 1) * 128], rb[:, :, :], False, True,
                             r=[t_tab, t_rb], w=[tPS[b]])
                    else:
                        P.mm(pv, ea[:, c * 128:(c + 1) * 128], rw[:, :, :], False, True,
                             r=[t_tab, t_rw], w=[tPS[b]])
                    qb = QB0 + i
                    addt, t_add = None, t_tab
                    if c == qb:
                        addt = caus[:, :, :]
                    elif kind == "win" and c == qb - 4:
                        addt = acaus[:, :, :]
                k = next_pt()
                if addt is not None:
                    m = next_tmp()
                    P.tt(v4(TMP[m][:, :]), pv, addt, ALU.add, r=[tPS[b], t_add], w=[t_TMP[m]])
                    P.act(PT[k][:, :], TMP[m][:, :], AF.Exp, r=[t_TMP[m]], w=[t_PT[k]])
                else:
                    P.act(PT[k][:, :], PS[b][:, 0:512], AF.Exp, r=[tPS[b]], w=[t_PT[k]])
                return k

            def combine(i, g, bi, o_fn, d_fn, t_acc, first):
                k = srr[0] % 4
                srr[0] += 1
                s_ = sm[k]
                t_s = t_sm[k]
                for r_ in range(4):
                    P.ts(s_[:, r_:r_ + 1], d_fn(r_), 1e-30, ALU.max, r=t_acc, w=[t_s])
                P.op("dve", lambda e: e.reciprocal(out=s_[:, 4:8], in_=s_[:, 0:4]), r=[t_s], w=[t_s])
                P.tt(s_[:, 8:12], s_[:, 4:8], GA[:, i, g * 12 + bi:g * 12 + 12:3], ALU.mult,
                     r=[t_s, t_GA[i]], w=[t_s])
                on = ONSA[i % 2]
                for r_ in range(4):
                    h0 = (4 * g + r_) * 64
                    if first:
                        P.ts(on[:, h0:h0 + 64], o_fn(r_), s_[:, 8 + r_:9 + r_], ALU.mult,
                             r=t_acc + [t_s], w=[t_ONSA[i % 2]])
                    else:
                        P.stt(on[:, h0:h0 + 64], o_fn(r_), s_[:, 8 + r_:9 + r_], on[:, h0:h0 + 64], ALU.mult, ALU.add,
                              r=t_acc + [t_s], w=[t_ONSA[i % 2]])
                return s_, t_s

            PIPE = 2
            CMPB = {0: (4, 5), 1: (2, 3)}
            WINB = {0: 4, 1: 5}
            SELB = {0: 2, 1: 3}

            def topk_chain(i, g, s_, t_s):
                ba, bb_ = CMPB[g]
                t_acc = [tPS[ba], tPS[bb_]]
                im, iw, t_im = impn[g], impw[g], t_imp[g]
                for r_ in range(4):
                    src = PS[CMPB[g][r_ // 2]][:, (r_ % 2) * 130 + 66:(r_ % 2) * 130 + 130]
                    if r_ == 0:
                        P.ts(im[:, :], src, s_[:, 4:5], ALU.mult, r=t_acc + [t_s], w=[t_im])
                    else:
                        P.stt(im[:, :], src, s_[:, 4 + r_:5 + r_], im[:, :], ALU.mult, ALU.add,
                              r=t_acc + [t_s], w=[t_im])
                P.tt(im[:, :], im[:, :], ma[:, i, :], ALU.mult, r=[t_im, t_tab], w=[t_im])
                P.tt(im[:, :], im[:, :], mb[:, i, :], ALU.add, r=[t_im, t_tab], w=[t_im])
                P.op("dve", lambda e: e.max(out=s_[:, 16:24], in_=im[:, :]), r=[t_im], w=[t_s])
                P.op("dve", lambda e: e.match_replace(out=iw[:, :], in_to_replace=s_[:, 16:24], in_values=im[:, :],
                                                      imm_value=-1.0e9), r=[t_im, t_s], w=[t_im])
                P.op("dve", lambda e: e.max(out=s_[:, 24:32], in_=iw[:, :]), r=[t_im], w=[t_s])
                P.ts(s_[:, 12:13], s_[:, 31:32], 0.0, ALU.max, r=[t_s], w=[t_s])
                P.ts(iw[:, :], im[:, :], s_[:, 12:13], ALU.is_ge, r=[t_im, t_s], w=[t_im])
                P.ts(iw[:, :], iw[:, :], -1.0, ALU.add, -NEG, ALU.mult, r=[t_im], w=[t_im])

            def sel_rows(i, g):
                rb, t_rb = RB[g][i % 2], t_RB[g][i % 2]
                P.tr(PS[6][0:64, 0:128], impw[g][:, 0:64], ident[:, :], r=[t_imp[g], t_const], w=[tPS[6]])
                P.copy("act", rb[0:64, :, :], PS[6][0:64, 0:128].unsqueeze(1).broadcast_to([64, 4, 128]),
                       r=[tPS[6]], w=[t_rb])

            def part_b(i, job, k):
                kind, g, c = job
                qb = QB0 + i
                if kind == "cmp":
                    for r_ in range(4):
                        bb = CMPB[g][r_ // 2]
                        P.mm(PS[bb][:, (r_ % 2) * 130:(r_ % 2) * 130 + 130], PT[k][:, r_ * 128:(r_ + 1) * 128],
                             VCX[:, c, g, :], start=(c == 0 and r_ % 2 == 0), stop=(c == 1),
                             r=[t_PT[k], t_VCX], w=[tPS[bb]], skip=True)
                    if c == 1:
                        t_acc = [tPS[CMPB[g][0]], tPS[CMPB[g][1]]]
                        s_, t_s = combine(i, g, 0,
                                          lambda r_: PS[CMPB[g][r_ // 2]][:, (r_ % 2) * 130:(r_ % 2) * 130 + 64],
                                          lambda r_: PS[CMPB[g][r_ // 2]][:, (r_ % 2) * 130 + 64:(r_ % 2) * 130 + 65],
                                          t_acc, True)
                        topk_chain(i, g, s_, t_s)
                elif kind == "win":
                    bw = WINB[g]
                    for r_ in range(4):
                        P.mm(PS[bw][:, r_ * 65:(r_ + 1) * 65], PT[k][:, r_ * 128:(r_ + 1) * 128], VW[:, c, g, 0:65],
                             start=(c == qb - 4 and r_ == 0), stop=(c == qb), r=[t_PT[k], t_VW[c]], w=[tPS[bw]],
                             skip=True)
                    if c == qb:
                        combine(i, g, 2, lambda r_: PS[bw][:, r_ * 65:r_ * 65 + 64],
                                lambda r_: PS[bw][:, r_ * 65 + 64:r_ * 65 + 65], [tPS[bw]], False)
                else:
                    bs = SELB[g]
                    for r_ in range(4):
                        P.mm(PS[bs][:, r_ * 65:(r_ + 1) * 65], PT[k][:, r_ * 128:(r_ + 1) * 128], VS[:, c, g, 0:65],
                             start=(c == 0 and r_ == 0), stop=(c == qb), r=[t_PT[k], t_VS[c]], w=[tPS[bs]],
                             skip=True)
                    if c == qb:
                        combine(i, g, 1, lambda r_: PS[bs][:, r_ * 65:r_ * 65 + 64],
                                lambda r_: PS[bs][:, r_ * 65 + 64:r_ * 65 + 65], [tPS[bs]], False)

            for i in range(NQB):
                qb = QB0 + i
                P.dma("sp", Bc[i % 2][:], c_bcmp[i], w=[t_Bc[i % 2]])
                for g in range(2):
                    P.dma("sp", RB[g][i % 2][64:68, :, :], c_rbc[:, i, g, :, :], w=[t_RB[g][i % 2]])
                    P.dma("sp", RW[g][i % 2][64:68, :, :], c_rbc[:, i, g, :, :], w=[t_RW[g][i % 2]])
                jobs = []
                for g in range(2):
                    for kc in range(2):
                        jobs.append(("cmp", g, kc))
                for g in range(2):
                    for dl in range(4, -1, -1):
                        jobs.append(("win", g, qb - dl))
                for g in range(2):
                    for c in range(qb + 1):
                        jobs.append(("sel", g, c))
                pend = []
                for job in jobs + [None, None]:
                    if job is not None:
                        kind, g, c = job
                        if kind == "sel" and c == 0:
                            sel_rows(i, g)
                        pend.append((job, s_tile(None, kind, g, i, c)))
                    if len(pend) > PIPE or (job is None and pend):
                        pj, pk = pend.pop(0)
                        part_b(i, pj, pk)
                on = ONSA[i % 2]
                for fc in range(4):
                    P.tr(PS[6][:, fc * 128:(fc + 1) * 128], on[:, fc * 128:(fc + 1) * 128], ident[:, :],
                         r=[t_ONSA[i % 2], t_const], w=[tPS[6]])
                P.copy("act", OAT[:, :, i * 128:(i + 1) * 128], v4(PS[6][:, 0:512]), r=[tPS[6]], w=[t_OAT[i]])
            P.barrier()
        if stage == 3:
            dump("OAT", OAT[:, :, :], t_OAT)
            return done()

        OBT = sb(PA, "OBT", [128, 4, NQ], BF16)
        t_OBT = trks(NQB)
        TQ0 = 1536
        with contextlib.ExitStack() as dl_:
            accD = sb(dl_, "accD", [128, 4, NQ], F32)
            t_acc = trks(NQB)
            hq = [sb(dl_, "hq%d" % k, [128, 8, 1024], BF16) for k in range(2)]
            t_hq = trks(2)
            wd = sb(dl_, "wd", [128, 8, 768], BF16)
            t_wd = Trk()
            QD = sb(dl_, "QD", [128, 2, T - TQ0], BF16)
            KD = sb(dl_, "KD", [128, 2, T], BF16)
            VT = sb(dl_, "VT", [128, 2, T], BF16)
            VD = sb(dl_, "VD", [128, NCH, 4, 66], BF16)
            BD = sb(dl_, "BD", [128, 2, 4, 128], F32)
            vdd = sb(dl_, "vdd", [128, 3, 32], BF16)
            t_QD, t_KD, t_VT, t_VD, t_BD, t_vdd = Trk(), Trk(), Trk(), Trk(), Trk(), Trk()
            P.dma("sp", vdd[:], c_validd[:, :, :], w=[t_vdd])
            hqrr = [0]
            for gd, (w_, r_) in enumerate(DIL):
                J = T // r_
                JQ = (T - TQ0) // r_
                jq0 = TQ0 // r_
                npc = J // 128
                c0w = 1304 + gd * 256
                load_wt(wd, t_wd, w_in, [(c0w, 256), (c0w + 768, 256), (c0w + 1536, 256)])
                P.dma("sp", BD[:], c_bdil[:, gd, :, :, :], w=[t_BD])
                P.copy("dve", VD[:, :, :, 64], vdd[:, gd, :].unsqueeze(2).broadcast_to([128, NCH, 4]),
                       r=[t_vdd], w=[t_VD])
                KDv = KD[:, :, :].rearrange("p a (rho j) -> p a rho j", rho=r_)
                VTv = VT[:, :, :].rearrange("p a (rho j) -> p a rho j", rho=r_)
                QDv = QD[:, :, :].rearrange("p a (rho j) -> p a rho j", rho=r_)
                for qt in range(4):
                    k = hqrr[0] % 2
                    hqrr[0] += 1
                    P.dma("sp", hq[k][:, :, :], hT_d[:, :, qt * 1024:(qt + 1) * 1024], r=t_hTd, w=[t_hq[k]])
                    for n0 in range(0, 1024, 512):
                        g0 = qt * 1024 + n0
                        for which in range(3):
                            if which == 0 and g0 < TQ0:
                                continue
                            for pr in range(2):
                                b = nextps(0, 4)
                                for dm in range(8):
                                    P.mm(PS[b][:, 0:512], wd[:, dm, which * 256 + pr * 128:which * 256 + pr * 128 + 128],
                                         hq[k][:, dm, n0:n0 + 512], start=(dm == 0), stop=(dm == 7),
                                         r=[t_wd, t_hq[k]], w=[tPS[b]])
                                src = PS[b][:, 0:512].rearrange("p (j rho) -> p rho j", rho=r_)
                                nj = 512 // r_
                                if which == 0:
                                    j0 = (g0 - TQ0) // r_
                                    P.copy(evac_eng(), QDv[:, pr, :, j0:j0 + nj], src, r=[tPS[b]], w=[t_QD],
                                           scale=0.125)
                                elif which == 1:
                                    j0 = g0 // r_
                                    P.copy(evac_eng(), KDv[:, pr, :, j0:j0 + nj], src, r=[tPS[b]], w=[t_KD])
                                else:
                                    j0 = g0 // r_
                                    P.copy(evac_eng(), VTv[:, pr, :, j0:j0 + nj], src, r=[tPS[b]], w=[t_VT])
                for ci in range(NCH):
                    for pr in range(2):
                        P.tr(PST[:, pr * 128:(pr + 1) * 128], VT[:, pr, ci * 128:(ci + 1) * 128], identb[:],
                             r=[t_VT, t_const], w=[tPST])
                    P.copy(evac_eng(), VD[:, ci, :, 0:64], PST[:, 0:256].rearrange("p (h d) -> p h d", h=4),
                           r=[tPST], w=[t_VD])
                def dil_a(rho, jb, q_lo, dls):
                    nq = 128 - q_lo
                    pts = {}
                    for dl in dls:
                        jc = jb - dl
                        kk = next_pt()
                        pts[dl] = kk
                        sb0 = 0 if dl == 0 else 4
                        for hh in range(4):
                            par, pr = hh % 2, hh // 2
                            base = 64 * par
                            kc0 = rho * J + jc * 128
                            qc0 = rho * JQ + (jb * 128 + q_lo - jq0)
                            P.mm(PS[sb0 + par][:, pr * 128 + q_lo:pr * 128 + 128],
                                 KD[base:base + 64, pr, kc0:kc0 + 128], QD[base:base + 64, pr, qc0:qc0 + nq],
                                 True, True, r=[t_KD, t_QD], w=[tPS[sb0 + par]])
                        for par in range(2):
                            m = next_tmp()
                            tv = TMP[m][:, 0:256].rearrange("p (a b) -> p a b", a=2)[:, :, q_lo:128]
                            pv_ = PS[sb0 + par][:, 0:256].rearrange("p (a b) -> p a b", a=2)[:, :, q_lo:128]
                            P.tt(tv, pv_, BD[:, dl, par::2, q_lo:128], ALU.add, r=[tPS[sb0 + par], t_BD],
                                 w=[t_TMP[m]])
                            P.act(v4(PT[kk][:, :])[:, par::2, q_lo:128], tv, AF.Exp, r=[t_TMP[m]], w=[t_PT[kk]])
                    return pts

                def dil_b(rho, jb, q_lo, dls, pts):
                    ab = 2 + ((rho * npc + jb) % 2)
                    for hh in range(4):
                        for n_, dl in enumerate(dls):
                            ci = rho * npc + (jb - dl)
                            P.mm(PS[ab][0:65, hh * 128 + q_lo:hh * 128 + 128], VD[:, ci, hh, 0:65],
                                 PT[pts[dl]][:, hh * 128 + q_lo:hh * 128 + 128],
                                 start=(n_ == 0), stop=(n_ == len(dls) - 1),
                                 r=[t_PT[pts[dl]], t_VD], w=[tPS[ab]])
                    tok0 = rho + r_ * (128 * jb + q_lo) - Q0
                    tok1 = rho + r_ * (128 * jb + 127) - Q0
                    blks = [t_acc[c] for c in range(tok0 // 128, tok1 // 128 + 1)]
                    dst = accD[0:65, :, tok0:tok1 + 1:r_]
                    srcp = v4(PS[ab][0:65, 0:512])[:, :, q_lo:128]
                    if gd == 0:
                        P.copy("dve", dst, srcp, r=[tPS[ab]], w=blks)
                    else:
                        P.tt(dst, srcp, dst, ALU.add, r=[tPS[ab]] + blks, w=blks)

                djobs = []
                for rho in range(r_):
                    for jb in range(npc):
                        jmin = -(-(Q0 - rho) // r_)
                        q_lo = max(0, jmin - 128 * jb)
                        if q_lo >= 128:
                            continue
                        djobs.append((rho, jb, q_lo, [dl for dl in (0, 1) if jb - dl >= 0]))
                dpend = None
                for dj in djobs + [None]:
                    cur = None
                    if dj is not None:
                        cur = (dj, dil_a(*dj))
                    if dpend is not None:
                        dil_b(*dpend[0], dpend[1])
                    dpend = cur
            s65 = sb(dl_, "s65", [128, 64], F32)
            t_s65 = Trk()
            P.dma("sp", s65[0:65, :], c_s65[:, :], w=[t_s65])
            for hh in range(4):
                n0 = 0
                while n0 < NQ:
                    n = min(512, NQ - n0)
                    b = nextps(4, 6)
                    tb = [t_acc[c] for c in chunks_of(n0, n)]
                    P.mm(PS[b][0:64, 0:n], s65[0:65, 0:64], accD[0:65, hh, n0:n0 + n], True, True,
                         r=[t_s65] + tb, w=[tPS[b]])
                    m = next_tmp()
                    P.ts(TMP[m][0:64, 0:n], PS[b][0:64, 0:n], 1e-30, ALU.max, r=[tPS[b]], w=[t_TMP[m]])
                    P.op("dve", lambda e, m=m, n=n: e.reciprocal(out=TMP[m][0:64, 0:n], in_=TMP[m][0:64, 0:n]),
                         r=[t_TMP[m]], w=[t_TMP[m]])
                    P.tt(OBT[0:64, hh, n0:n0 + n], accD[0:64, hh, n0:n0 + n], TMP[m][0:64, 0:n], ALU.mult,
                         r=tb + [t_TMP[m]], w=[t_OBT[c] for c in chunks_of(n0, n)])
                    n0 += n
            P.barrier()
        if stage == 4:
            dump("OBT", OBT[0:64, :, :], t_OBT)
            return done()

        with contextlib.ExitStack() as mg:
            WPA = sb(mg, "WPA", [128, 4, D], BF16)
            WPB = sb(mg, "WPB", [128, 4, D], BF16)
            WO = sb(mg, "WO", [128, 8, D], BF16)
            MX = sb(mg, "MX", [128, 8, NQ], BF16)
            hTq = sb(mg, "hTq", [128, 8, NQ], BF16)
            wm = [sb(mg, "wm%d" % k, [128, 8, 256], BF16) for k in range(2)]
            xr = [sb(mg, "xr%d" % k, [128, D], F32) for k in range(2)]
            t_WPA, t_WPB, t_WO, t_hTq = Trk(), Trk(), Trk(), Trk()
            t_MX = trks(NQB)
            t_wm = trks(2)
            t_xr = trks(2)
            P.dma("pool", WPA[:, :, :], w_pa.rearrange("(c p) n -> p c n", p=128), w=[t_WPA])
            P.dma("pool", WPB[0:64, :, :], w_pb.rearrange("(h d) n -> d h n", d=64), w=[t_WPB])
            P.dma_multi("pool", [(WO[:, 0:4, :], w_out[0:512, :].rearrange("(c p) n -> p c n", p=128)),
                                 (WO[:, 4:8, :], w_out[512:1024, :].rearrange("(c p) n -> p c n", p=128))],
                        w=[t_WO])
            P.dma("sp", hTq[:, :, :], hT_d[:, :, Q0:T], r=t_hTd, w=[t_hTq])
            ntiles = []
            n0 = 0
            while n0 < NQ:
                n = min(512, NQ - n0)
                ntiles.append((n0, n))
                n0 += n
            for mc in range(8):
                k = mc % 2
                load_wt(wm[k], t_wm[k], w_in, [(3608 + mc * 128, 128), (4632 + mc * 128, 128)])
                for (n0, n) in ntiles:
                    tb = chunks_of(n0, n)
                    b1, b2_, b3, b4 = [nextps(0, 7) for _ in range(4)]
                    for fc in range(4):
                        P.mm(PS[b1][:, 0:n], WPA[:, fc, mc * 128:(mc + 1) * 128], OAT[:, fc, n0:n0 + n],
                             fc == 0, fc == 3, r=[t_WPA] + [t_OAT[c] for c in tb], w=[tPS[b1]])
                    for hh in range(4):
                        P.mm(PS[b2_][:, 0:n], WPB[0:64, hh, mc * 128:(mc + 1) * 128], OBT[0:64, hh, n0:n0 + n],
                             hh == 0, hh == 3, r=[t_WPB] + [t_OBT[c] for c in tb], w=[tPS[b2_]])
                    for dm in range(8):
                        P.mm(PS[b3][:, 0:n], wm[k][:, dm, 0:128], hTq[:, dm, n0:n0 + n], dm == 0, dm == 7,
                             r=[t_wm[k], t_hTq], w=[tPS[b3]])
                    for dm in range(8):
                        P.mm(PS[b4][:, 0:n], wm[k][:, dm, 128:256], hTq[:, dm, n0:n0 + n], dm == 0, dm == 7,
                             r=[t_wm[k], t_hTq], w=[tPS[b4]])
                    ma_, mb_ = next_tmp(), next_tmp()
                    P.act(TMP[ma_][:, 0:n], PS[b3][:, 0:n], AF.Sigmoid, r=[tPS[b3]], w=[t_TMP[ma_]])
                    P.act(TMP[mb_][:, 0:n], PS[b4][:, 0:n], AF.Sigmoid, r=[tPS[b4]], w=[t_TMP[mb_]])
                    P.tt(TMP[ma_][:, 0:n], TMP[ma_][:, 0:n], PS[b1][:, 0:n], ALU.mult, r=[t_TMP[ma_], tPS[b1]],
                         w=[t_TMP[ma_]])
                    P.tt(TMP[mb_][:, 0:n], TMP[mb_][:, 0:n], PS[b2_][:, 0:n], ALU.mult, r=[t_TMP[mb_], tPS[b2_]],
                         w=[t_TMP[mb_]])
                    P.tt(MX[:, mc, n0:n0 + n], TMP[ma_][:, 0:n], TMP[mb_][:, 0:n], ALU.add,
                         r=[t_TMP[ma_], t_TMP[mb_]], w=[t_MX[c] for c in tb])
            for i in range(NQB):
                k = i % 2
                P.dma("sp", xr[k][:, :], x[Q0 + i * 128:Q0 + (i + 1) * 128, :], w=[t_xr[k]])
                for half in range(2):
                    b = nextps(0, 7)
                    for mc in range(8):
                        P.mm(PS[b][:, 0:512], MX[:, mc, i * 128:(i + 1) * 128], WO[:, mc, half * 512:(half + 1) * 512],
                             mc == 0, mc == 7, r=[t_MX[i], t_WO], w=[tPS[b]])
                    P.tt(xr[k][:, half * 512:(half + 1) * 512], PS[b][:, 0:512], xr[k][:, half * 512:(half + 1) * 512],
                         ALU.add, r=[tPS[b], t_xr[k]], w=[t_xr[k]])
                P.dma("sp", xm_d[i * 128:(i + 1) * 128, :], xr[k][:, :], r=[t_xr[k]], w=[t_xmd[i]])
            P.barrier()
        if stage == 5:
            return done()
    P.barrier()

    with contextlib.ExitStack() as FF:
        H2T = sb(FF, "H2T", [128, 8, 2050], BF16)
        t_H2T = trks(NQB)
        gfr = sb(FF, "gfr", [128, D], F32)
        gfin = sb(FF, "gfin", [128, D], F32)
        t_g = Trk()
        P.dma("sp", gfr[:], g_ffn[0:1, :].partition_broadcast(128), w=[t_g])
        P.dma("sp", gfin[:], g_fin[0:1, :].partition_broadcast(128), w=[t_g])
        cwr = sb(FF, "cwr", [128, 4, 128], F32)
        cw = sb(FF, "cw", [128, 4, 22], F32)
        t_cw = Trk()
        P.dma("sp", cwr[0:22, 0:3, :], conv_w.rearrange("k (j p) -> j k p", p=128), w=[t_cw])
        P.dma("sp", cwr[0:22, 3, :], conv_b.rearrange("o (j p) -> (o j) p", p=128), w=[t_cw])
        for kk in range(4):
            b = nextps(0, 7)
            P.tr(PS[b][:, 0:22], cwr[0:22, kk, :], ident[0:22, 0:22], r=[t_cw, t_const], w=[tPS[b]])
            P.copy("dve", cw[:, kk, :], PS[b][:, 0:22], r=[tPS[b]], w=[t_cw])
        xt2 = [sb(FF, "fx%d" % i, [128, D], F32) for i in range(2)]
        t_xt = trks(2)
        xn2 = [sb(FF, "fn%d" % i, [128, D], BF16) for i in range(2)]
        t_xn = trks(2)
        yo2 = [sb(FF, "yo%d" % i, [128, D], F32) for i in range(2)]
        t_yo = trks(2)
        junk = sb(FF, "fjunk", [128, D], F32)
        ss2 = [sb(FF, "fss%d" % i, [128, 4], F32) for i in range(2)]
        t_ss = trks(2)
        for i in range(NQB):
            k = i % 2
            P.dma("sp", xt2[k][:], xm_d[i * 128:(i + 1) * 128, :], r=[t_xmd[i]], w=[t_xt[k]])
            rmsnorm_tile(junk, ss2[k], t_ss[k], xt2[k][:], t_xt[k], gfr[:], t_g, xn2[k][:], t_xn[k])
            for c in range(8):
                P.tr(PST[:, c * 128:(c + 1) * 128], xn2[k][:, c * 128:(c + 1) * 128], identb[:],
                     r=[t_xn[k], t_const], w=[tPST])
            pv8 = PST[:, :].rearrange("p (c n) -> p c n", c=8)
            if i == 0:
                P.ts(H2T[:, :, 0:2], pv8[:, :, 126:128], halo[:, 0:1], ALU.mult, r=[tPST, t_const], w=[t_H2T[0]])
            else:
                P.copy(evac_eng(), H2T[:, :, 2 + (i - 1) * 128:2 + i * 128], pv8, r=[tPST], w=[t_H2T[i]])
        if stage == 6:
            dump("H2T", H2T[:, :, 0:258], t_H2T)
            dump("cw", cw[:, :, :], [t_cw])
            return done()
        WD = sb(FF, "WD", [128, NFF, D], BF16)
        t_WD = Trk()
        wdv = w_down.rearrange("(j p) n -> p j n", p=128)
        P.dma_multi("pool", [(WD[:, j0:min(j0 + 6, NFF), :], wdv[:, j0:min(j0 + 6, NFF), :]) for j0 in range(0, NFF, 6)],
                    w=[t_WD])
        AT = sb(FF, "AT", [128, NFF, 1024], BF16)
        t_AT = trks(NFF)
        wu = [sb(FF, "wu%d" % k, [128, 8, 256], BF16) for k in range(2)]
        t_wu = trks(2)
        U = [sb(FF, "U%d" % k, [128, 1026], F32) for k in range(2)]
        t_U = trks(2)
        C1s = [sb(FF, "C1%d" % k, [128, 1024], F32) for k in range(2)]
        C2s = [sb(FF, "C2%d" % k, [128, 1024], F32) for k in range(2)]
        t_C1s, t_C2s = trks(2), trks(2)
        FG = [sb(FF, "FG%d" % k, [128, 512], F32) for k in range(2)]
        t_FG = trks(2)

        def gelu2(out, t_out, xin, t_x, n):
            a, b2 = FG[0][:, 0:n], FG[1][:, 0:n]
            P.tt(a, xin, xin, ALU.mult, r=[t_x], w=[t_FG[0]])
            P.ts(a, a, 0.044715, ALU.mult, 1.0, ALU.add, r=[t_FG[0]], w=[t_FG[0]])
            P.tt(a, a, xin, ALU.mult, r=[t_FG[0], t_x], w=[t_FG[0]])
            P.act(b2, a, AF.Sigmoid, r=[t_FG[0]], w=[t_FG[1]], scale=1.5957691216057308)
            P.tt(out, b2, xin, ALU.mult, r=[t_FG[1], t_x], w=t_out)

        jrr = [0]
        for th in range(2):
            base = 2 + th * 1024
            hts = [t_H2T[c] for c in range(max(0, th * 8), th * 8 + 9)]
            for j in range(NFF):
                k = jrr[0] % 2
                jrr[0] += 1
                C1, C2, t_C1, t_C2 = C1s[k], C2s[k], t_C1s[k], t_C2s[k]
                P.dma_multi("pool", [(wu[k][:, :, 0:128], w_up[:, j * 128:(j + 1) * 128].rearrange("(c p) n -> p c n", p=128)),
                                     (wu[k][:, :, 128:256],
                                      w_up[:, DFF + j * 128:DFF + (j + 1) * 128].rearrange("(c p) n -> p c n", p=128))],
                            w=[t_wu[k]])
                b = nextps(0, 7)
                for dm in range(8):
                    P.mm(PS[b][:, 0:2], wu[k][:, dm, 0:128], H2T[:, dm, base - 2:base], dm == 0, dm == 7,
                         r=[t_wu[k]] + hts, w=[tPS[b]])
                P.copy("dve", U[k][:, 0:2], PS[b][:, 0:2], r=[tPS[b]], w=[t_U[k]])
                for nt in range(2):
                    b = nextps(0, 7)
                    for dm in range(8):
                        P.mm(PS[b][:, 0:512], wu[k][:, dm, 0:128], H2T[:, dm, base + nt * 512:base + (nt + 1) * 512],
                             dm == 0, dm == 7, r=[t_wu[k]] + hts, w=[tPS[b]])
                    P.copy("act", U[k][:, 2 + nt * 512:2 + (nt + 1) * 512], PS[b][:, 0:512], r=[tPS[b]], w=[t_U[k]])
                bg = []
                for nt in range(2):
                    b = nextps(0, 7)
                    bg.append(b)
                    for dm in range(8):
                        P.mm(PS[b][:, 0:512], wu[k][:, dm, 128:256], H2T[:, dm, base + nt * 512:base + (nt + 1) * 512],
                             dm == 0, dm == 7, r=[t_wu[k]] + hts, w=[tPS[b]])
                P.ts(C1[:, :], U[k][:, 2:1026], cw[:, 2, j:j + 1], ALU.mult, cw[:, 3, j:j + 1], ALU.add,
                     r=[t_U[k], t_cw], w=[t_C1])
                P.stt(C1[:, :], U[k][:, 1:1025], cw[:, 1, j:j + 1], C1[:, :], ALU.mult, ALU.add,
                      r=[t_U[k], t_cw, t_C1], w=[t_C1])
                P.stt(C1[:, :], U[k][:, 0:1024], cw[:, 0, j:j + 1], C1[:, :], ALU.mult, ALU.add,
                      r=[t_U[k], t_cw, t_C1], w=[t_C1])
                P.act(C2[:, :], C1[:, :], AF.Gelu_apprx_tanh, r=[t_C1], w=[t_C2])
                for nt in range(2):
                    sl = slice(nt * 512, (nt + 1) * 512)
                    P.tt(AT[:, j, sl], C2[:, sl], PS[bg[nt]][:, 0:512], ALU.mult, r=[t_C2, tPS[bg[nt]]], w=[t_AT[j]])
            if stage == 7:
                dump("AT", AT[:, 0:2, :], t_AT)
                return done()
            for kt in range(8):
                i = 1 + th * 8 + kt
                k = kt % 2
                P.dma("sp", xt2[k][:], xm_d[i * 128:(i + 1) * 128, :], r=[t_xmd[i]], w=[t_xt[k]])
                for half in range(2):
                    b = nextps(0, 7)
                    for j in range(NFF):
                        P.mm(PS[b][:, 0:512], AT[:, j, kt * 128:(kt + 1) * 128], WD[:, j, half * 512:(half + 1) * 512],
                             j == 0, j == NFF - 1, r=[t_AT[j], t_WD], w=[tPS[b]])
                    P.tt(xt2[k][:, half * 512:(half + 1) * 512], PS[b][:, 0:512], xt2[k][:, half * 512:(half + 1) * 512],
                         ALU.add, r=[tPS[b], t_xt[k]], w=[t_xt[k]])
                rmsnorm_tile(junk, ss2[k], t_ss[k], xt2[k][:], t_xt[k], gfin[:], t_g, yo2[k][:], t_yo[k])
                P.dma("sp", y[(th * 8 + kt) * 128:(th * 8 + kt + 1) * 128, :], yo2[k][:], r=[t_yo[k]])
    return done()


W_NAMES = ["g_mix", "w_in", "pe_cmp_k", "w_cmp_k1", "w_cmp_k2", "pe_cmp_v", "w_cmp_v1", "w_cmp_v2",
           "w_proj_nsa", "w_proj_dil", "w_out", "g_ffn", "w_up", "conv_w", "conv_b", "w_down", "g_final"]


def make_in_maps(inputs, cores):
    x = np.asarray(inputs["x"], dtype=np.float32)
    shared = {}
    for n in W_NAMES:
        a = np.asarray(inputs[n], dtype=np.float32)
        if n == "g_final":
            a = a.reshape(1, D)
        elif a.shape[0] == 1:
            a = a[0]
        if a.ndim == 1:
            a = a.reshape(1, -1)
        shared[n] = np.ascontiguousarray(a)
    tabs = [const_tables(0), const_tables(1)]
    maps = []
    for (b, hf) in cores:
        m = dict(shared)
        if hf == 1:
            xl = x[b]
        else:
            xl = np.concatenate([np.zeros((2048, D), np.float32), x[b, :2048]], axis=0)
        m["x"] = np.ascontiguousarray(xl)
        for k, v in tabs[hf].items():
            m["c_" + k] = v
        maps.append(m)
    return maps


_NC_CACHE = {}


N_LAUNCH = 1


def kernel(**inputs):
    if "nc" not in _NC_CACHE:
        _NC_CACHE["nc"] = build()
    nc = _NC_CACHE["nc"]
    cores = [(b, hf) for b in range(4) for hf in range(2)]
    out = np.zeros((4, T, D), np.float32)
    per = len(cores) // N_LAUNCH
    for li in range(N_LAUNCH):
        cs = cores[li * per:(li + 1) * per]
        maps = make_in_maps(inputs, cs)
        res = run_bass_kernel_spmd(nc, maps, core_ids=list(range(len(cs))))
        for ci, (b, hf) in enumerate(cs):
            out[b, hf * 2048:(hf + 1) * 2048, :] = res.results[ci]["y"]
    return out
```

```python
import contextlib
import numpy as np
import ml_dtypes
import concourse.bass as bass
import concourse.mybir as mybir
from concourse.bass_utils import run_bass_kernel_spmd

F32 = mybir.dt.float32
BF16 = mybir.dt.bfloat16
AF = mybir.ActivationFunctionType
ALU = mybir.AluOpType
AX = mybir.AxisListType

D = 1024
T = 4096
NCH = 32
QB0 = 15
NQB = 17
NQ = NQB * 128
Q0 = QB0 * 128
DFF = 2816
NFF = 22
DIN = 5656
NEG = -30000.0
DIL = ((128, 1), (512, 4), (2048, 16))
NDS = 48
NDS_HW = 32
SW_LIMIT = 300


class Trk:
    __slots__ = ("w", "r", "excl")

    def __init__(self, excl=False):
        self.w = []
        self.r = []
        self.excl = excl


def trks_ex(n):
    return [Trk(True) for _ in range(n)]


def trks(n):
    return [Trk() for _ in range(n)]


class Prog:
    def __init__(self):
        self.nc = bass.Bass("TRN2", target_bir_lowering=False)
        nc = self.nc
        self.es = contextlib.ExitStack()
        self.E = {"pe": nc.tensor, "act": nc.scalar, "dve": nc.vector, "pool": nc.gpsimd, "sp": nc.sync}
        self.sem = {k: self.es.enter_context(nc.semaphore("s_" + k)) for k in self.E}
        self.cnt = {k: 0 for k in self.E}
        self.seen = {k: {} for k in self.E}
        self.seen_seq = {k: {} for k in self.E}
        self.opseq = {k: 0 for k in self.E}
        self.last_ins = {k: None for k in self.E}
        self.sigmap = {k: [] for k in self.E}
        self.dsem = [self.es.enter_context(nc.semaphore("d%d" % i)) for i in range(NDS)]
        self.dcnt = [0] * NDS
        self.dnext = 0
        self.dnext_sw = NDS_HW
        self.sw_out = []
        self.ninst = 0

    def _resolve(self, key, seq):
        sm = self.sigmap[key]
        lo, hi = 0, len(sm)
        while lo < hi:
            mid = (lo + hi) // 2
            if sm[mid][0] >= seq:
                hi = mid
            else:
                lo = mid + 1
        if lo < len(sm):
            return sm[lo][1]
        self.last_ins[key].then_inc(self.sem[key], 1)
        self.cnt[key] += 1
        sm.append((self.opseq[key], self.cnt[key]))
        return self.cnt[key]

    def _wait(self, eng, dep):
        key, val = dep
        if key == "pe" and eng == "pe":
            return
        if isinstance(key, str):
            if self.seen_seq[eng].get(key, 0) >= val:
                return
            self.seen_seq[eng][key] = val
            val = self._resolve(key, val)
        if self.seen[eng].get(key, 0) >= val:
            return
        self.seen[eng][key] = val
        sem = self.sem[key] if isinstance(key, str) else self.dsem[key[1]]
        self.E[eng].wait_ge(sem, val)

    def _deps(self, eng, r, w):
        deps = set()
        for t in r:
            deps.update(t.w)
            if t.excl:
                deps.update(d for d in t.r if d[0] != eng)
        for t in w:
            deps.update(t.w)
            deps.update(t.r)
        best = {}
        for (k, v) in deps:
            if best.get(k, 0) < v:
                best[k] = v
        for k in sorted(best, key=str):
            self._wait(eng, (k, best[k]))

    def _mark(self, toks, r, w):
        for t in w:
            t.w = list(toks)
            t.r = []
        for t in r:
            if t not in w:
                t.r.extend(toks)

    def op(self, eng, fn, r=(), w=()):
        self._deps(eng, r, w)
        ins = fn(self.E[eng])
        self.opseq[eng] += 1
        self.last_ins[eng] = ins
        self.ninst += 1
        tok = (eng, self.opseq[eng])
        self._mark([tok], r, w)
        return tok

    def dma(self, q, out, in_, r=(), w=()):
        return self.dma_multi(q, [(out, in_)], r=r, w=w)

    def dma_multi(self, q, pieces, r=(), w=()):
        self._deps(q, r, w)
        toks = []
        for (out, in_) in pieces:
            if q == "pool":
                nd = 1
                for d_ in list(out.shape)[:-1]:
                    nd *= int(d_)
                nd = nd // 16 + 2
                while self.sw_out and sum(n_ for (_, n_) in self.sw_out) + nd > SW_LIMIT:
                    tok0, _ = self.sw_out.pop(0)
                    self._wait(q, tok0)
                j = self.dnext_sw
                self.dnext_sw = NDS_HW + (j + 1 - NDS_HW) % (NDS - NDS_HW)
            else:
                j = self.dnext
                self.dnext = (j + 1) % NDS_HW
            if self.dcnt[j] > 0:
                self._wait(q, (("d", j), 16 * self.dcnt[j]))
            self.dcnt[j] += 1
            self.E[q].dma_start(out=out, in_=in_).then_inc(self.dsem[j], 16)
            self.ninst += 1
            toks.append((("d", j), 16 * self.dcnt[j]))
            if q == "pool":
                self.sw_out.append((toks[-1], nd))
        self._mark(toks, r, w)
        return toks

    def barrier(self):
        for e in self.E:
            for e2 in self.E:
                if e2 != e and self.opseq[e2] > 0:
                    self._wait(e, (e2, self.opseq[e2]))
            for j in range(NDS):
                if self.dcnt[j] > 0:
                    self._wait(e, (("d", j), 16 * self.dcnt[j]))

    def mm(self, out, lhsT, rhs, start, stop, r=(), w=(), skip=False):
        if skip:
            return self.op("pe", lambda e: e.matmul(out, lhsT=lhsT, rhs=rhs, start=start, stop=stop,
                                                    skip_group_check=True), r=r, w=w)
        return self.op("pe", lambda e: e.matmul(out, lhsT=lhsT, rhs=rhs, start=start, stop=stop), r=r, w=w)

    def tr(self, out, in_, ident, r=(), w=()):
        return self.op("pe", lambda e: e.transpose(out, in_, ident), r=r, w=w)

    def act(self, out, in_, func, r=(), w=(), **kw):
        return self.op("act", lambda e: e.activation(out=out, in_=in_, func=func, **kw), r=r, w=w)

    def copy(self, eng, out, in_, r=(), w=(), scale=None):
        if eng == "act":
            if scale is None:
                return self.act(out, in_, AF.Copy, r=r, w=w)
            return self.act(out, in_, AF.Copy, r=r, w=w, scale=float(scale))
        if scale is None:
            return self.op(eng, lambda e: e.tensor_copy(out=out, in_=in_), r=r, w=w)
        return self.op(eng, lambda e: e.tensor_scalar(out=out, in0=in_, scalar1=float(scale), scalar2=None,
                                                      op0=ALU.mult), r=r, w=w)

    def tt(self, out, in0, in1, op, r=(), w=(), eng="dve"):
        return self.op(eng, lambda e: e.tensor_tensor(out=out, in0=in0, in1=in1, op=op), r=r, w=w)

    def ts(self, out, in0, s1, op0, s2=None, op1=None, r=(), w=(), eng="dve"):
        if op1 is None:
            return self.op(eng, lambda e: e.tensor_scalar(out=out, in0=in0, scalar1=s1, scalar2=None, op0=op0),
                           r=r, w=w)
        return self.op(eng, lambda e: e.tensor_scalar(out=out, in0=in0, scalar1=s1, scalar2=s2, op0=op0, op1=op1),
                       r=r, w=w)

    def stt(self, out, in0, scalar, in1, op0, op1, r=(), w=()):
        return self.op("dve", lambda e: e.scalar_tensor_tensor(out=out, in0=in0, scalar=scalar, in1=in1,
                                                               op0=op0, op1=op1), r=r, w=w)


def alibi(n):
    return (2.0 ** (-8.0 * np.arange(1, n + 1) / n)).astype(np.float64)


def bf(a):
    return np.asarray(a, dtype=np.float32).astype(ml_dtypes.bfloat16)


def const_tables(hf):
    c = {}
    c["ident"] = np.eye(128, dtype=np.float32)
    c["identb"] = bf(np.eye(128))
    pos = np.arange(T)
    gpos = pos if hf == 1 else pos - 2048
    valid = (gpos >= 0).astype(np.float32)
    c["validtm"] = bf(valid.reshape(NCH, 128).T)
    vd = np.zeros((128, 3, 32), np.float32)
    for g, (w, r) in enumerate(DIL):
        npc = 32 // r
        for rho in range(r):
            for jc in range(npc):
                tok = rho + r * (128 * jc + np.arange(128))
                vd[:, g, rho * npc + jc] = valid[tok]
    c["validd"] = bf(vd)
    c["halo"] = np.full((128, 1), 1.0 if hf == 1 else 0.0, np.float32)
    ea = np.zeros((128, T), np.float32)
    ea[pos // 64, pos] = 1.0
    ea[64] = 128.0 * (pos // 128)
    ea[65] = pos % 128
    ea[66] = 1.0
    ea[67] = 1.0
    c["ea"] = bf(ea)
    sl = alibi(8)
    rbc = np.zeros((4, NQB, 2, 4, 128), np.float32)
    ql = np.arange(128)
    for i in range(NQB):
        qb = QB0 + i
        for g in range(2):
            for r in range(4):
                s = sl[4 * g + r]
                rbc[0, i, g, r, :] = s
                rbc[1, i, g, r, :] = s
                rbc[2, i, g, r, :] = -s * 128.0 * qb
                rbc[3, i, g, r, :] = -s * ql
    c["rbc"] = bf(rbc)
    kk = np.arange(128)[:, None]
    qq = np.arange(128)[None, :]
    caus = np.where(kk <= qq, 0.0, NEG).astype(np.float32)
    acaus = np.where(kk > qq, 0.0, NEG).astype(np.float32)
    c["caus"] = np.ascontiguousarray(np.broadcast_to(caus[:, None, :], (128, 4, 128)))
    c["acaus"] = np.ascontiguousarray(np.broadcast_to(acaus[:, None, :], (128, 4, 128)))
    bc = np.zeros((NQB, 128, 2, 2, 4, 128), np.float32)
    for i in range(NQB):
        t = (QB0 + i) * 128 + np.arange(128)
        for kc in range(2):
            cc = kc * 128 + np.arange(128)
            cend = 16 * cc + 31
            cstart_g = 16 * cc - (0 if hf == 1 else 2048)
            dist = t[None, :] - cend[:, None]
            ok = (dist >= 0) & (cstart_g[:, None] >= 0) & (cc[:, None] < 255)
            for g in range(2):
                for r in range(4):
                    bc[i, :, kc, g, r, :] = np.where(ok, -sl[4 * g + r] * dist, NEG)
    c["bcmp"] = bc
    cc = np.arange(256)
    ss = np.arange(64)
    ov = np.clip(np.minimum(16 * cc[:, None] + 32, 64 * ss[None, :] + 64)
                 - np.maximum(16 * cc[:, None], 64 * ss[None, :]), 0, None) / 32.0
    c["ovl"] = bf(ov.reshape(2, 128, 64).transpose(1, 0, 2))
    ma = np.zeros((128, NQB, 64), np.float32)
    mb = np.zeros((128, NQB, 64), np.float32)
    b0 = 0 if hf == 1 else 32
    for i in range(NQB):
        t = (QB0 + i) * 128 + np.arange(128)
        cur = t // 64
        for s in range(64):
            forced = (s == cur) | (s == b0)
            future = (s > cur) | (s < b0)
            normal = (~forced) & (~future)
            ma[:, i, s] = normal
            mb[:, i, s] = np.where(future, -1.0, np.where(forced, 1.0e4, 0.0))
    c["ma"] = ma
    c["mb"] = mb
    sd = alibi(12).reshape(3, 4)
    bd = np.zeros((128, 3, 2, 4, 128), np.float32)
    for g, (w, r) in enumerate(DIL):
        for dl in range(2):
            dj = dl * 128 + qq - kk
            ok = (dj >= 0) & (dj <= 128)
            for h in range(4):
                bd[:, g, dl, h, :] = np.where(ok, -sd[g, h] * r * dj, NEG)
    c["bdil"] = bd
    s65 = np.zeros((65, 64), np.float32)
    s65[64, :] = 1.0
    c["s65"] = s65
    return c


def build(stage=99):
    P = Prog()
    with P.es:
        return _build(P, stage)


def _build(P, stage):
    nc = P.nc
    es = P.es

    def dram_in(name, shape, dt=F32):
        return nc.dram_tensor(name, list(shape), dt, kind="ExternalInput").ap()

    def sb(stack, name, shape, dt):
        return stack.enter_context(nc.sbuf_tensor(name, list(shape), dt))

    def ps(stack, name, shape, dt):
        return stack.enter_context(nc.psum_tensor(name, list(shape), dt))

    x = dram_in("x", [T, D])
    g_mix = dram_in("g_mix", [1, D])
    w_in = dram_in("w_in", [D, DIN])
    pe_k = dram_in("pe_cmp_k", [32, 64])
    w_k1 = dram_in("w_cmp_k1", [2048, 128])
    w_k2 = dram_in("w_cmp_k2", [128, 64])
    pe_v = dram_in("pe_cmp_v", [32, 64])
    w_v1 = dram_in("w_cmp_v1", [2048, 128])
    w_v2 = dram_in("w_cmp_v2", [128, 64])
    w_pa = dram_in("w_proj_nsa", [512, D])
    w_pb = dram_in("w_proj_dil", [256, D])
    w_out = dram_in("w_out", [D, D])
    g_ffn = dram_in("g_ffn", [1, D])
    w_up = dram_in("w_up", [D, 2 * DFF])
    conv_w = dram_in("conv_w", [3, DFF])
    conv_b = dram_in("conv_b", [1, DFF])
    w_down = dram_in("w_down", [DFF, D])
    g_fin = dram_in("g_final", [1, D])
    c_ident = dram_in("c_ident", [128, 128])
    c_identb = dram_in("c_identb", [128, 128], BF16)
    c_validtm = dram_in("c_validtm", [128, 32], BF16)
    c_validd = dram_in("c_validd", [128, 3, 32], BF16)
    c_halo = dram_in("c_halo", [128, 1])
    c_ea = dram_in("c_ea", [128, T], BF16)
    c_rbc = dram_in("c_rbc", [4, NQB, 2, 4, 128], BF16)
    c_caus = dram_in("c_caus", [128, 4, 128])
    c_acaus = dram_in("c_acaus", [128, 4, 128])
    c_bcmp = dram_in("c_bcmp", [NQB, 128, 2, 2, 4, 128])
    c_ovl = dram_in("c_ovl", [128, 2, 64], BF16)
    c_ma = dram_in("c_ma", [128, NQB, 64])
    c_mb = dram_in("c_mb", [128, NQB, 64])
    c_bdil = dram_in("c_bdil", [128, 3, 2, 4, 128])
    c_s65 = dram_in("c_s65", [65, 64])
    y = nc.dram_tensor("y", [2048, D], F32, kind="ExternalOutput").ap()
    xm_d = nc.dram_tensor("xm_scratch", [NQ, D], F32, kind="Internal").ap()
    t_xmd = trks(NQB)

    def dump(name, ap, trk):
        o = nc.dram_tensor("dbg_" + name, list(ap.shape), ap.dtype, kind="ExternalOutput").ap()
        P.dma("sp", o, ap, r=trk)

    def done():
        for j in range(NDS):
            if P.dcnt[j] > 0:
                P._wait("sp", (("d", j), 16 * P.dcnt[j]))
        return nc

    PS = [ps(es, "ps%d" % i, [128, 512], F32) for i in range(8)]
    PST = PS[7][:, :].bitcast(BF16)
    tPS = trks_ex(8)
    tPST = tPS[7]
    psrr = [0]

    def nextps(lo, hi):
        k = lo + psrr[0] % (hi - lo)
        psrr[0] += 1
        return k

    def v4(ap):
        return ap.rearrange("p (a b) -> p a b", a=4)

    ident = sb(es, "ident", [128, 128], F32)
    identb = sb(es, "identb", [128, 128], BF16)
    halo = sb(es, "halo", [128, 1], F32)
    t_const = Trk()
    P.dma("sp", ident[:], c_ident[:, :], w=[t_const])
    P.dma("sp", identb[:], c_identb[:, :], w=[t_const])
    P.dma("sp", halo[:], c_halo[:, :], w=[t_const])

    evac_rr = [0]

    def evac_eng():
        evac_rr[0] += 1
        return "act" if evac_rr[0] % 2 else "dve"

    def rmsnorm_tile(junk, ss, t_s, xt, t_x, gt, t_g, out_t, t_out):
        P.act(junk[:], xt, AF.Square, r=[t_x], w=[t_s], accum_out=ss[:, 0:1])
        P.ts(ss[:, 1:2], ss[:, 0:1], 1.0 / D, ALU.mult, 1e-6, ALU.add, r=[t_s], w=[t_s])
        P.act(ss[:, 2:3], ss[:, 1:2], AF.Sqrt, r=[t_s], w=[t_s])
        P.op("dve", lambda e: e.reciprocal(out=ss[:, 3:4], in_=ss[:, 2:3]), r=[t_s], w=[t_s])
        P.stt(out_t, xt, ss[:, 3:4], gt, ALU.mult, ALU.mult, r=[t_x, t_g, t_s], w=[t_out])

    def chunks_of(tok0, n):
        return list(range(tok0 // 128, (tok0 + n - 1) // 128 + 1))

    hT_d = nc.dram_tensor("hT_scratch", [128, 8, T], BF16, kind="Internal").ap()
    t_hTd = trks(2)
    with contextlib.ExitStack() as PA:
        OAT = sb(PA, "OAT", [128, 4, NQ], BF16)
        t_OAT = trks(NQB)
        NPT = 5
        PT = [sb(PA, "PT%d" % i, [128, 512], BF16) for i in range(NPT)]
        t_PT = trks(NPT)
        ptrr = [0]
        NTMP = 3
        TMP = [sb(PA, "TMP%d" % i, [128, 512], F32) for i in range(NTMP)]
        t_TMP = trks(NTMP)
        tmprr = [0]

        def next_pt():
            k = ptrr[0] % NPT
            ptrr[0] += 1
            return k

        def next_tmp():
            k = tmprr[0] % NTMP
            tmprr[0] += 1
            return k

        GT = [sb(PA, "GT%d" % i, [128, 512], F32) for i in range(2)]
        t_GT = trks(2)

        def gelu_tanh(out, t_out, xin, t_x, n):
            a, b2 = GT[0][:, 0:n], GT[1][:, 0:n]
            P.tt(a, xin, xin, ALU.mult, r=[t_x], w=[t_GT[0]])
            P.ts(a, a, 0.044715, ALU.mult, 1.0, ALU.add, r=[t_GT[0]], w=[t_GT[0]])
            P.tt(a, a, xin, ALU.mult, r=[t_GT[0], t_x], w=[t_GT[0]])
            P.act(b2, a, AF.Sigmoid, r=[t_GT[0]], w=[t_GT[1]], scale=1.5957691216057308)
            P.tt(out, b2, xin, ALU.mult, r=[t_GT[1], t_x], w=t_out)

        def load_wt(dst, t_dst, src, col_specs):
            o = 0
            pieces = []
            for (c0, n) in col_specs:
                pieces.append((dst[:, :, o:o + n], src[:, c0:c0 + n].rearrange("(c p) n -> p c n", p=128)))
                o += n
            P.dma_multi("pool", pieces, w=[t_dst])

        def phase1(ph, hdst, t_hdst, tiles):
            xt2 = [sb(ph, "xt%d" % i, [128, D], F32) for i in range(2)]
            t_xt = trks(2)
            xn2 = [sb(ph, "xn%d" % i, [128, D], BF16) for i in range(2)]
            t_xn = trks(2)
            gmr = sb(ph, "gmr", [128, D], F32)
            t_gm = Trk()
            junk = sb(ph, "junk", [128, D], F32)
            ss2 = [sb(ph, "ss%d" % i, [128, 4], F32) for i in range(2)]
            t_ss = trks(2)
            P.dma("sp", gmr[:], g_mix[0:1, :].partition_broadcast(128), w=[t_gm])

            def run(tiles_):
                for kk, t in enumerate(tiles_):
                    k = kk % 2
                    P.dma("sp", xt2[k][:], x[t * 128:(t + 1) * 128, :], w=[t_xt[k]])
                    rmsnorm_tile(junk, ss2[k], t_ss[k], xt2[k][:], t_xt[k], gmr[:], t_gm, xn2[k][:], t_xn[k])
                    for c in range(8):
                        P.tr(PST[:, c * 128:(c + 1) * 128], xn2[k][:, c * 128:(c + 1) * 128], identb[:],
                             r=[t_xn[k], t_const], w=[tPST])
                    P.copy(evac_eng(), hdst[:, :, kk * 128:(kk + 1) * 128],
                           PST[:, :].rearrange("p (c n) -> p c n", c=8), r=[tPST], w=[t_hdst[kk]])
            return run

        with contextlib.ExitStack() as nsa:
            QN = [sb(nsa, "QN%d" % g, [128, 4, NQ], BF16) for g in range(2)]
            t_QN = trks(NQB)
            KS = sb(nsa, "KS", [128, T], BF16)
            KW = sb(nsa, "KW", [128, T], BF16)
            t_KS = trks(NCH)
            t_KW = trks(NCH)
            VS = sb(nsa, "VS", [128, NCH, 2, 66], BF16)
            VW = sb(nsa, "VW", [128, NCH, 2, 66], BF16)
            t_VS = trks(NCH)
            t_VW = trks(NCH)
            GA = sb(nsa, "GA", [128, NQB, 24], F32)
            t_GA = trks(NQB)
            KC = sb(nsa, "KC", [128, 256], BF16)
            t_KC = Trk()
            VCX = sb(nsa, "VCX", [128, 2, 2, 130], BF16)
            t_VCX = Trk()
            vtm = sb(nsa, "vtm", [128, 32], BF16)
            t_vtm = Trk()
            P.dma("sp", vtm[:], c_validtm[:, :], w=[t_vtm])
            P.copy("dve", VS[:, :, :, 64], vtm[:].unsqueeze(2).broadcast_to([128, NCH, 2]), r=[t_vtm], w=t_VS)
            P.copy("dve", VW[:, :, :, 64], vtm[:].unsqueeze(2).broadcast_to([128, NCH, 2]), r=[t_vtm], w=t_VW)
            P.op("dve", lambda e: e.memset(QN[0][64:128, :, :], 0.0), w=t_QN)
            P.op("dve", lambda e: e.memset(QN[1][0:64, :, :], 0.0), w=t_QN)

            with contextlib.ExitStack() as ph:
                hTh = sb(ph, "hTh", [128, 8, 2048], BF16)
                t_hTh = trks(16)
                wq = sb(ph, "wq", [128, 8, 512], BF16)
                wk = sb(ph, "wk", [128, 8, 512], BF16)
                wv = sb(ph, "wv", [128, 8, 280], BF16)
                t_wq, t_wk, t_wv = Trk(), Trk(), Trk()
                load_wt(wq, t_wq, w_in, [(0, 64), (256, 64), (64, 64), (320, 64), (128, 64), (384, 64), (192, 64),
                                         (448, 64)])
                load_wt(wk, t_wk, w_in, [(768, 128), (1024, 128), (512, 128), (640, 128)])
                load_wt(wv, t_wv, w_in, [(896, 128), (1152, 128), (1280, 24)])
                SRD = [sb(ph, "SRD%d" % kv, [128, 2, 16, 256], BF16) for kv in range(2)]
                t_SRD = trks(2)
                for kv in range(2):
                    P.op("dve", lambda e, kv=kv: e.memset(SRD[kv][:, 1, :, 255:256], 0.0), w=[t_SRD[kv]])
                with contextlib.ExitStack() as p1s:
                    run_p1 = phase1(p1s, hTh, t_hTh, None)
                    for hh_ in range(2):
                        run_p1(list(range(16 * hh_, 16 * hh_ + 16)))
                        if stage != 23.1:
                            P.dma("sp", hT_d[:, :, hh_ * 2048:(hh_ + 1) * 2048], hTh[:, :, :], r=t_hTh,
                                  w=[t_hTd[hh_]])
                        if stage == 1 and hh_ == 0:
                            dump("hTh", hTh[:, :, Q0:2048], t_hTh)
                            return done()

                        def fm(wt, t_w, wc0, lt0, n, dst, t_dst, scale=None, eng=None, dst2=None):
                            b = nextps(0, 4)
                            for dm in range(8):
                                P.mm(PS[b][:, 0:n], wt[:, dm, wc0:wc0 + 128], hTh[:, dm, lt0:lt0 + n],
                                     start=(dm == 0), stop=(dm == 7),
                                     r=[t_w] + [t_hTh[c] for c in chunks_of(lt0, n)], w=[tPS[b]])
                            return b

                        if stage == 23.2:
                            continue
                        qtiles = [(Q0, 128, 0)] if hh_ == 0 else [(n0, 512, 128 + n0) for n0 in range(0, 2048, 512)]
                        for r_ in range(4):
                            for (lt0, n, qoff) in qtiles:
                                b = fm(wq, t_wq, 128 * r_, lt0, n, None, None)
                                tq = [t_QN[c] for c in chunks_of(qoff, n)]
                                P.copy("act", QN[0][0:64, r_, qoff:qoff + n], PS[b][0:64, 0:n], r=[tPS[b]], w=tq,
                                       scale=0.125)
                                P.copy("dve", QN[1][64:128, r_, qoff:qoff + n], PS[b][64:128, 0:n], r=[tPS[b]], w=tq,
                                       scale=0.125)
                        for n0 in range(0, 2048, 512):
                            g0 = hh_ * 2048 + n0
                            b = fm(wk, t_wk, 0, n0, 512, None, None)
                            P.copy(evac_eng(), KS[:, g0:g0 + 512], PS[b][:, 0:512], r=[tPS[b]],
                                   w=[t_KS[c] for c in chunks_of(g0, 512)])
                            b = fm(wk, t_wk, 128, n0, 512, None, None)
                            P.copy(evac_eng(), KW[:, g0:g0 + 512], PS[b][:, 0:512], r=[tPS[b]],
                                   w=[t_KW[c] for c in chunks_of(g0, 512)])
                            for kv in range(2):
                                b = fm(wk, t_wk, 256 + 128 * kv, n0, 512, None, None)
                                c0 = g0 // 16
                                pv_ = PS[b][:, 0:512].rearrange("d (c p) -> d p c", p=16)
                                P.copy("dve", SRD[kv][:, 0, :, c0:c0 + 32], pv_, r=[tPS[b]], w=[t_SRD[kv]])
                                if g0 == 0:
                                    P.copy("dve", SRD[kv][:, 1, :, 0:31],
                                           PS[b][:, 16:512].rearrange("d (c p) -> d p c", p=16),
                                           r=[tPS[b]], w=[t_SRD[kv]])
                                else:
                                    P.copy("dve", SRD[kv][:, 1, :, c0 - 1:c0 + 31], pv_, r=[tPS[b]], w=[t_SRD[kv]])
                        for tl in range(16):
                            t = hh_ * 16 + tl
                            b = nextps(0, 4)
                            for dm in range(8):
                                P.mm(PS[b][:, 0:280], hTh[:, dm, tl * 128:(tl + 1) * 128], wv[:, dm, 0:280],
                                     start=(dm == 0), stop=(dm == 7), r=[t_wv, t_hTh[tl]], w=[tPS[b]])
                            P.copy("dve", VS[:, t, :, 0:64], PS[b][:, 0:128].rearrange("p (g d) -> p g d", g=2),
                                   r=[tPS[b]], w=[t_VS[t]])
                            P.copy("dve", VW[:, t, :, 0:64], PS[b][:, 128:256].rearrange("p (g d) -> p g d", g=2),
                                   r=[tPS[b]], w=[t_VW[t]])
                            if t >= QB0:
                                P.act(GA[:, t - QB0, :], PS[b][:, 256:280], AF.Sigmoid, r=[tPS[b]],
                                      w=[t_GA[t - QB0]])
                P.barrier()
                if stage == 23.2:
                    dump("hTh", hTh[:, :, 0:128], t_hTh)
                    return done()
                if stage in (23, 23.1):
                    dump("QN0", QN[0][:, :, 0:256], t_QN)
                    dump("QN1", QN[1][:, :, 0:256], t_QN)
                    dump("KS", KS[:, Q0:Q0 + 256], t_KS)
                    dump("KW", KW[:, Q0:Q0 + 256], t_KW)
                    dump("VS", VS[:, 16:18, :, 0:65], t_VS)
                    dump("GA", GA[:, 0:2, :], t_GA)
                    return done()

                W1 = sb(ph, "W1", [128, 32, 128], BF16)
                W2 = sb(ph, "W2", [128, 64], BF16)
                peT = sb(ph, "peT", [128, 64], F32)
                peTb = sb(ph, "peTb", [128, 32, 2], BF16)
                hb = sb(ph, "hb", [128, 1], F32)
                HT = sb(ph, "HT", [128, 2, 256], BF16)
                t_W1 = Trk()
                t_W2 = Trk()
                t_pe = Trk()
                t_hb = Trk()
                t_HT = Trk()
                ovl = sb(ph, "ovl", [128, 2, 64], BF16)
                t_ovl = Trk()
                P.dma("sp", ovl[:], c_ovl[:, :, :], w=[t_ovl])
                for kv in range(2):
                    w1d, w2d, ped = ((w_k1, w_k2, pe_k), (w_v1, w_v2, pe_v))[kv]
                    w1v = w1d.rearrange("(p d) h -> d p h", d=64)
                    P.dma_multi("pool", [(W1[0:64, :, :], w1v), (W1[64:128, :, :], w1v)], w=[t_W1])
                    P.dma("pool", W2[:, :], w2d[:, :], w=[t_W2])
                    P.dma("sp", peT[0:32, 0:64], ped[:, :], w=[t_pe])
                    b = nextps(0, 4)
                    P.tr(PS[b][0:64, 0:32], peT[0:32, 0:64], ident[0:32, 0:32], r=[t_pe, t_const], w=[tPS[b]])
                    P.copy("dve", peTb[0:64, :, :], PS[b][0:64, 0:32].unsqueeze(2).broadcast_to([64, 32, 2]),
                           r=[tPS[b]], w=[t_pe])
                    b = nextps(0, 4)
                    for p_ in range(32):
                        P.mm(PS[b][:, 0:2], W1[0:64, p_, :], peTb[0:64, p_, :], start=(p_ == 0),
                             stop=(p_ == 31), r=[t_W1, t_pe], w=[tPS[b]])
                    P.copy("dve", hb[:, 0:1], PS[b][:, 0:1], r=[tPS[b]], w=[t_hb])
                    m = next_tmp()
                    for g in range(2):
                        b = 2 * g + (kv % 2)
                        for p_ in range(32):
                            P.mm(PS[b][:, 0:256], W1[64 * g:64 * g + 64, p_, :],
                                 SRD[kv][64 * g:64 * g + 64, p_ // 16, p_ % 16, :],
                                 start=(p_ == 0), stop=(p_ == 31), r=[t_W1, t_SRD[kv]], w=[tPS[b]])
                        P.ts(TMP[m][:, g * 256:(g + 1) * 256], PS[b][:, 0:256], hb[:, 0:1], ALU.add,
                             r=[tPS[b], t_hb], w=[t_TMP[m]])
                    gelu_tanh(HT[:, :, :].rearrange("p g c -> p (g c)"), [t_HT], TMP[m][:, :], t_TMP[m], 512)
                    if kv == 0:
                        for g in range(2):
                            b = nextps(4, 6)
                            P.mm(PS[b][0:64, 0:256], W2[:, :], HT[:, g, :], start=True, stop=True,
                                 r=[t_W2, t_HT], w=[tPS[b]])
                            P.copy("dve", KC[64 * g:64 * g + 64, :], PS[b][0:64, 0:256], r=[tPS[b]], w=[t_KC])
                    else:
                        for kc in range(2):
                            for g in range(2):
                                b = nextps(4, 6)
                                P.mm(PS[b][:, 0:64], HT[:, g, kc * 128:(kc + 1) * 128], W2[:, :], start=True,
                                     stop=True, r=[t_W2, t_HT], w=[tPS[b]])
                                P.copy("dve", VCX[:, kc, g, 0:64], PS[b][:, 0:64], r=[tPS[b]], w=[t_VCX])
                                P.copy("dve", VCX[:, kc, g, 66:130], ovl[:, kc, :], r=[t_ovl], w=[t_VCX])
                        P.op("dve", lambda e: e.memset(VCX[:, :, :, 64:65], 1.0), w=[t_VCX])
                        P.op("dve", lambda e: e.memset(VCX[:, :, :, 65:66], 0.0), w=[t_VCX])
                P.barrier()

            if stage == 2:
                dump("KC", KC[:, :], [t_KC])
                dump("VCX", VCX[:, :, :, :], [t_VCX])
                return done()

            ea = sb(nsa, "ea", [128, T], BF16)
            caus = sb(nsa, "caus", [128, 4, 128], F32)
            acaus = sb(nsa, "acaus", [128, 4, 128], F32)
            ma = sb(nsa, "ma", [128, NQB, 64], F32)
            mb = sb(nsa, "mb", [128, NQB, 64], F32)
            t_tab = Trk()
            P.dma("sp", ea[:], c_ea[:, :], w=[t_tab])
            P.dma("sp", caus[:], c_caus[:, :, :], w=[t_tab])
            P.dma("sp", acaus[:], c_acaus[:, :, :], w=[t_tab])
            P.dma("sp", ma[:], c_ma[:, :, :], w=[t_tab])
            P.dma("sp", mb[:], c_mb[:, :, :], w=[t_tab])
            RB = [[sb(nsa, "RB%d%d" % (g, k), [128, 4, 128], BF16) for k in range(2)] for g in range(2)]
            RW = [[sb(nsa, "RW%d%d" % (g, k), [128, 4, 128], BF16) for k in range(2)] for g in range(2)]
            t_RB = [[Trk() for k in range(2)] for g in range(2)]
            t_RW = [[Trk() for k in range(2)] for g in range(2)]
            for g in range(2):
                for k in range(2):
                    P.op("dve", lambda e, g=g, k=k: e.memset(RB[g][k][:, :, :], 0.0), w=[t_RB[g][k]])
                    P.op("dve", lambda e, g=g, k=k: e.memset(RW[g][k][:, :, :], 0.0), w=[t_RW[g][k]])
            Bc = [sb(nsa, "Bc%d" % k, [128, 2, 2, 4, 128], F32) for k in range(2)]
            t_Bc = trks(2)
            ONSA = [sb(nsa, "ONSA%d" % k, [128, 512], F32) for k in range(2)]
            t_ONSA = trks(2)
            sm = [sb(nsa, "sm%d" % k, [128, 32], F32) for k in range(4)]
            t_sm = trks(4)
            impn = [sb(nsa, "impn%d" % k, [128, 64], F32) for k in range(2)]
            impw = [sb(nsa, "impw%d" % k, [128, 64], F32) for k in range(2)]
            t_imp = trks(2)
            srr = [0]
            srot = [0]

            def s_tile(src_fn, kind, g, i, c):
                qs = slice(i * 128, (i + 1) * 128)
                rb = RB[g][i % 2]
                t_rb = t_RB[g][i % 2]
                rw = RW[g][i % 2]
                t_rw = t_RW[g][i % 2]
                b = (0, 1, 6, 7)[srot[0] % 4]
                srot[0] += 1
                pv = v4(PS[b][:, 0:512])
                if kind == "cmp":
                    P.mm(pv, KC[:, c * 128:(c + 1) * 128], QN[g][:, :, qs], True, True,
                         r=[t_KC, t_QN[i]], w=[tPS[b]])
                    addt = Bc[i % 2][:, c, g, :, :]
                    t_add = t_Bc[i % 2]
                else:
                    KX, t_KX = (KS, t_KS) if kind == "sel" else (KW, t_KW)
                    P.mm(pv, KX[:, c * 128:(c + 1) * 128], QN[g][:, :, qs], True, False,
                         r=[t_KX[c], t_QN[i]], w=[tPS[b]])
                    if kind == "sel":
                        P.mm(pv, ea[:, c * 128:(c + 1) * 128], rb[:, :, :], False, True,
                             r=[t_tab, t_rb], w=[tPS[b]])
                    else:
                        P.mm(pv, ea[:, c * 128:(c + 1) * 128], rw[:, :, :], False, True,
                             r=[t_tab, t_rw], w=[tPS[b]])
                    qb = QB0 + i
                    addt, t_add = None, t_tab
                    if c == qb:
                        addt = caus[:, :, :]
                    elif kind == "win" and c == qb - 4:
                        addt = acaus[:, :, :]
                k = next_pt()
                if addt is not None:
                    m = next_tmp()
                    P.tt(v4(TMP[m][:, :]), pv, addt, ALU.add, r=[tPS[b], t_add], w=[t_TMP[m]])
                    P.act(PT[k][:, :], TMP[m][:, :], AF.Exp, r=[t_TMP[m]], w=[t_PT[k]])
                else:
                    P.act(PT[k][:, :], PS[b][:, 0:512], AF.Exp, r=[tPS[b]], w=[t_PT[k]])
                return k

            def combine(i, g, bi, o_fn, d_fn, t_acc, first):
                k = srr[0] % 4
                srr[0] += 1
                s_ = sm[k]
                t_s = t_sm[k]
                for r_ in range(4):
                    P.ts(s_[:, r_:r_ + 1], d_fn(r_), 1e-30, ALU.max, r=t_acc, w=[t_s])
                P.op("dve", lambda e: e.reciprocal(out=s_[:, 4:8], in_=s_[:, 0:4]), r=[t_s], w=[t_s])
                P.tt(s_[:, 8:12], s_[:, 4:8], GA[:, i, g * 12 + bi:g * 12 + 12:3], ALU.mult,
                     r=[t_s, t_GA[i]], w=[t_s])
                on = ONSA[i % 2]
                for r_ in range(4):
                    h0 = (4 * g + r_) * 64
                    if first:
                        P.ts(on[:, h0:h0 + 64], o_fn(r_), s_[:, 8 + r_:9 + r_], ALU.mult,
                             r=t_acc + [t_s], w=[t_ONSA[i % 2]])
                    else:
                        P.stt(on[:, h0:h0 + 64], o_fn(r_), s_[:, 8 + r_:9 + r_], on[:, h0:h0 + 64], ALU.mult, ALU.add,
                              r=t_acc + [t_s], w=[t_ONSA[i % 2]])
                return s_, t_s

            PIPE = 3
            CMPB = {0: (4, 5), 1: (2, 3)}
            WINB = {0: 4, 1: 5}
            SELB = {0: 2, 1: 3}

            def topk_chain(i, g, s_, t_s):
                ba, bb_ = CMPB[g]
                t_acc = [tPS[ba], tPS[bb_]]
                im, iw, t_im = impn[g], impw[g], t_imp[g]
                for r_ in range(4):
                    src = PS[CMPB[g][r_ // 2]][:, (r_ % 2) * 130 + 66:(r_ % 2) * 130 + 130]
                    if r_ == 0:
                        P.ts(im[:, :], src, s_[:, 4:5], ALU.mult, r=t_acc + [t_s], w=[t_im])
                    else:
                        P.stt(im[:, :], src, s_[:, 4 + r_:5 + r_], im[:, :], ALU.mult, ALU.add,
                              r=t_acc + [t_s], w=[t_im])
                P.tt(im[:, :], im[:, :], ma[:, i, :], ALU.mult, r=[t_im, t_tab], w=[t_im])
                P.tt(im[:, :], im[:, :], mb[:, i, :], ALU.add, r=[t_im, t_tab], w=[t_im])
                P.op("dve", lambda e: e.max(out=s_[:, 16:24], in_=im[:, :]), r=[t_im], w=[t_s])
                P.op("dve", lambda e: e.match_replace(out=iw[:, :], in_to_replace=s_[:, 16:24], in_values=im[:, :],
                                                      imm_value=-1.0e9), r=[t_im, t_s], w=[t_im])
                P.op("dve", lambda e: e.max(out=s_[:, 24:32], in_=iw[:, :]), r=[t_im], w=[t_s])
                P.ts(s_[:, 12:13], s_[:, 31:32], 0.0, ALU.max, r=[t_s], w=[t_s])
                P.ts(iw[:, :], im[:, :], s_[:, 12:13], ALU.is_ge, r=[t_im, t_s], w=[t_im])
                P.ts(iw[:, :], iw[:, :], -1.0, ALU.add, -NEG, ALU.mult, r=[t_im], w=[t_im])

            def sel_rows(i, g):
                rb, t_rb = RB[g][i % 2], t_RB[g][i % 2]
                P.tr(PS[6][0:64, 0:128], impw[g][:, 0:64], ident[:, :], r=[t_imp[g], t_const], w=[tPS[6]])
                P.copy("act", rb[0:64, :, :], PS[6][0:64, 0:128].unsqueeze(1).broadcast_to([64, 4, 128]),
                       r=[tPS[6]], w=[t_rb])

            def part_b(i, job, k):
                kind, g, c = job
                qb = QB0 + i
                if kind == "cmp":
                    for r_ in range(4):
                        bb = CMPB[g][r_ // 2]
                        P.mm(PS[bb][:, (r_ % 2) * 130:(r_ % 2) * 130 + 130], PT[k][:, r_ * 128:(r_ + 1) * 128],
                             VCX[:, c, g, :], start=(c == 0 and r_ % 2 == 0), stop=(c == 1),
                             r=[t_PT[k], t_VCX], w=[tPS[bb]], skip=True)
                    if c == 1:
                        t_acc = [tPS[CMPB[g][0]], tPS[CMPB[g][1]]]
                        s_, t_s = combine(i, g, 0,
                                          lambda r_: PS[CMPB[g][r_ // 2]][:, (r_ % 2) * 130:(r_ % 2) * 130 + 64],
                                          lambda r_: PS[CMPB[g][r_ // 2]][:, (r_ % 2) * 130 + 64:(r_ % 2) * 130 + 65],
                                          t_acc, True)
                        topk_chain(i, g, s_, t_s)
                elif kind == "win":
                    bw = WINB[g]
                    for r_ in range(4):
                        P.mm(PS[bw][:, r_ * 65:(r_ + 1) * 65], PT[k][:, r_ * 128:(r_ + 1) * 128], VW[:, c, g, 0:65],
                             start=(c == qb - 4 and r_ == 0), stop=(c == qb), r=[t_PT[k], t_VW[c]], w=[tPS[bw]],
                             skip=True)
                    if c == qb:
                        combine(i, g, 2, lambda r_: PS[bw][:, r_ * 65:r_ * 65 + 64],
                                lambda r_: PS[bw][:, r_ * 65 + 64:r_ * 65 + 65], [tPS[bw]], False)
                else:
                    bs = SELB[g]
                    for r_ in range(4):
                        P.mm(PS[bs][:, r_ * 65:(r_ + 1) * 65], PT[k][:, r_ * 128:(r_ + 1) * 128], VS[:, c, g, 0:65],
                             start=(c == 0 and r_ == 0), stop=(c == qb), r=[t_PT[k], t_VS[c]], w=[tPS[bs]],
                             skip=True)
                    if c == qb:
                        combine(i, g, 1, lambda r_: PS[bs][:, r_ * 65:r_ * 65 + 64],
                                lambda r_: PS[bs][:, r_ * 65 + 64:r_ * 65 + 65], [tPS[bs]], False)

            for i in range(NQB):
                qb = QB0 + i
                P.dma("sp", Bc[i % 2][:], c_bcmp[i], w=[t_Bc[i % 2]])
                for g in range(2):
                    P.dma("sp", RB[g][i % 2][64:68, :, :], c_rbc[:, i, g, :, :], w=[t_RB[g][i % 2]])
                    P.dma("sp", RW[g][i % 2][64:68, :, :], c_rbc[:, i, g, :, :], w=[t_RW[g][i % 2]])
                jobs = []
                for g in range(2):
                    for kc in range(2):
                        jobs.append(("cmp", g, kc))
                for g in range(2):
                    for dl in range(4, -1, -1):
                        jobs.append(("win", g, qb - dl))
                for g in range(2):
                    for c in range(qb + 1):
                        jobs.append(("sel", g, c))
                pend = []
                for job in jobs + [None] * PIPE:
                    if job is not None:
                        kind, g, c = job
                        if kind == "sel" and c == 0:
                            sel_rows(i, g)
                        pend.append((job, s_tile(None, kind, g, i, c)))
                    if len(pend) > PIPE or (job is None and pend):
                        pj, pk = pend.pop(0)
                        part_b(i, pj, pk)
                on = ONSA[i % 2]
                for fc in range(4):
                    P.tr(PS[6][:, fc * 128:(fc + 1) * 128], on[:, fc * 128:(fc + 1) * 128], ident[:, :],
                         r=[t_ONSA[i % 2], t_const], w=[tPS[6]])
                P.copy("act", OAT[:, :, i * 128:(i + 1) * 128], v4(PS[6][:, 0:512]), r=[tPS[6]], w=[t_OAT[i]])
            P.barrier()
        if stage == 3:
            dump("OAT", OAT[:, :, :], t_OAT)
            return done()

        OBT = sb(PA, "OBT", [128, 4, NQ], BF16)
        t_OBT = trks(NQB)
        TQ0 = 1536
        with contextlib.ExitStack() as dl_:
            accD = sb(dl_, "accD", [128, 4, NQ], F32)
            t_acc = trks(NQB)
            hq = [sb(dl_, "hq%d" % k, [128, 8, 1024], BF16) for k in range(2)]
            t_hq = trks(2)
            wd = sb(dl_, "wd", [128, 8, 768], BF16)
            t_wd = Trk()
            QD = sb(dl_, "QD", [128, 2, T - TQ0], BF16)
            KD = sb(dl_, "KD", [128, 2, T], BF16)
            VT = sb(dl_, "VT", [128, 2, T], BF16)
            VD = sb(dl_, "VD", [128, NCH, 4, 66], BF16)
            BD = sb(dl_, "BD", [128, 2, 4, 128], F32)
            vdd = sb(dl_, "vdd", [128, 3, 32], BF16)
            t_QD, t_KD, t_VT, t_VD, t_BD, t_vdd = Trk(), Trk(), Trk(), Trk(), Trk(), Trk()
            P.dma("sp", vdd[:], c_validd[:, :, :], w=[t_vdd])
            hqrr = [0]
            for gd, (w_, r_) in enumerate(DIL):
                J = T // r_
                JQ = (T - TQ0) // r_
                jq0 = TQ0 // r_
                npc = J // 128
                c0w = 1304 + gd * 256
                load_wt(wd, t_wd, w_in, [(c0w, 256), (c0w + 768, 256), (c0w + 1536, 256)])
                P.dma("sp", BD[:], c_bdil[:, gd, :, :, :], w=[t_BD])
                P.copy("dve", VD[:, :, :, 64], vdd[:, gd, :].unsqueeze(2).broadcast_to([128, NCH, 4]),
                       r=[t_vdd], w=[t_VD])
                KDv = KD[:, :, :].rearrange("p a (rho j) -> p a rho j", rho=r_)
                VTv = VT[:, :, :].rearrange("p a (rho j) -> p a rho j", rho=r_)
                QDv = QD[:, :, :].rearrange("p a (rho j) -> p a rho j", rho=r_)
                for qt in range(4):
                    k = hqrr[0] % 2
                    hqrr[0] += 1
                    P.dma("sp", hq[k][:, :, :], hT_d[:, :, qt * 1024:(qt + 1) * 1024], r=t_hTd, w=[t_hq[k]])
                    for n0 in range(0, 1024, 512):
                        g0 = qt * 1024 + n0
                        for which in range(3):
                            if which == 0 and g0 < TQ0:
                                continue
                            for pr in range(2):
                                b = nextps(0, 4)
                                for dm in range(8):
                                    P.mm(PS[b][:, 0:512], wd[:, dm, which * 256 + pr * 128:which * 256 + pr * 128 + 128],
                                         hq[k][:, dm, n0:n0 + 512], start=(dm == 0), stop=(dm == 7),
                                         r=[t_wd, t_hq[k]], w=[tPS[b]])
                                src = PS[b][:, 0:512].rearrange("p (j rho) -> p rho j", rho=r_)
                                nj = 512 // r_
                                if which == 0:
                                    j0 = (g0 - TQ0) // r_
                                    P.copy(evac_eng(), QDv[:, pr, :, j0:j0 + nj], src, r=[tPS[b]], w=[t_QD],
                                           scale=0.125)
                                elif which == 1:
                                    j0 = g0 // r_
                                    P.copy(evac_eng(), KDv[:, pr, :, j0:j0 + nj], src, r=[tPS[b]], w=[t_KD])
                                else:
                                    j0 = g0 // r_
                                    P.copy(evac_eng(), VTv[:, pr, :, j0:j0 + nj], src, r=[tPS[b]], w=[t_VT])
                for ci in range(NCH):
                    for pr in range(2):
                        P.tr(PST[:, pr * 128:(pr + 1) * 128], VT[:, pr, ci * 128:(ci + 1) * 128], identb[:],
                             r=[t_VT, t_const], w=[tPST])
                    P.copy(evac_eng(), VD[:, ci, :, 0:64], PST[:, 0:256].rearrange("p (h d) -> p h d", h=4),
                           r=[tPST], w=[t_VD])
                def dil_a(rho, jb, q_lo, dls):
                    nq = 128 - q_lo
                    pts = {}
                    for dl in dls:
                        jc = jb - dl
                        kk = next_pt()
                        pts[dl] = kk
                        sb0 = 0 if dl == 0 else 4
                        for hh in range(4):
                            par, pr = hh % 2, hh // 2
                            base = 64 * par
                            kc0 = rho * J + jc * 128
                            qc0 = rho * JQ + (jb * 128 + q_lo - jq0)
                            P.mm(PS[sb0 + par][:, pr * 128 + q_lo:pr * 128 + 128],
                                 KD[base:base + 64, pr, kc0:kc0 + 128], QD[base:base + 64, pr, qc0:qc0 + nq],
                                 True, True, r=[t_KD, t_QD], w=[tPS[sb0 + par]])
                        for par in range(2):
                            m = next_tmp()
                            tv = TMP[m][:, 0:256].rearrange("p (a b) -> p a b", a=2)[:, :, q_lo:128]
                            pv_ = PS[sb0 + par][:, 0:256].rearrange("p (a b) -> p a b", a=2)[:, :, q_lo:128]
                            P.tt(tv, pv_, BD[:, dl, par::2, q_lo:128], ALU.add, r=[tPS[sb0 + par], t_BD],
                                 w=[t_TMP[m]])
                            P.act(v4(PT[kk][:, :])[:, par::2, q_lo:128], tv, AF.Exp, r=[t_TMP[m]], w=[t_PT[kk]])
                    return pts

                def dil_b(rho, jb, q_lo, dls, pts):
                    ab = 2 + ((rho * npc + jb) % 2)
                    for hh in range(4):
                        for n_, dl in enumerate(dls):
                            ci = rho * npc + (jb - dl)
                            P.mm(PS[ab][0:65, hh * 128 + q_lo:hh * 128 + 128], VD[:, ci, hh, 0:65],
                                 PT[pts[dl]][:, hh * 128 + q_lo:hh * 128 + 128],
                                 start=(n_ == 0), stop=(n_ == len(dls) - 1),
                                 r=[t_PT[pts[dl]], t_VD], w=[tPS[ab]])
                    tok0 = rho + r_ * (128 * jb + q_lo) - Q0
                    tok1 = rho + r_ * (128 * jb + 127) - Q0
                    blks = [t_acc[c] for c in range(tok0 // 128, tok1 // 128 + 1)]
                    dst = accD[0:65, :, tok0:tok1 + 1:r_]
                    srcp = v4(PS[ab][0:65, 0:512])[:, :, q_lo:128]
                    if gd == 0:
                        P.copy("dve", dst, srcp, r=[tPS[ab]], w=blks)
                    else:
                        P.tt(dst, srcp, dst, ALU.add, r=[tPS[ab]] + blks, w=blks)

                djobs = []
                for rho in range(r_):
                    for jb in range(npc):
                        jmin = -(-(Q0 - rho) // r_)
                        q_lo = max(0, jmin - 128 * jb)
                        if q_lo >= 128:
                            continue
                        djobs.append((rho, jb, q_lo, [dl for dl in (0, 1) if jb - dl >= 0]))
                dpend = None
                for dj in djobs + [None]:
                    cur = None
                    if dj is not None:
                        cur = (dj, dil_a(*dj))
                    if dpend is not None:
                        dil_b(*dpend[0], dpend[1])
                    dpend = cur
            s65 = sb(dl_, "s65", [128, 64], F32)
            t_s65 = Trk()
            P.dma("sp", s65[0:65, :], c_s65[:, :], w=[t_s65])
            for hh in range(4):
                n0 = 0
                while n0 < NQ:
                    n = min(512, NQ - n0)
                    b = nextps(4, 6)
                    tb = [t_acc[c] for c in chunks_of(n0, n)]
                    P.mm(PS[b][0:64, 0:n], s65[0:65, 0:64], accD[0:65, hh, n0:n0 + n], True, True,
                         r=[t_s65] + tb, w=[tPS[b]])
                    m = next_tmp()
                    P.ts(TMP[m][0:64, 0:n], PS[b][0:64, 0:n], 1e-30, ALU.max, r=[tPS[b]], w=[t_TMP[m]])
                    P.op("dve", lambda e, m=m, n=n: e.reciprocal(out=TMP[m][0:64, 0:n], in_=TMP[m][0:64, 0:n]),
                         r=[t_TMP[m]], w=[t_TMP[m]])
                    P.tt(OBT[0:64, hh, n0:n0 + n], accD[0:64, hh, n0:n0 + n], TMP[m][0:64, 0:n], ALU.mult,
                         r=tb + [t_TMP[m]], w=[t_OBT[c] for c in chunks_of(n0, n)])
                    n0 += n
            P.barrier()
        if stage == 4:
            dump("OBT", OBT[0:64, :, :], t_OBT)
            return done()

        with contextlib.ExitStack() as mg:
            WPA = sb(mg, "WPA", [128, 4, D], BF16)
            WPB = sb(mg, "WPB", [128, 4, D], BF16)
            WO = sb(mg, "WO", [128, 8, D], BF16)
            MX = sb(mg, "MX", [128, 8, NQ], BF16)
            hTq = sb(mg, "hTq", [128, 8, NQ], BF16)
            wm = [sb(mg, "wm%d" % k, [128, 8, 256], BF16) for k in range(2)]
            xr = [sb(mg, "xr%d" % k, [128, D], F32) for k in range(2)]
            t_WPA, t_WPB, t_WO, t_hTq = Trk(), Trk(), Trk(), Trk()
            t_MX = trks(NQB)
            t_wm = trks(2)
            t_xr = trks(2)
            P.dma("pool", WPA[:, :, :], w_pa.rearrange("(c p) n -> p c n", p=128), w=[t_WPA])
            P.dma("pool", WPB[0:64, :, :], w_pb.rearrange("(h d) n -> d h n", d=64), w=[t_WPB])
            P.dma_multi("pool", [(WO[:, 0:4, :], w_out[0:512, :].rearrange("(c p) n -> p c n", p=128)),
                                 (WO[:, 4:8, :], w_out[512:1024, :].rearrange("(c p) n -> p c n", p=128))],
                        w=[t_WO])
            P.dma("sp", hTq[:, :, :], hT_d[:, :, Q0:T], r=t_hTd, w=[t_hTq])
            ntiles = []
            n0 = 0
            while n0 < NQ:
                n = min(512, NQ - n0)
                ntiles.append((n0, n))
                n0 += n
            for mc in range(8):
                k = mc % 2
                load_wt(wm[k], t_wm[k], w_in, [(3608 + mc * 128, 128), (4632 + mc * 128, 128)])
                for (n0, n) in ntiles:
                    tb = chunks_of(n0, n)
                    b1, b2_, b3, b4 = [nextps(0, 7) for _ in range(4)]
                    for fc in range(4):
                        P.mm(PS[b1][:, 0:n], WPA[:, fc, mc * 128:(mc + 1) * 128], OAT[:, fc, n0:n0 + n],
                             fc == 0, fc == 3, r=[t_WPA] + [t_OAT[c] for c in tb], w=[tPS[b1]])
                    for hh in range(4):
                        P.mm(PS[b2_][:, 0:n], WPB[0:64, hh, mc * 128:(mc + 1) * 128], OBT[0:64, hh, n0:n0 + n],
                             hh == 0, hh == 3, r=[t_WPB] + [t_OBT[c] for c in tb], w=[tPS[b2_]])
                    for dm in range(8):
                        P.mm(PS[b3][:, 0:n], wm[k][:, dm, 0:128], hTq[:, dm, n0:n0 + n], dm == 0, dm == 7,
                             r=[t_wm[k], t_hTq], w=[tPS[b3]])
                    for dm in range(8):
                        P.mm(PS[b4][:, 0:n], wm[k][:, dm, 128:256], hTq[:, dm, n0:n0 + n], dm == 0, dm == 7,
                             r=[t_wm[k], t_hTq], w=[tPS[b4]])
                    ma_, mb_ = next_tmp(), next_tmp()
                    P.act(TMP[ma_][:, 0:n], PS[b3][:, 0:n], AF.Sigmoid, r=[tPS[b3]], w=[t_TMP[ma_]])
                    P.act(TMP[mb_][:, 0:n], PS[b4][:, 0:n], AF.Sigmoid, r=[tPS[b4]], w=[t_TMP[mb_]])
                    P.tt(TMP[ma_][:, 0:n], TMP[ma_][:, 0:n], PS[b1][:, 0:n], ALU.mult, r=[t_TMP[ma_], tPS[b1]],
                         w=[t_TMP[ma_]])
                    P.tt(TMP[mb_][:, 0:n], TMP[mb_][:, 0:n], PS[b2_][:, 0:n], ALU.mult, r=[t_TMP[mb_], tPS[b2_]],
                         w=[t_TMP[mb_]])
                    P.tt(MX[:, mc, n0:n0 + n], TMP[ma_][:, 0:n], TMP[mb_][:, 0:n], ALU.add,
                         r=[t_TMP[ma_], t_TMP[mb_]], w=[t_MX[c] for c in tb])
            for i in range(NQB):
                k = i % 2
                P.dma("sp", xr[k][:, :], x[Q0 + i * 128:Q0 + (i + 1) * 128, :], w=[t_xr[k]])
                for half in range(2):
                    b = nextps(0, 7)
                    for mc in range(8):
                        P.mm(PS[b][:, 0:512], MX[:, mc, i * 128:(i + 1) * 128], WO[:, mc, half * 512:(half + 1) * 512],
                             mc == 0, mc == 7, r=[t_MX[i], t_WO], w=[tPS[b]])
                    P.tt(xr[k][:, half * 512:(half + 1) * 512], PS[b][:, 0:512], xr[k][:, half * 512:(half + 1) * 512],
                         ALU.add, r=[tPS[b], t_xr[k]], w=[t_xr[k]])
                P.dma("sp", xm_d[i * 128:(i + 1) * 128, :], xr[k][:, :], r=[t_xr[k]], w=[t_xmd[i]])
            P.barrier()
        if stage == 5:
            return done()
    P.barrier()

    with contextlib.ExitStack() as FF:
        H2T = sb(FF, "H2T", [128, 8, 2050], BF16)
        t_H2T = trks(NQB)
        gfr = sb(FF, "gfr", [128, D], F32)
        gfin = sb(FF, "gfin", [128, D], F32)
        t_g = Trk()
        P.dma("sp", gfr[:], g_ffn[0:1, :].partition_broadcast(128), w=[t_g])
        P.dma("sp", gfin[:], g_fin[0:1, :].partition_broadcast(128), w=[t_g])
        cwr = sb(FF, "cwr", [128, 4, 128], F32)
        cw = sb(FF, "cw", [128, 4, 22], F32)
        t_cw = Trk()
        P.dma("sp", cwr[0:22, 0:3, :], conv_w.rearrange("k (j p) -> j k p", p=128), w=[t_cw])
        P.dma("sp", cwr[0:22, 3, :], conv_b.rearrange("o (j p) -> (o j) p", p=128), w=[t_cw])
        for kk in range(4):
            b = nextps(0, 7)
            P.tr(PS[b][:, 0:22], cwr[0:22, kk, :], ident[0:22, 0:22], r=[t_cw, t_const], w=[tPS[b]])
            P.copy("dve", cw[:, kk, :], PS[b][:, 0:22], r=[tPS[b]], w=[t_cw])
        xt2 = [sb(FF, "fx%d" % i, [128, D], F32) for i in range(2)]
        t_xt = trks(2)
        xn2 = [sb(FF, "fn%d" % i, [128, D], BF16) for i in range(2)]
        t_xn = trks(2)
        yo2 = [sb(FF, "yo%d" % i, [128, D], F32) for i in range(2)]
        t_yo = trks(2)
        junk = sb(FF, "fjunk", [128, D], F32)
        ss2 = [sb(FF, "fss%d" % i, [128, 4], F32) for i in range(2)]
        t_ss = trks(2)
        for i in range(NQB):
            k = i % 2
            P.dma("sp", xt2[k][:], xm_d[i * 128:(i + 1) * 128, :], r=[t_xmd[i]], w=[t_xt[k]])
            rmsnorm_tile(junk, ss2[k], t_ss[k], xt2[k][:], t_xt[k], gfr[:], t_g, xn2[k][:], t_xn[k])
            for c in range(8):
                P.tr(PST[:, c * 128:(c + 1) * 128], xn2[k][:, c * 128:(c + 1) * 128], identb[:],
                     r=[t_xn[k], t_const], w=[tPST])
            pv8 = PST[:, :].rearrange("p (c n) -> p c n", c=8)
            if i == 0:
                P.ts(H2T[:, :, 0:2], pv8[:, :, 126:128], halo[:, 0:1], ALU.mult, r=[tPST, t_const], w=[t_H2T[0]])
            else:
                P.copy(evac_eng(), H2T[:, :, 2 + (i - 1) * 128:2 + i * 128], pv8, r=[tPST], w=[t_H2T[i]])
        if stage == 6:
            dump("H2T", H2T[:, :, 0:258], t_H2T)
            dump("cw", cw[:, :, :], [t_cw])
            return done()
        WD = sb(FF, "WD", [128, NFF, D], BF16)
        t_WD = Trk()
        wdv = w_down.rearrange("(j p) n -> p j n", p=128)
        P.dma_multi("pool", [(WD[:, j0:min(j0 + 6, NFF), :], wdv[:, j0:min(j0 + 6, NFF), :]) for j0 in range(0, NFF, 6)],
                    w=[t_WD])
        AT = sb(FF, "AT", [128, NFF, 1024], BF16)
        t_AT = trks(NFF)
        wu = [sb(FF, "wu%d" % k, [128, 8, 256], BF16) for k in range(2)]
        t_wu = trks(2)
        U = [sb(FF, "U%d" % k, [128, 1026], F32) for k in range(2)]
        t_U = trks(2)
        C1s = [sb(FF, "C1%d" % k, [128, 1024], F32) for k in range(2)]
        C2s = [sb(FF, "C2%d" % k, [128, 1024], F32) for k in range(2)]
        t_C1s, t_C2s = trks(2), trks(2)
        FG = [sb(FF, "FG%d" % k, [128, 512], F32) for k in range(2)]
        t_FG = trks(2)

        def gelu2(out, t_out, xin, t_x, n):
            a, b2 = FG[0][:, 0:n], FG[1][:, 0:n]
            P.tt(a, xin, xin, ALU.mult, r=[t_x], w=[t_FG[0]])
            P.ts(a, a, 0.044715, ALU.mult, 1.0, ALU.add, r=[t_FG[0]], w=[t_FG[0]])
            P.tt(a, a, xin, ALU.mult, r=[t_FG[0], t_x], w=[t_FG[0]])
            P.act(b2, a, AF.Sigmoid, r=[t_FG[0]], w=[t_FG[1]], scale=1.5957691216057308)
            P.tt(out, b2, xin, ALU.mult, r=[t_FG[1], t_x], w=t_out)

        jrr = [0]
        for th in range(2):
            base = 2 + th * 1024
            hts = [t_H2T[c] for c in range(max(0, th * 8), th * 8 + 9)]
            for j in range(NFF):
                k = jrr[0] % 2
                jrr[0] += 1
                C1, C2, t_C1, t_C2 = C1s[k], C2s[k], t_C1s[k], t_C2s[k]
                P.dma_multi("pool", [(wu[k][:, :, 0:128], w_up[:, j * 128:(j + 1) * 128].rearrange("(c p) n -> p c n", p=128)),
                                     (wu[k][:, :, 128:256],
                                      w_up[:, DFF + j * 128:DFF + (j + 1) * 128].rearrange("(c p) n -> p c n", p=128))],
                            w=[t_wu[k]])
                b = nextps(0, 7)
                for dm in range(8):
                    P.mm(PS[b][:, 0:2], wu[k][:, dm, 0:128], H2T[:, dm, base - 2:base], dm == 0, dm == 7,
                         r=[t_wu[k]] + hts, w=[tPS[b]])
                P.copy("dve", U[k][:, 0:2], PS[b][:, 0:2], r=[tPS[b]], w=[t_U[k]])
                for nt in range(2):
                    b = nextps(0, 7)
                    for dm in range(8):
                        P.mm(PS[b][:, 0:512], wu[k][:, dm, 0:128], H2T[:, dm, base + nt * 512:base + (nt + 1) * 512],
                             dm == 0, dm == 7, r=[t_wu[k]] + hts, w=[tPS[b]])
                    P.copy("act", U[k][:, 2 + nt * 512:2 + (nt + 1) * 512], PS[b][:, 0:512], r=[tPS[b]], w=[t_U[k]])
                bg = []
                for nt in range(2):
                    b = nextps(0, 7)
                    bg.append(b)
                    for dm in range(8):
                        P.mm(PS[b][:, 0:512], wu[k][:, dm, 128:256], H2T[:, dm, base + nt * 512:base + (nt + 1) * 512],
                             dm == 0, dm == 7, r=[t_wu[k]] + hts, w=[tPS[b]])
                P.ts(C1[:, :], U[k][:, 2:1026], cw[:, 2, j:j + 1], ALU.mult, cw[:, 3, j:j + 1], ALU.add,
                     r=[t_U[k], t_cw], w=[t_C1])
                P.stt(C1[:, :], U[k][:, 1:1025], cw[:, 1, j:j + 1], C1[:, :], ALU.mult, ALU.add,
                      r=[t_U[k], t_cw, t_C1], w=[t_C1])
                P.stt(C1[:, :], U[k][:, 0:1024], cw[:, 0, j:j + 1], C1[:, :], ALU.mult, ALU.add,
                      r=[t_U[k], t_cw, t_C1], w=[t_C1])
                P.act(C2[:, :], C1[:, :], AF.Gelu_apprx_tanh, r=[t_C1], w=[t_C2])
                for nt in range(2):
                    sl = slice(nt * 512, (nt + 1) * 512)
                    P.tt(AT[:, j, sl], C2[:, sl], PS[bg[nt]][:, 0:512], ALU.mult, r=[t_C2, tPS[bg[nt]]], w=[t_AT[j]])
            if stage == 7:
                dump("AT", AT[:, 0:2, :], t_AT)
                return done()
            for kt in range(8):
                i = 1 + th * 8 + kt
                k = kt % 2
                P.dma("sp", xt2[k][:], xm_d[i * 128:(i + 1) * 128, :], r=[t_xmd[i]], w=[t_xt[k]])
                for half in range(2):
                    b = nextps(0, 7)
                    for j in range(NFF):
                        P.mm(PS[b][:, 0:512], AT[:, j, kt * 128:(kt + 1) * 128], WD[:, j, half * 512:(half + 1) * 512],
                             j == 0, j == NFF - 1, r=[t_AT[j], t_WD], w=[tPS[b]])
                    P.tt(xt2[k][:, half * 512:(half + 1) * 512], PS[b][:, 0:512], xt2[k][:, half * 512:(half + 1) * 512],
                         ALU.add, r=[tPS[b], t_xt[k]], w=[t_xt[k]])
                rmsnorm_tile(junk, ss2[k], t_ss[k], xt2[k][:], t_xt[k], gfin[:], t_g, yo2[k][:], t_yo[k])
                P.dma("sp", y[(th * 8 + kt) * 128:(th * 8 + kt + 1) * 128, :], yo2[k][:], r=[t_yo[k]])
    return done()


W_NAMES = ["g_mix", "w_in", "pe_cmp_k", "w_cmp_k1", "w_cmp_k2", "pe_cmp_v", "w_cmp_v1", "w_cmp_v2",
           "w_proj_nsa", "w_proj_dil", "w_out", "g_ffn", "w_up", "conv_w", "conv_b", "w_down", "g_final"]


def make_in_maps(inputs, cores):
    x = np.asarray(inputs["x"], dtype=np.float32)
    shared = {}
    for n in W_NAMES:
        a = np.asarray(inputs[n], dtype=np.float32)
        if n == "g_final":
            a = a.reshape(1, D)
        elif a.shape[0] == 1:
            a = a[0]
        if a.ndim == 1:
            a = a.reshape(1, -1)
        shared[n] = np.ascontiguousarray(a)
    tabs = [const_tables(0), const_tables(1)]
    maps = []
    for (b, hf) in cores:
        m = dict(shared)
        if hf == 1:
            xl = x[b]
        else:
            xl = np.concatenate([np.zeros((2048, D), np.float32), x[b, :2048]], axis=0)
        m["x"] = np.ascontiguousarray(xl)
        for k, v in tabs[hf].items():
            m["c_" + k] = v
        maps.append(m)
    return maps


_NC_CACHE = {}


N_LAUNCH = 1


def kernel(**inputs):
    if "nc" not in _NC_CACHE:
        _NC_CACHE["nc"] = build()
    nc = _NC_CACHE["nc"]
    cores = [(b, hf) for b in range(4) for hf in range(2)]
    out = np.zeros((4, T, D), np.float32)
    per = len(cores) // N_LAUNCH
    for li in range(N_LAUNCH):
        cs = cores[li * per:(li + 1) * per]
        maps = make_in_maps(inputs, cs)
        res = run_bass_kernel_spmd(nc, maps, core_ids=list(range(len(cs))))
        for ci, (b, hf) in enumerate(cs):
            out[b, hf * 2048:(hf + 1) * 2048, :] = res.results[ci]["y"]
    return out
```
